# Optimizing a Trainium2 kernel written in Bass

```python
import math
import jax, jax.numpy as jnp
from jax import lax
import numpy as np

D_MODEL = 4096
BATCH = 2
SEQ = 8192
DEPTH = 1
DEC_BATCH = 8
DEC_SEQ = 16
PAST_LEN = 2048

CHUNK = 64
Q_BLOCK = 128
MIX_WIDTH = D_MODEL
ATTN_WIDTH = MIX_WIDTH // 2
V_HEAD_DIM = 128
ATTN_HEADS = ATTN_WIDTH // V_HEAD_DIM
QK_DIM = V_HEAD_DIM // 2
ATTN_SCALE = QK_DIM ** -0.5
SSD_INNER = MIX_WIDTH - ATTN_WIDTH
SSD_HEAD_DIM = 64
SSD_HEADS = SSD_INNER // SSD_HEAD_DIM
SSD_GROUPS = 8
D_STATE = 128
CONV_WIDTH = 4
CONV_DIM = SSD_INNER + 2 * SSD_GROUPS * D_STATE
SSD_CHUNK = CHUNK
D_FF = ((-(-8 * D_MODEL // 3) + 255) // 256) * 256
EPS = 1e-6
IN_DIM = 3 * ATTN_WIDTH + SSD_INNER + CONV_DIM + SSD_HEADS
IN_SPLITS = [ATTN_WIDTH, 2 * ATTN_WIDTH, 3 * ATTN_WIDTH,
             3 * ATTN_WIDTH + SSD_INNER, 3 * ATTN_WIDTH + SSD_INNER + CONV_DIM]

kernel_name = "hybrid_diffattn_ssd_stream_step"


def rms_norm(x, w):
    xf = x.astype(jnp.float32)
    y = xf * lax.rsqrt(jnp.mean(xf * xf, axis=-1, keepdims=True) + EPS)
    return (y * w.astype(jnp.float32)).astype(x.dtype)


def lambda_init(layer):
    return 0.8 - 0.6 * math.exp(-0.3 * layer)


def diff_mix(q1, q2, k1, k2, v, lam, mask):
    s1 = jnp.einsum("bqhd,bkhd->bhqk", q1, k1, preferred_element_type=jnp.float32) * ATTN_SCALE
    s2 = jnp.einsum("bqhd,bkhd->bhqk", q2, k2, preferred_element_type=jnp.float32) * ATTN_SCALE
    if mask is not None:
        s1 = jnp.where(mask, s1, -jnp.inf)
        s2 = jnp.where(mask, s2, -jnp.inf)
    a = jax.nn.softmax(s1, axis=-1) - lam * jax.nn.softmax(s2, axis=-1)
    return jnp.einsum("bhqk,bkhd->bqhd", a.astype(v.dtype), v)


def diff_attention_prompt(q1, q2, k1, k2, v, lam):
    b, s = q1.shape[0], q1.shape[1]
    key_chunk = jnp.arange(s) // CHUNK

    def one_block(start):
        qb1 = lax.dynamic_slice_in_dim(q1, start, Q_BLOCK, axis=1)
        qb2 = lax.dynamic_slice_in_dim(q2, start, Q_BLOCK, axis=1)
        q_chunk = (start + jnp.arange(Q_BLOCK)) // CHUNK
        mask = key_chunk[None, :] <= q_chunk[:, None]
        return diff_mix(qb1, qb2, k1, k2, v, lam, mask)

    out = lax.map(one_block, jnp.arange(0, s, Q_BLOCK))
    return jnp.moveaxis(out, 0, 1).reshape(b, s, ATTN_HEADS, V_HEAD_DIM)


def ssd_scan(x, dt, A, Bm, Cm, init_state, chunk):
    b, l, h, p = x.shape
    g, n = Bm.shape[2], Bm.shape[3]
    r = h // g
    c = l // chunk
    f32 = jnp.float32
    x = x.astype(f32)
    dt = dt.astype(f32)
    Bc = Bm.astype(f32).reshape(b, c, chunk, g, n)
    Cc = Cm.astype(f32).reshape(b, c, chunk, g, n)
    xdt = (x * dt[..., None]).reshape(b, c, chunk, g, r, p)
    dA = jnp.moveaxis((dt * A).reshape(b, c, chunk, g, r), 2, -1)
    a_cs = jnp.cumsum(dA, axis=-1)
    tri = jnp.tril(jnp.ones((chunk, chunk), dtype=bool))
    seg = a_cs[..., :, None] - a_cs[..., None, :]
    decay_in = jnp.exp(jnp.where(tri, seg, -jnp.inf))
    cb = jnp.einsum("bcqgn,bcsgn->bcgqs", Cc, Bc)
    y_diag = jnp.einsum("bcgqs,bcgrqs,bcsgrp->bcqgrp", cb, decay_in, xdt)
    decay_to_end = jnp.exp(a_cs[..., -1:] - a_cs)
    states = jnp.einsum("bcsgn,bcgrs,bcsgrp->bcgrpn", Bc, decay_to_end, xdt)
    chunk_decay = jnp.exp(a_cs[..., -1])

    def step(carry, inp):
        st, dec = inp
        return carry * dec[..., None, None] + st, carry

    init = init_state.astype(f32).reshape(b, g, r, p, n)
    final, prev = lax.scan(step, init, (jnp.moveaxis(states, 1, 0), jnp.moveaxis(chunk_decay, 1, 0)))
    prev = jnp.moveaxis(prev, 0, 1)
    y_off = jnp.einsum("bcqgn,bcgrpn,bcgrq->bcqgrp", Cc, prev, jnp.exp(a_cs))
    y = (y_diag + y_off).reshape(b, l, h, p)
    return y, final.reshape(b, h, p, n)


def ssd_branch(z, xbc, dt_raw, conv_prev, ssm_prev, p, chunk):
    b, l, _ = xbc.shape
    xpad = jnp.concatenate([conv_prev.astype(xbc.dtype), xbc], axis=1)
    conv = p["conv_b"]
    for i in range(CONV_WIDTH):
        conv = conv + p["conv_w"][i] * xpad[:, i:i + l]
    new_conv = xpad[:, xpad.shape[1] - (CONV_WIDTH - 1):]
    act = jax.nn.silu(conv)
    xs, Bm, Cm = jnp.split(act, [SSD_INNER, SSD_INNER + SSD_GROUPS * D_STATE], axis=-1)
    xs = xs.reshape(b, l, SSD_HEADS, SSD_HEAD_DIM)
    Bm = Bm.reshape(b, l, SSD_GROUPS, D_STATE)
    Cm = Cm.reshape(b, l, SSD_GROUPS, D_STATE)
    dt = jax.nn.softplus(dt_raw.astype(jnp.float32) + p["dt_bias"].astype(jnp.float32))
    A = -jnp.exp(p["A_log"].astype(jnp.float32))
    y, final = ssd_scan(xs, dt, A, Bm, Cm, ssm_prev, chunk)
    y = y + p["D_skip"].astype(jnp.float32)[:, None] * xs.astype(jnp.float32)
    y = y.reshape(b, l, SSD_INNER) * jax.nn.silu(z.astype(jnp.float32))
    yg = y.reshape(b, l, SSD_GROUPS, SSD_INNER // SSD_GROUPS)
    yg = yg * lax.rsqrt(jnp.mean(yg * yg, axis=-1, keepdims=True) + EPS)
    y = yg.reshape(b, l, SSD_INNER) * p["ssd_norm_w"].astype(jnp.float32)
    return y.astype(z.dtype), new_conv, final.astype(z.dtype)


def hybrid_layer(x, p, layer, past_k, past_v, conv_prev, ssm_prev, ssd_chunk):
    b, l, _ = x.shape
    h = rms_norm(x, p["norm1_w"])
    proj = h @ p["w_in"]
    q, k, v, z, xbc, dt_raw = jnp.split(proj, IN_SPLITS, axis=-1)
    q = q.reshape(b, l, ATTN_HEADS, 2 * QK_DIM)
    k = k.reshape(b, l, ATTN_HEADS, 2 * QK_DIM)
    v = v.reshape(b, l, ATTN_HEADS, V_HEAD_DIM)
    lam0 = lambda_init(layer)
    f32 = jnp.float32
    lam = (jnp.exp(jnp.sum(p["lambda_q1"].astype(f32) * p["lambda_k1"].astype(f32)))
           - jnp.exp(jnp.sum(p["lambda_q2"].astype(f32) * p["lambda_k2"].astype(f32))) + lam0)
    if past_k is None:
        o = diff_attention_prompt(q[..., :QK_DIM], q[..., QK_DIM:], k[..., :QK_DIM], k[..., QK_DIM:], v, lam)
    else:
        k_all = jnp.concatenate([past_k.astype(k.dtype), k], axis=1)
        v_all = jnp.concatenate([past_v.astype(v.dtype), v], axis=1)
        o = diff_mix(q[..., :QK_DIM], q[..., QK_DIM:], k_all[..., :QK_DIM], k_all[..., QK_DIM:], v_all, lam, None)
    o = rms_norm(o, p["subln_w"]) * (1.0 - lam0)
    attn_out = o.reshape(b, l, ATTN_WIDTH)
    ssd_out, new_conv, new_ssm = ssd_branch(z, xbc, dt_raw, conv_prev, ssm_prev, p, ssd_chunk)
    mix = jnp.concatenate([attn_out, ssd_out.astype(attn_out.dtype)], axis=-1)
    x = x + mix @ p["w_out"]
    h2 = rms_norm(x, p["norm2_w"])
    x = x + (jax.nn.silu(h2 @ p["w_gate"]) * (h2 @ p["w_up"])) @ p["w_down"]
    return x, k, v, new_conv, new_ssm


def setup_inputs(seed: int = 0) -> dict:
    key = jax.random.key(seed)
    ks = jax.random.split(key, 26)
    f32 = jnp.float32
    nrm = lambda k, shape, s: jax.random.normal(k, shape, f32) * s
    dt0 = jnp.exp(jax.random.uniform(ks[15], (DEPTH, SSD_HEADS), f32)
                  * (math.log(0.1) - math.log(0.001)) + math.log(0.001))
    return {
        "x_prompt": nrm(ks[0], (BATCH, SEQ, D_MODEL), 1.0),
        "x_sample": nrm(ks[1], (DEC_BATCH, DEC_SEQ, D_MODEL), 1.0),
        "cache_k": nrm(ks[2], (DEPTH, DEC_BATCH, PAST_LEN, ATTN_HEADS, 2 * QK_DIM), 1.0),
        "cache_v": nrm(ks[3], (DEPTH, DEC_BATCH, PAST_LEN, ATTN_HEADS, V_HEAD_DIM), 1.0),
        "state_conv": nrm(ks[4], (DEPTH, DEC_BATCH, CONV_WIDTH - 1, CONV_DIM), 1.0),
        "state_ssm": nrm(ks[5], (DEPTH, DEC_BATCH, SSD_HEADS, SSD_HEAD_DIM, D_STATE), 0.1),
        "norm1_w": 1.0 + nrm(ks[6], (DEPTH, D_MODEL), 0.01),
        "w_in": nrm(ks[7], (DEPTH, D_MODEL, IN_DIM), D_MODEL ** -0.5),
        "lambda_q1": nrm(ks[8], (DEPTH, QK_DIM), 0.1),
        "lambda_k1": nrm(ks[9], (DEPTH, QK_DIM), 0.1),
        "lambda_q2": nrm(ks[10], (DEPTH, QK_DIM), 0.1),
        "lambda_k2": nrm(ks[11], (DEPTH, QK_DIM), 0.1),
        "subln_w": 1.0 + nrm(ks[12], (DEPTH, V_HEAD_DIM), 0.01),
        "conv_w": nrm(ks[13], (DEPTH, CONV_WIDTH, CONV_DIM), CONV_WIDTH ** -0.5),
        "conv_b": nrm(ks[14], (DEPTH, CONV_DIM), 0.01),
        "dt_bias": dt0 + jnp.log(-jnp.expm1(-dt0)),
        "A_log": jnp.log(jax.random.uniform(ks[16], (DEPTH, SSD_HEADS), f32, 1.0, 16.0)),
        "D_skip": 1.0 + nrm(ks[17], (DEPTH, SSD_HEADS), 0.01),
        "ssd_norm_w": 1.0 + nrm(ks[18], (DEPTH, SSD_INNER), 0.01),
        "w_out": nrm(ks[19], (DEPTH, MIX_WIDTH, D_MODEL), MIX_WIDTH ** -0.5),
        "norm2_w": 1.0 + nrm(ks[20], (DEPTH, D_MODEL), 0.01),
        "w_gate": nrm(ks[21], (DEPTH, D_MODEL, D_FF), D_MODEL ** -0.5),
        "w_up": nrm(ks[22], (DEPTH, D_MODEL, D_FF), D_MODEL ** -0.5),
        "w_down": nrm(ks[23], (DEPTH, D_FF, D_MODEL), D_FF ** -0.5),
        "final_norm_w": 1.0 + nrm(ks[24], (D_MODEL,), 0.01),
    }


def reference(x_prompt, x_sample, cache_k, cache_v, state_conv, state_ssm,
              norm1_w, w_in, lambda_q1, lambda_k1, lambda_q2, lambda_k2, subln_w,
              conv_w, conv_b, dt_bias, A_log, D_skip, ssd_norm_w, w_out,
              norm2_w, w_gate, w_up, w_down, final_norm_w):
    xp, xs = x_prompt, x_sample
    kp_l, vp_l, cp_l, sp_l = [], [], [], []
    ks_l, vs_l, cs_l, ss_l = [], [], [], []
    for layer in range(DEPTH):
        p = {
            "norm1_w": norm1_w[layer], "w_in": w_in[layer],
            "lambda_q1": lambda_q1[layer], "lambda_k1": lambda_k1[layer],
            "lambda_q2": lambda_q2[layer], "lambda_k2": lambda_k2[layer],
            "subln_w": subln_w[layer], "conv_w": conv_w[layer], "conv_b": conv_b[layer],
            "dt_bias": dt_bias[layer], "A_log": A_log[layer], "D_skip": D_skip[layer],
            "ssd_norm_w": ssd_norm_w[layer], "w_out": w_out[layer], "norm2_w": norm2_w[layer],
            "w_gate": w_gate[layer], "w_up": w_up[layer], "w_down": w_down[layer],
        }
        b_p = xp.shape[0]
        conv0 = jnp.zeros((b_p, CONV_WIDTH - 1, CONV_DIM), xp.dtype)
        ssm0 = jnp.zeros((b_p, SSD_HEADS, SSD_HEAD_DIM, D_STATE), jnp.float32)
        xp, kp, vp, cp, sp = hybrid_layer(xp, p, layer, None, None, conv0, ssm0, SSD_CHUNK)
        xs, kn, vn, cn, sn = hybrid_layer(xs, p, layer, cache_k[layer], cache_v[layer],
                                          state_conv[layer], state_ssm[layer], xs.shape[1])
        kp_l.append(kp); vp_l.append(vp); cp_l.append(cp); sp_l.append(sp)
        ks_l.append(kn); vs_l.append(vn); cs_l.append(cn); ss_l.append(sn)
    y_prompt = rms_norm(xp, final_norm_w)
    y_sample = rms_norm(xs, final_norm_w)
    return (y_prompt, y_sample,
            jnp.stack(kp_l), jnp.stack(vp_l), jnp.stack(cp_l), jnp.stack(sp_l),
            jnp.stack(ks_l), jnp.stack(vs_l), jnp.stack(cs_l), jnp.stack(ss_l))
```

```python
import contextlib
import numpy as np
import concourse.bass as bass
import concourse.mybir as mybir
from concourse.bass_utils import run_bass_kernel_spmd

F32 = mybir.dt.float32
BF16 = mybir.dt.bfloat16
AF = mybir.ActivationFunctionType
ALU = mybir.AluOpType
EPS = 1e-6
NSLOT = 4
KDMA = 8


class Cfg:
    def __init__(self, D=4096, SEQ=8192, DFF=11008, PAST=2048, DEC=16, G=8):
        self.D, self.SEQ, self.DFF, self.PAST, self.DEC, self.G = D, SEQ, DFF, PAST, DEC, G
        self.AW = D // 2
        self.H = self.AW // 128
        self.SI = D - self.AW
        self.SH = self.SI // 64
        self.CD = self.SI + 2 * G * 128
        self.IN = 3 * self.AW + self.SI + self.CD + self.SH
        self.TQ = SEQ // NSLOT
        self.NT = self.TQ // 128
        self.KC = D // 128
        self.oQ, self.oK, self.oV, self.oZ = 0, self.AW, 2 * self.AW, 3 * self.AW
        self.oX = 3 * self.AW + self.SI
        self.oDT = self.oX + self.CD


class Buf:
    def __init__(self, name):
        self.name, self.w, self.r = name, None, []


class Prog:
    ENG = ["pe", "act", "dve", "pool", "sp"]

    def __init__(self, nc, stack):
        self.nc = nc
        self.ops = {e: [] for e in self.ENG}
        self.esem = {e: stack.enter_context(nc.semaphore("s_" + e)) for e in ["pe", "act", "dve", "pool"]}
        self.cnt = {e: 0 for e in self.esem}
        self.dsem = {q: [stack.enter_context(nc.semaphore("d_%s%d" % (q, i))) for i in range(KDMA)]
                     for q in ["sp", "pool"]}
        self.dn = {"sp": 0, "pool": 0}
        self.seen = {e: {} for e in self.ENG}
        self.pend = {e: [] for e in self.ENG}

    def op(self, eng, fn, reads=(), writes=(), dma=False):
        waits = list(self.pend[eng])
        self.pend[eng] = []
        for b in reads:
            if b.w:
                waits.append(b.w)
        for b in writes:
            if b.w:
                waits.append(b.w)
            waits.extend(b.r)
        if dma:
            m = self.dn[eng]
            sem = self.dsem[eng][m % KDMA]
            prev = 16 * (m // KDMA)
            if prev:
                waits.append((sem, prev))
            ev = (sem, prev + 16)
            inc = 16
            self.dn[eng] += 1
        else:
            self.cnt[eng] += 1
            ev = (self.esem[eng], self.cnt[eng])
            inc = 1
        need = {}
        for s, v in waits:
            if eng == "pe" and s is self.esem["pe"]:
                continue
            if self.seen[eng].get(id(s), 0) >= v:
                continue
            if need.get(id(s), (s, 0))[1] < v:
                need[id(s)] = (s, v)
        for k, (s, v) in need.items():
            self.seen[eng][k] = v
        self.ops[eng].append((list(need.values()), fn, ev[0], inc))
        for b in reads:
            b.r.append(ev)
        for b in writes:
            b.w, b.r = ev, []
        return ev

    def all_events(self):
        evs = [(self.esem[e], self.cnt[e]) for e in self.esem if self.cnt[e]]
        for q in self.dsem:
            m = self.dn[q]
            for i in range(KDMA):
                n = (m - i + KDMA - 1) // KDMA if m > i else 0
                if n:
                    evs.append((self.dsem[q][i], 16 * n))
        return evs

    def barrier(self):
        evs = self.all_events()
        for e in self.ENG:
            self.pend[e] = list(evs)

    def finish(self):
        self.barrier()
        for e in self.ENG:
            need = {}
            for s, v in self.pend[e]:
                if self.seen[e].get(id(s), 0) < v:
                    need[id(s)] = (s, v)
            self.ops[e].append((list(need.values()), None, None, 0))

    def emit(self):
        nc = self.nc
        names = {"pe": "tensor", "act": "scalar", "dve": "vector", "pool": "gpsimd", "sp": "sync"}
        with nc.Block() as block:
            for e in self.ENG:
                def body(engobj, e=e):
                    for need, fn, sem, inc in self.ops[e]:
                        for s, v in need:
                            engobj.wait_ge(s, v)
                        if fn is not None:
                            fn(engobj).then_inc(sem, inc)
                getattr(block, names[e])(body)


def build(cfg, phases="ABXDCEF", dbg=()):
    c = cfg
    nc = bass.Bass("TRN2", target_bir_lowering=False)
    D, KC, IN, H, AW, SI, SH, G, CD, DFF = c.D, c.KC, c.IN, c.H, c.AW, c.SI, c.SH, c.G, c.CD, c.DFF
    TQ, NT, DEC, PAST = c.TQ, c.NT, c.DEC, c.PAST
    NTOK = NSLOT * TQ
    NTILE = NSLOT * NT
    T0 = NTILE - NT
    GB = G * 128
    NKS = PAST // 128 + 1
    LAM0 = 0.2

    def din(name, shape, dt=F32):
        return nc.dram_tensor(name, list(shape), dt, kind="ExternalInput").ap()

    def dout(name, shape, dt=F32):
        return nc.dram_tensor(name, list(shape), dt, kind="ExternalOutput").ap()

    def dscr(name, shape, dt):
        kind = "ExternalOutput" if name in dbg else "Internal"
        return nc.dram_tensor(name, list(shape), dt, kind=kind).ap()

    xq = din("xq", [NTOK, D])
    valid = din("valid", [NTOK, 1])
    xs = din("xs", [DEC, D])
    ck = din("ck", [PAST, AW])
    cv = din("cv", [PAST, AW])
    sconv = din("sconv", [3, CD])
    sssm = din("sssm", [SH * 64, 128])
    norm1_w = din("norm1_w", [1, D])
    w_in = din("w_in", [D, IN])
    lam_in = din("lam_in", [1, 256])
    subln_w = din("subln_w", [1, 128])
    conv_w = din("conv_w", [4, CD])
    conv_b = din("conv_b", [1, CD])
    dt_bias = din("dt_bias", [1, SH])
    A_log = din("A_log", [1, SH])
    D_skip = din("D_skip", [1, SH])
    ssd_norm_w = din("ssd_norm_w", [1, SI])
    w_out = din("w_out", [D, D])
    norm2_w = din("norm2_w", [1, D])
    w_gate = din("w_gate", [D, DFF])
    w_up = din("w_up", [D, DFF])
    w_down = din("w_down", [DFF, D])
    final_norm_w = din("final_norm_w", [1, D])
    consts_in = din("consts", [128, 4 * 128])

    y_q = dout("y_q", [TQ, D])
    y_s = dout("y_s", [DEC, D])
    k_q = dout("k_q", [TQ, AW])
    v_q = dout("v_q", [TQ, AW])
    conv_p = dout("conv_p", [3, CD])
    ssm_p = dout("ssm_p", [SH * 64, 128])
    k_s = dout("k_s", [DEC, AW])
    v_s = dout("v_s", [DEC, AW])
    conv_s = dout("conv_s", [3, CD])
    ssm_s = dout("ssm_s", [SH * 64, 128])

    hT_d = dscr("hT_d", [NTILE + 1, 128, KC * 128], BF16)
    pQ_d = dscr("pQ_d", [TQ + DEC, AW], F32)
    pK_d = dscr("pK_d", [NTOK + DEC, AW], F32)
    pV_d = dscr("pV_d", [NTOK + DEC, AW], F32)
    pZ_d = dscr("pZ_d", [TQ + DEC, SI], F32)
    pX_d = dscr("pX_d", [NTOK + DEC + 6, CD], F32)
    pDT_d = dscr("pDT_d", [NTOK + DEC, SH], F32)
    act_d = dscr("act_d", [NTOK + DEC, CD], F32)
    KT_d = dscr("KT_d", [H, 128, NTOK], BF16)
    QT_d = dscr("QT_d", [H, 128, TQ], BF16)
    VA_d = dscr("VA_d", [H, 128, NTILE * 129], BF16)
    KTs_d = dscr("KTs_d", [H, 128, NKS * 128], BF16)
    QTs_d = dscr("QTs_d", [H, 128, DEC], BF16)
    VAs_d = dscr("VAs_d", [H, 128, NKS * 129], BF16)
    mix_d = dscr("mix_d", [TQ + DEC, D], BF16)
    mT_d = dscr("mT_d", [NT + 1, 128, KC * 128], BF16)
    x1_d = dscr("x1_d", [TQ + DEC, D], F32)
    h2T_d = dscr("h2T_d", [NT + 1, 128, KC * 128], BF16)
    ff_d = dscr("ff_d", [TQ + DEC, DFF], BF16)
    ffT_d = dscr("ffT_d", [NT + 1, 128, DFF], BF16)
    x2_d = dscr("x2_d", [TQ + DEC, D], F32)

    with contextlib.ExitStack() as stack:
        P = Prog(nc, stack)
        ARENA = 44 * 1024
        arena = stack.enter_context(nc.sbuf_tensor("arena", [128, ARENA], F32))
        psum = stack.enter_context(nc.psum_tensor("psum", [128, 8 * 512], F32))
        top = [0]

        def f32v(words):
            o = top[0]
            top[0] += words
            assert top[0] <= ARENA, "SBUF arena overflow %d" % top[0]
            return arena[:, o:o + words]

        def bf16v(elems):
            return f32v((elems + 1) // 2).bitcast(BF16)[:, :elems]

        def bank(i):
            return psum[:, i * 512:(i + 1) * 512]

        pbuf = [Buf("ps%d" % i) for i in range(8)]

        def O(eng, method, reads, writes, *a, **kw):
            return P.op(eng, lambda e: getattr(e, method)(*a, **kw), reads, writes)

        def DMA(q, out, in_, reads=(), writes=()):
            return P.op(q, lambda e: e.dma_start(out=out, in_=in_), reads, writes, dma=True)

        def COPY(eng, out, in_, reads, writes):
            if eng == "act":
                return O("act", "copy", reads, writes, out=out, in_=in_)
            return O(eng, "tensor_copy", reads, writes, out=out, in_=in_)

        cst = f32v(512)
        Bc = Buf("consts")
        DMA("sp", cst, consts_in[:, :], writes=[Bc])
        ident_f, tri_le, tri_gt, ones_f = (cst[:, i * 128:(i + 1) * 128] for i in range(4))
        ident_b = bf16v(128)
        O("dve", "tensor_copy", [Bc], [Bc], out=ident_b, in_=ident_f)
        base_top = top[0]

        def own_rows(ot):
            return 128 if ot < NT else DEC

        def normT_pass(src_of, ntiles, rows_of, ncols, wrow, dstT, src_bf16=False):
            mark = top[0]
            nch = ncols // 128
            grp = 8 if nch % 8 == 0 else (4 if nch % 4 == 0 else (2 if nch % 2 == 0 else 1))
            Bw = Buf("nw")
            if wrow is not None:
                wb = f32v(ncols)
                DMA("sp", wb, wrow.partition_broadcast(128), writes=[Bw])
            xt = [None if src_bf16 else f32v(ncols) for _ in range(2)]
            Bxt = [Buf("xt") for _ in range(2)]
            junk = None if src_bf16 else bf16v(ncols)
            Bj = Buf("junk")
            xn = [bf16v(ncols) for _ in range(2)]
            Bxn = [Buf("xn") for _ in range(2)]
            st = [f32v(4) for _ in range(2)]
            Bst = [Buf("st") for _ in range(2)]
            hs = [bf16v(ncols) for _ in range(2)]
            Bhs = [Buf("hs") for _ in range(2)]
            for t in range(ntiles):
                i = t % 2
                rows = rows_of(t)
                if src_bf16:
                    DMA("sp", xn[i][:rows, :], src_of(t), writes=[Bxn[i]])
                else:
                    DMA("sp", xt[i][:rows, :], src_of(t), writes=[Bxt[i]])
                    if wrow is not None:
                        O("pool", "memset", [], [Bst[i]], st[i], 0.0)
                        O("act", "activation", [Bxt[i]], [Bj, Bst[i]], out=junk[:rows, :], in_=xt[i][:rows, :],
                          func=AF.Square, accum_out=st[i][:rows, 0:1])
                        O("dve", "tensor_scalar", [Bst[i]], [Bst[i]], out=st[i][:rows, 1:2], in0=st[i][:rows, 0:1],
                          scalar1=1.0 / ncols, scalar2=EPS, op0=ALU.mult, op1=ALU.add)
                        O("act", "sqrt", [Bst[i]], [Bst[i]], out=st[i][:rows, 3:4], in_=st[i][:rows, 1:2])
                        O("dve", "reciprocal", [Bst[i]], [Bst[i]], out=st[i][:rows, 2:3], in_=st[i][:rows, 3:4])
                        O("dve", "scalar_tensor_tensor", [Bxt[i], Bst[i], Bw], [Bxn[i]], out=xn[i][:rows, :],
                          in0=xt[i][:rows, :], scalar=st[i][:rows, 2:3], in1=wb[:rows, :], op0=ALU.mult, op1=ALU.mult)
                    else:
                        O("dve", "tensor_copy", [Bxt[i]], [Bxn[i]], out=xn[i][:rows, :], in_=xt[i][:rows, :])
                for g in range(nch // grp):
                    bk = g % 2
                    pv = bank(bk).bitcast(BF16)
                    for kk in range(grp):
                        k = g * grp + kk
                        O("pe", "transpose", [Bxn[i], Bc], [pbuf[bk]], out=pv[:, kk * 128:kk * 128 + rows],
                          in_=xn[i][:rows, k * 128:(k + 1) * 128], identity=ident_b[:rows, :rows])
                    COPY("act" if g % 2 == 0 else "dve", hs[i][:, g * grp * 128:(g + 1) * grp * 128],
                         pv[:, :grp * 128], [pbuf[bk]], [Bhs[i]])
                DMA("sp", dstT[t, :, :], hs[i], reads=[Bhs[i]])
            P.barrier()
            top[0] = mark

        def proj_pass(srcT, KCH, weights, NCOL, CW, tiles_of_block, rows_of, evac):
            mark = top[0]
            nw = len(weights)
            wv = [[bf16v(KCH * CW) for _ in range(2)] for _ in range(nw)]
            Bwv = [[Buf("wv") for _ in range(2)] for _ in range(nw)]
            hb = [bf16v(KCH * 128) for _ in range(3)]
            Bhb = [Buf("hb") for _ in range(3)]
            nblk = (NCOL + CW - 1) // CW
            n_h = 0
            for cb in range(nblk):
                c0 = cb * CW
                cw = min(CW, NCOL - c0)
                wi = cb % 2
                wviews = []
                for j, w in enumerate(weights):
                    wview = wv[j][wi].rearrange("p (k c) -> p k c", k=KCH)
                    DMA("pool", wview[:, :, :cw], w.rearrange("(k p) c -> p k c", p=128)[:, :, c0:c0 + cw],
                        writes=[Bwv[j][wi]])
                    wviews.append(wview)
                for t in tiles_of_block(c0, cw):
                    rows = rows_of(t)
                    hi = n_h % 3
                    n_h += 1
                    hview = hb[hi].rearrange("p (k n) -> p k n", k=KCH)
                    DMA("sp", hb[hi], srcT[t, :, :], writes=[Bhb[hi]])
                    bks = []
                    for j in range(nw):
                        bk = 2 + (2 * n_h + j) % 6
                        bks.append(bk)
                        for k in range(KCH):
                            O("pe", "matmul", [Bhb[hi], Bwv[j][wi]], [pbuf[bk]], bank(bk)[:rows, :cw],
                              hview[:, k, :rows], wviews[j][:, k, :cw], start=(k == 0), stop=(k == KCH - 1))
                    evac(t, rows, c0, cw, bks, n_h)
            P.barrier()
            top[0] = mark

        if "A" in phases:
            normT_pass(lambda t: xq[t * 128:(t + 1) * 128, :] if t < NTILE else xs[:, :], NTILE + 1,
                       lambda t: 128 if t < NTILE else DEC, D, norm1_w[0:1, :], hT_d)

        if "B" in phases:
            segs = [("Q", 0, AW, pQ_d, True), ("K", c.oK, AW, pK_d, False), ("V", c.oV, AW, pV_d, False),
                    ("Z", c.oZ, SI, pZ_d, True), ("X", c.oX, CD, pX_d, False), ("DT", c.oDT, SH, pDT_d, False)]
            for name, s0, sw, dst, own_only in segs:
                ob = [f32v(512) for _ in range(3)]
                Bob = [Buf("ob") for _ in range(3)]

                def evacB(t, rows, c0, cw, bks, n, dst=dst, own_only=own_only, name=name, ob=ob, Bob=Bob):
                    i = n % 3
                    COPY("act" if n % 2 == 0 else "dve", ob[i][:rows, :cw], bank(bks[0])[:rows, :cw], [pbuf[bks[0]]], [Bob[i]])
                    if own_only:
                        r0 = (t - T0) * 128
                    elif name == "X":
                        r0 = 3 + t * 128 if t < NTILE else NTOK + 6
                    else:
                        r0 = t * 128
                    DMA("sp", dst[r0:r0 + rows, c0:c0 + cw], ob[i][:rows, :cw], reads=[Bob[i]])

                tiles = list(range(T0, NTILE + 1)) if own_only else list(range(NTILE + 1))
                proj_pass(hT_d, KC, [w_in[:, s0:s0 + sw]], sw, 512, lambda c0, cw, tiles=tiles: tiles,
                          lambda t: 128 if t < NTILE else DEC, evacB)
                top[0] = base_top
            DMA("sp", k_q[:, :], pK_d[T0 * 128:NTOK, :])
            DMA("sp", v_q[:, :], pV_d[T0 * 128:NTOK, :])
            DMA("sp", conv_p[:, :], pX_d[3 + NTOK - 3:3 + NTOK, :])
            DMA("sp", k_s[:, :], pK_d[NTOK:NTOK + DEC, :])
            DMA("sp", v_s[:, :], pV_d[NTOK:NTOK + DEC, :])
            DMA("sp", conv_s[:, :], pX_d[NTOK + 6 + DEC - 3:NTOK + 6 + DEC, :])
            zt = f32v(CD)
            Bz = Buf("z")
            O("pool", "memset", [], [Bz], zt[:3, :], 0.0)
            DMA("sp", pX_d[0:3, :], zt[:3, :], reads=[Bz])
            DMA("sp", pX_d[NTOK + 3:NTOK + 6, :], sconv[:, :])
            P.barrier()
            top[0] = base_top

        if "X" in phases:
            CC = min(1024, CD)
            for ch0 in range(0, CD, CC):
                cwt = [f32v(CC) for _ in range(5)]
                Bcw = Buf("cw")
                for i in range(4):
                    DMA("sp", cwt[i], conv_w[i:i + 1, ch0:ch0 + CC].partition_broadcast(128), writes=[Bcw])
                DMA("sp", cwt[4], conv_b[0:1, ch0:ch0 + CC].partition_broadcast(128), writes=[Bcw])
                win = [[f32v(CC) for _ in range(4)] for _ in range(2)]
                Bwin = [[Buf("win") for _ in range(4)] for _ in range(2)]
                for t in range(NTILE + 1):
                    rows = 128 if t < NTILE else DEC
                    r0 = 3 + t * 128 if t < NTILE else NTOK + 6
                    g0 = t * 128 if t < NTILE else NTOK
                    s = t % 2
                    for i in range(4):
                        DMA("sp", win[s][i][:rows, :], pX_d[r0 - 3 + i:r0 - 3 + i + rows, ch0:ch0 + CC], writes=[Bwin[s][i]])
                    for i in range(4):
                        O("pool", "tensor_tensor", [Bwin[s][i], Bcw], [Bwin[s][i]], out=win[s][i][:rows, :],
                          in0=win[s][i][:rows, :], in1=cwt[i][:rows, :], op=ALU.mult)
                    O("dve", "tensor_tensor", [Bwin[s][0], Bwin[s][1]], [Bwin[s][0]], out=win[s][0][:rows, :],
                      in0=win[s][0][:rows, :], in1=win[s][1][:rows, :], op=ALU.add)
                    O("dve", "tensor_tensor", [Bwin[s][2], Bwin[s][3]], [Bwin[s][2]], out=win[s][2][:rows, :],
                      in0=win[s][2][:rows, :], in1=win[s][3][:rows, :], op=ALU.add)
                    O("dve", "tensor_tensor", [Bwin[s][0], Bwin[s][2]], [Bwin[s][0]], out=win[s][0][:rows, :],
                      in0=win[s][0][:rows, :], in1=win[s][2][:rows, :], op=ALU.add)
                    O("dve", "tensor_tensor", [Bwin[s][0], Bcw], [Bwin[s][0]], out=win[s][0][:rows, :],
                      in0=win[s][0][:rows, :], in1=cwt[4][:rows, :], op=ALU.add)
                    O("act", "activation", [Bwin[s][0]], [Bwin[s][1]], out=win[s][1][:rows, :], in_=win[s][0][:rows, :],
                      func=AF.Silu)
                    DMA("sp", act_d[g0:g0 + rows, ch0:ch0 + CC], win[s][1][:rows, :], reads=[Bwin[s][1]])
                P.barrier()
                top[0] = base_top

        if "D" in phases:
            dtb, albc, dskb = f32v(SH), f32v(SH), f32v(SH)
            nwb = f32v(SI)
            Bk = Buf("ssdconst")
            DMA("sp", dtb, dt_bias[0:1, :].partition_broadcast(128), writes=[Bk])
            DMA("sp", albc, A_log[0:1, :].partition_broadcast(128), writes=[Bk])
            DMA("sp", dskb, D_skip[0:1, :].partition_broadcast(128), writes=[Bk])
            DMA("sp", nwb, ssd_norm_w[0:1, :].partition_broadcast(128), writes=[Bk])
            Abc = f32v(SH)
            O("act", "activation", [Bk], [Bk], out=Abc, in_=albc, func=AF.Exp)
            O("dve", "tensor_scalar", [Bk], [Bk], out=Abc, in0=Abc, scalar1=-1.0, scalar2=None, op0=ALU.mult)
            S = f32v(SI)
            BS = Buf("S")
            O("pool", "memset", [], [BS], S, 0.0)
            Sb = bf16v(SI)
            BSb = Buf("Sb")
            NB = 2
            xa = [f32v(SI) for _ in range(NB)]
            Ba = [f32v(GB) for _ in range(NB)]
            Ca = [f32v(GB) for _ in range(NB)]
            za = [f32v(SI) for _ in range(NB)]
            dtr = [f32v(SH) for _ in range(NB)]
            vl = [f32v(1) for _ in range(NB)]
            Bld = [Buf("ld") for _ in range(NB)]
            sm = [f32v(8 * SH) for _ in range(NB)]
            Bsm = [Buf("sm") for _ in range(NB)]
            xdt = [bf16v(SI) for _ in range(NB)]
            xdte = [bf16v(SI) for _ in range(NB)]
            Bb = [bf16v(GB) for _ in range(NB)]
            Cb = [bf16v(GB) for _ in range(NB)]
            Bx = [Buf("xd") for _ in range(NB)]
            Y = [f32v(SI) for _ in range(NB)]
            BY = [Buf("Y") for _ in range(NB)]
            yo = [bf16v(SI) for _ in range(NB)]
            Byo = [Buf("yo") for _ in range(NB)]
            BT, CT = bf16v(128), bf16v(128)
            BBT = Buf("BT")
            cbm = f32v(128)
            Bcbm = Buf("cbm")
            Lm = f32v(4 * 128)
            BL = Buf("L")
            Em = f32v(4 * 128)
            BE = Buf("E")
            Mm = bf16v(4 * 128)
            BM = Buf("M")
            tmpg = f32v(256)
            Btg = Buf("tg")
            gst = f32v(4 * G)
            Bgst = Buf("gst")
            tr = f32v(128)
            Btr = Buf("tr")

            def ssd_tile(t, n):
                i = n % NB
                own = t >= T0
                samp = t == NTILE
                rows = DEC if samp else 128
                g0 = NTOK if samp else t * 128
                o0 = (t - T0) * 128
                DMA("sp", xa[i][:rows, :], act_d[g0:g0 + rows, 0:SI], writes=[Bld[i]])
                DMA("sp", Ba[i][:rows, :], act_d[g0:g0 + rows, SI:SI + GB], writes=[Bld[i]])
                DMA("sp", dtr[i][:rows, :], pDT_d[g0:g0 + rows, :], writes=[Bld[i]])
                if samp:
                    O("pool", "memset", [], [Bld[i]], vl[i], 1.0)
                else:
                    DMA("sp", vl[i][:rows, :], valid[g0:g0 + rows, :], writes=[Bld[i]])
                if own:
                    DMA("sp", Ca[i][:rows, :], act_d[g0:g0 + rows, SI + GB:SI + 2 * GB], writes=[Bld[i]])
                    DMA("sp", za[i][:rows, :], pZ_d[o0:o0 + rows, :], writes=[Bld[i]])
                m = sm[i]
                dt_, dA_, w2_, eacs_, dte_, cdec_, tmp_ = (m[:, j * SH:(j + 1) * SH] for j in range(7))
                O("dve", "tensor_tensor", [Bld[i], Bk], [Bsm[i]], out=tmp_[:rows, :], in0=dtr[i][:rows, :], in1=dtb[:rows, :], op=ALU.add)
                O("act", "activation", [Bsm[i]], [Bsm[i]], out=tmp_[:rows, :], in_=tmp_[:rows, :], func=AF.Exp)
                O("act", "activation", [Bsm[i]], [Bsm[i]], out=tmp_[:rows, :], in_=tmp_[:rows, :], func=AF.Ln, bias=1.0)
                O("dve", "tensor_scalar", [Bsm[i], Bld[i]], [Bsm[i]], out=dt_[:rows, :], in0=tmp_[:rows, :],
                  scalar1=vl[i][:rows, 0:1], scalar2=None, op0=ALU.mult)
                O("dve", "tensor_tensor", [Bsm[i], Bk], [Bsm[i]], out=dA_[:rows, :], in0=dt_[:rows, :], in1=Abc[:rows, :], op=ALU.mult)
                pb = bank(0)
                O("pe", "matmul", [Bsm[i], Bc], [pbuf[0]], pb[:rows, 0:SH], tri_le[:rows, :rows], dA_[:rows, :], start=True, stop=True)
                O("pe", "matmul", [Bsm[i], Bc], [pbuf[0]], pb[:rows, SH:2 * SH], tri_gt[:rows, :rows], dA_[:rows, :], start=True, stop=True)
                O("pe", "matmul", [Bsm[i], Bc], [pbuf[0]], pb[:, 2 * SH:3 * SH], ones_f[:rows, :], dA_[:rows, :], start=True, stop=True)
                O("act", "activation", [pbuf[0]], [Bsm[i]], out=eacs_[:rows, :], in_=pb[:rows, 0:SH], func=AF.Exp)
                O("act", "activation", [pbuf[0]], [Bsm[i]], out=dte_[:rows, :], in_=pb[:rows, SH:2 * SH], func=AF.Exp)
                O("act", "activation", [pbuf[0]], [Bsm[i]], out=cdec_, in_=pb[:, 2 * SH:3 * SH], func=AF.Exp)
                O("dve", "tensor_tensor", [Bsm[i]], [Bsm[i]], out=w2_[:rows, :], in0=dt_[:rows, :], in1=dte_[:rows, :], op=ALU.mult)
                xa3 = xa[i].rearrange("p (h d) -> p h d", d=64)
                O("dve", "tensor_tensor", [Bld[i], Bsm[i]], [Bx[i]], out=xdt[i].rearrange("p (h d) -> p h d", d=64)[:rows],
                  in0=xa3[:rows], in1=dt_[:rows, :].unsqueeze(2).to_broadcast([rows, SH, 64]), op=ALU.mult)
                O("pool", "tensor_tensor", [Bld[i], Bsm[i]], [Bx[i]], out=xdte[i].rearrange("p (h d) -> p h d", d=64)[:rows],
                  in0=xa3[:rows], in1=w2_[:rows, :].unsqueeze(2).to_broadcast([rows, SH, 64]), op=ALU.mult)
                O("act", "copy", [Bld[i]], [Bx[i]], out=Bb[i][:rows, :], in_=Ba[i][:rows, :])
                if own:
                    O("act", "copy", [Bld[i]], [Bx[i]], out=Cb[i][:rows, :], in_=Ca[i][:rows, :])
                    O("pool", "tensor_copy", [BS], [BSb], out=Sb, in_=S)
                    for g in range(G):
                        pv = bank(1).bitcast(BF16)
                        O("pe", "transpose", [Bx[i], Bc], [pbuf[1]], out=pv[:, 0:rows], in_=Bb[i][:rows, g * 128:(g + 1) * 128], identity=ident_b[:rows, :rows])
                        O("pe", "transpose", [Bx[i], Bc], [pbuf[1]], out=pv[:, 128:128 + rows], in_=Cb[i][:rows, g * 128:(g + 1) * 128], identity=ident_b[:rows, :rows])
                        O("dve", "tensor_copy", [pbuf[1]], [BBT], out=BT[:, :rows], in_=pv[:, 0:rows])
                        O("dve", "tensor_copy", [pbuf[1]], [BBT], out=CT[:, :rows], in_=pv[:, 128:128 + rows])
                        O("pe", "matmul", [BBT], [pbuf[2]], bank(2)[:rows, :rows], BT[:, :rows], CT[:, :rows], start=True, stop=True)
                        O("dve", "tensor_tensor", [pbuf[2], Bc], [Bcbm], out=cbm[:rows, :rows], in0=bank(2)[:rows, :rows], in1=tri_le[:rows, :rows], op=ALU.mult)
                        for r in range(4):
                            h = 4 * g + r
                            O("pool" if r % 2 else "dve", "tensor_scalar", [Bc, Bsm[i]], [BL], out=Lm[:rows, r * 128:r * 128 + rows],
                              in0=tri_gt[:rows, :rows], scalar1=dA_[:rows, h:h + 1], scalar2=None, op0=ALU.mult)
                        for r in range(4):
                            O("pe", "matmul", [BL, Bc], [pbuf[3]], bank(3)[:rows, r * 128:r * 128 + rows], Lm[:rows, r * 128:r * 128 + rows],
                              tri_le[:rows, :rows], start=True, stop=True)
                        for r in range(4):
                            O("act", "activation", [pbuf[3]], [BE], out=Em[:rows, r * 128:r * 128 + rows], in_=bank(3)[:rows, r * 128:r * 128 + rows], func=AF.Exp)
                            O("dve", "tensor_tensor", [BE, Bcbm], [BM], out=Mm[:rows, r * 128:r * 128 + rows], in0=Em[:rows, r * 128:r * 128 + rows],
                              in1=cbm[:rows, :rows], op=ALU.mult)
                        for r in range(4):
                            h = 4 * g + r
                            O("pe", "matmul", [BM, Bx[i]], [pbuf[4]], bank(4)[:rows, r * 64:(r + 1) * 64], Mm[:rows, r * 128:r * 128 + rows],
                              xdt[i][:rows, h * 64:(h + 1) * 64], start=True, stop=True)
                        O("pe", "matmul", [BBT, BSb], [pbuf[4]], bank(4)[:rows, 256:512], CT[:, :rows], Sb[:, g * 256:(g + 1) * 256], start=True, stop=True)
                        O("dve", "tensor_tensor", [pbuf[4], Bsm[i]], [Btg], out=tmpg.rearrange("p (h d) -> p h d", d=64)[:rows],
                          in0=bank(4)[:, 256:512].rearrange("p (h d) -> p h d", d=64)[:rows],
                          in1=eacs_[:rows, 4 * g:4 * g + 4].unsqueeze(2).to_broadcast([rows, 4, 64]), op=ALU.mult)
                        O("dve", "tensor_tensor", [pbuf[4], Btg], [BY[i]], out=Y[i][:rows, g * 256:(g + 1) * 256], in0=bank(4)[:rows, 0:256],
                          in1=tmpg[:rows, :], op=ALU.add)
                O("dve", "tensor_tensor", [BS, Bsm[i], BSb], [BS], out=S.rearrange("p (h d) -> p h d", d=64),
                  in0=S.rearrange("p (h d) -> p h d", d=64), in1=cdec_.unsqueeze(2).to_broadcast([128, SH, 64]), op=ALU.mult)
                for gp in range(0, G, 2):
                    bk = 5 + (gp // 2) % 2
                    for g in (gp, gp + 1):
                        O("pe", "matmul", [Bx[i]], [pbuf[bk]], bank(bk)[:, (g - gp) * 256:(g - gp + 1) * 256], Bb[i][:rows, g * 128:(g + 1) * 128],
                          xdte[i][:rows, g * 256:(g + 1) * 256], start=True, stop=True)
                    O("dve", "tensor_tensor", [BS, pbuf[bk]], [BS], out=S[:, gp * 256:(gp + 2) * 256], in0=S[:, gp * 256:(gp + 2) * 256],
                      in1=bank(bk), op=ALU.add)
                if own:
                    O("pool", "tensor_tensor", [Bld[i], Bk, Bx[i]], [Bld[i]], out=xa3[:rows], in0=xa3[:rows],
                      in1=dskb[:rows, :].unsqueeze(2).to_broadcast([rows, SH, 64]), op=ALU.mult)
                    O("dve", "tensor_tensor", [BY[i], Bld[i]], [BY[i]], out=Y[i][:rows, :], in0=Y[i][:rows, :], in1=xa[i][:rows, :], op=ALU.add)
                    O("act", "activation", [Bld[i]], [Bld[i]], out=za[i][:rows, :], in_=za[i][:rows, :], func=AF.Silu)
                    O("dve", "tensor_tensor", [BY[i], Bld[i]], [BY[i]], out=Y[i][:rows, :], in0=Y[i][:rows, :], in1=za[i][:rows, :], op=ALU.mult)
                    O("pool", "memset", [], [Bgst], gst, 0.0)
                    for g in range(G):
                        O("act", "activation", [BY[i]], [Bld[i], Bgst], out=xa[i][:rows, g * 256:(g + 1) * 256], in_=Y[i][:rows, g * 256:(g + 1) * 256],
                          func=AF.Square, accum_out=gst[:rows, g:g + 1])
                    O("dve", "tensor_scalar", [Bgst], [Bgst], out=gst[:rows, G:2 * G], in0=gst[:rows, 0:G], scalar1=1.0 / 256, scalar2=EPS,
                      op0=ALU.mult, op1=ALU.add)
                    O("act", "sqrt", [Bgst], [Bgst], out=gst[:rows, 2 * G:3 * G], in_=gst[:rows, G:2 * G])
                    O("dve", "reciprocal", [Bgst], [Bgst], out=gst[:rows, 3 * G:4 * G], in_=gst[:rows, 2 * G:3 * G])
                    O("dve", "tensor_tensor", [BY[i], Bgst], [BY[i]], out=Y[i].rearrange("p (g d) -> p g d", d=256)[:rows],
                      in0=Y[i].rearrange("p (g d) -> p g d", d=256)[:rows],
                      in1=gst[:rows, 3 * G:4 * G].unsqueeze(2).to_broadcast([rows, G, 256]), op=ALU.mult)
                    O("dve", "tensor_tensor", [BY[i], Bk], [Byo[i]], out=yo[i][:rows, :], in0=Y[i][:rows, :], in1=nwb[:rows, :], op=ALU.mult)
                    DMA("sp", mix_d[o0:o0 + rows, AW:AW + SI], yo[i][:rows, :], reads=[Byo[i]])

            def state_out(dst):
                for j in range(SI // 128):
                    O("pe", "transpose", [BS, Bc], [pbuf[7]], out=bank(7)[:, 0:128], in_=S[:, j * 128:(j + 1) * 128], identity=ident_f)
                    O("dve", "tensor_copy", [pbuf[7]], [Btr], out=tr, in_=bank(7)[:, 0:128])
                    DMA("sp", dst[j * 128:(j + 1) * 128, :], tr, reads=[Btr])

            n = 0
            for t in range(NTILE):
                ssd_tile(t, n)
                n += 1
            state_out(ssm_p)
            for j in range(SI // 128):
                DMA("sp", tr, sssm[j * 128:(j + 1) * 128, :], writes=[Btr])
                O("pe", "transpose", [Btr, Bc], [pbuf[7]], out=bank(7)[:, 0:128], in_=tr, identity=ident_f)
                O("dve", "tensor_copy", [pbuf[7], BSb], [BS], out=S[:, j * 128:(j + 1) * 128], in_=bank(7)[:, 0:128])
            ssd_tile(NTILE, n)
            state_out(ssm_s)
            P.barrier()
            top[0] = base_top

        if "C" in phases:
            lq = f32v(256)
            lt = f32v(8)
            Bl = Buf("lam")
            DMA("sp", lq, lam_in[0:1, :].partition_broadcast(128), writes=[Bl])
            O("pool", "memset", [], [Bl], lt, 0.0)
            ljunk = f32v(64)
            O("dve", "tensor_tensor", [Bl], [Bl], out=lq[:, 0:64], in0=lq[:, 0:64], in1=lq[:, 64:128], op=ALU.mult)
            O("dve", "tensor_tensor", [Bl], [Bl], out=lq[:, 128:192], in0=lq[:, 128:192], in1=lq[:, 192:256], op=ALU.mult)
            O("act", "activation", [Bl], [Bl], out=ljunk, in_=lq[:, 0:64], func=AF.Copy, accum_out=lt[:, 0:1])
            O("act", "activation", [Bl], [Bl], out=ljunk, in_=lq[:, 128:192], func=AF.Copy, accum_out=lt[:, 1:2])
            O("act", "activation", [Bl], [Bl], out=lt[:, 2:4], in_=lt[:, 0:2], func=AF.Exp)
            O("dve", "tensor_tensor", [Bl], [Bl], out=lt[:, 4:5], in0=lt[:, 2:3], in1=lt[:, 3:4], op=ALU.subtract)
            O("dve", "tensor_scalar", [Bl], [Bl], out=lt[:, 5:6], in0=lt[:, 4:5], scalar1=LAM0, scalar2=-1.0, op0=ALU.add, op1=ALU.mult)
            nlam = lt[:, 5:6]
            slw = f32v(128)
            DMA("sp", slw, subln_w[0:1, :].partition_broadcast(128), writes=[Bl])
            O("dve", "tensor_scalar", [Bl], [Bl], out=slw, in0=slw, scalar1=1.0 - LAM0, scalar2=None, op0=ALU.mult)
            c_top = top[0]

            def prep(ksrc, vsrc, qsrc, vlsrc, rows, kt, KTd, VAd, QTd, qcol, n):
                i = n % 2
                kf, vf, qf = pk[i], pvv[i], pq[i]
                DMA("sp", kf[:rows, :], ksrc, writes=[Bpk[i]])
                DMA("sp", vf[:rows, :], vsrc, writes=[Bpk[i]])
                kb_, vb_ = pkb[i], pvb[i]
                O("act", "copy", [Bpk[i]], [Bpb[i]], out=kb_[:rows, :], in_=kf[:rows, :])
                vb3 = vb_.rearrange("p (h d) -> p h d", d=129)
                O("dve", "tensor_copy", [Bpk[i]], [Bpb[i]], out=vb3[:rows, :, 0:128], in_=vf.rearrange("p (h d) -> p h d", d=128)[:rows])
                if vlsrc is None:
                    O("pool", "memset", [], [Bpb[i]], vb3[:rows, :, 128:129], 1.0)
                else:
                    DMA("sp", pvl[i][:rows, :], vlsrc, writes=[Bpk[i]])
                    O("pool", "tensor_copy", [Bpk[i]], [Bpb[i]], out=vb3[:rows, :, 128:129],
                      in_=pvl[i][:rows, 0:1].unsqueeze(1).to_broadcast([rows, H, 1]))
                DMA("sp", VAd.rearrange("h p (t d) -> p h t d", d=129)[:rows, :, kt, :], vb3[:rows], reads=[Bpb[i]])
                srcs = [(kb_, KTd, kt * 128)]
                if qsrc is not None:
                    DMA("sp", qf[:rows, :], qsrc, writes=[Bpk[i]])
                    O("act", "activation", [Bpk[i]], [Bpb[i]], out=pqb[i][:rows, :], in_=qf[:rows, :], func=AF.Copy, scale=0.125)
                    srcs.append((pqb[i], QTd, qcol))
                for sb_, dstd, col in srcs:
                    for g in range((H + 7) // 8):
                        hh = min(8, H - g * 8)
                        pv_ = bank(7).bitcast(BF16)
                        for j in range(hh):
                            h = g * 8 + j
                            O("pe", "transpose", [Bpb[i], Bc], [pbuf[7]], out=pv_[:, j * 128:j * 128 + rows], in_=sb_[:rows, h * 128:(h + 1) * 128],
                              identity=ident_b[:rows, :rows])
                        O("dve", "tensor_copy", [pbuf[7]], [Bpt], out=ptt[:, :hh * 128], in_=pv_[:, :hh * 128])
                        DMA("sp", dstd.rearrange("h p n -> p h n")[:, g * 8:g * 8 + hh, col:col + rows],
                            ptt.rearrange("p (h n) -> p h n", n=128)[:, :hh, :rows], reads=[Bpt])

            pk = [f32v(AW) for _ in range(2)]
            pvv = [f32v(AW) for _ in range(2)]
            pq = [f32v(AW) for _ in range(2)]
            pvl = [f32v(1) for _ in range(2)]
            Bpk = [Buf("pk") for _ in range(2)]
            pkb = [bf16v(AW) for _ in range(2)]
            pqb = [bf16v(AW) for _ in range(2)]
            pvb = [bf16v(H * 129) for _ in range(2)]
            Bpb = [Buf("pb") for _ in range(2)]
            ptt = bf16v(1024)
            Bpt = Buf("pt")
            n = 0
            for t in range(NTILE):
                own = t >= T0
                prep(pK_d[t * 128:(t + 1) * 128, :], pV_d[t * 128:(t + 1) * 128, :],
                     pQ_d[(t - T0) * 128:(t - T0 + 1) * 128, :] if own else None, valid[t * 128:(t + 1) * 128, :],
                     128, t, KT_d, VA_d, QT_d, (t - T0) * 128, n)
                n += 1
            for kt in range(NKS - 1):
                prep(ck[kt * 128:(kt + 1) * 128, :], cv[kt * 128:(kt + 1) * 128, :], None, None, 128, kt, KTs_d, VAs_d, None, 0, n)
                n += 1
            prep(pK_d[NTOK:NTOK + DEC, :], pV_d[NTOK:NTOK + DEC, :], pQ_d[TQ:TQ + DEC, :], None, DEC, NKS - 1, KTs_d, VAs_d, QTs_d, 0, n)
            P.barrier()
            top[0] = c_top

            def attn(KTd, VAd, QTd, nq, nkt, rows_last, causal, orow0, hn):
                QB = min(512, nq)
                SW = min(128, QB)
                sub = QB // SW
                i = hn % 2
                kts = (nkt - 1) * 128 + rows_last
                DMA("sp", KTb[i][:, :kts], KTd[:, :kts], writes=[BKT[i]])
                DMA("sp", VAb[i][:, :nkt * 129], VAd[:, :nkt * 129], writes=[BKT[i]])
                DMA("sp", QTb[i][:, :nq], QTd[:, :nq], writes=[BKT[i]])
                VA3 = VAb[i].rearrange("p (t d) -> p t d", d=129)
                npair = 0
                for qb in range(nq // QB):
                    q0 = qb * QB
                    kt_last = (T0 + qb * sub + sub - 1) if causal else nkt - 1
                    d0 = (T0 + qb * sub) if causal else nkt

                    def acc(m, si):
                        if si < 3:
                            return 4 + m, si * 129
                        return 6, m * 129
                    for kt in range(kt_last + 1):
                        rk = rows_last if kt == nkt - 1 else 128
                        di = kt - d0 if kt >= d0 else -1
                        qlo = di * SW if di >= 0 else 0
                        ps = npair % 3
                        npair += 1
                        for m in range(2):
                            sbk = (npair % 2) * 2 + m
                            O("pe", "matmul", [BKT[i]], [pbuf[sbk]], bank(sbk)[:rk, qlo:QB], KTb[i][m * 64:(m + 1) * 64, kt * 128:kt * 128 + rk],
                              QTb[i][m * 64:(m + 1) * 64, q0 + qlo:q0 + QB], start=True, stop=True)
                            O("act", "activation", [pbuf[sbk]], [BpT[ps][m]], out=pT[ps][m][:rk, qlo:QB], in_=bank(sbk)[:rk, qlo:QB], func=AF.Exp)
                            if di >= 0:
                                O("pool", "memset", [], [BpT[ps][m]], pT[ps][m][64:128, qlo:qlo + 64], 0.0)
                        for si in range(max(di, 0), sub):
                            last_for_si = (d0 + si) if causal else nkt - 1
                            for m in range(2):
                                ab, ao = acc(m, si)
                                first_in_bank = (kt == 0) and ao == 0 and (ab != 6 or m == 0)
                                O("pe", "matmul", [BpT[ps][m], BKT[i]], [pbuf[ab]], bank(ab)[:SW, ao:ao + 129], pT[ps][m][:rk, si * SW:(si + 1) * SW],
                                  VA3[:rk, kt, :], start=first_in_bank, stop=(kt == last_for_si), skip_group_check=True)
                    for si in range(sub):
                        a1b, a1o = acc(0, si)
                        a2b, a2o = acc(1, si)
                        o1 = bank(a1b)[:SW, a1o:a1o + 129]
                        o2 = bank(a2b)[:SW, a2o:a2o + 129]
                        j = si % 2
                        s_ = fs[j]
                        O("dve", "reciprocal", [pbuf[a1b]], [Bfs[j]], out=s_[:SW, 0:1], in_=o1[:, 128:129])
                        O("dve", "reciprocal", [pbuf[a2b]], [Bfs[j]], out=s_[:SW, 1:2], in_=o2[:, 128:129])
                        O("dve", "tensor_tensor", [Bfs[j], Bl], [Bfs[j]], out=s_[:SW, 2:3], in0=s_[:SW, 1:2], in1=nlam[:SW, :], op=ALU.mult)
                        O("act", "activation", [pbuf[a1b], Bfs[j]], [Bfa[j]], out=fa[j][:SW, :], in_=o1[:, 0:128], func=AF.Copy, scale=s_[:SW, 0:1])
                        O("dve", "scalar_tensor_tensor", [pbuf[a2b], Bfs[j], Bfa[j]], [Bfa[j]], out=fa[j][:SW, :], in0=o2[:, 0:128],
                          scalar=s_[:SW, 2:3], in1=fa[j][:SW, :], op0=ALU.mult, op1=ALU.add)
                        O("pool", "memset", [], [Bfs[j]], s_[:, 3:4], 0.0)
                        O("act", "activation", [Bfa[j]], [Bfj, Bfs[j]], out=fj[:SW, :], in_=fa[j][:SW, :], func=AF.Square, accum_out=s_[:SW, 3:4])
                        O("dve", "tensor_scalar", [Bfs[j]], [Bfs[j]], out=s_[:SW, 4:5], in0=s_[:SW, 3:4], scalar1=1.0 / 128, scalar2=EPS,
                          op0=ALU.mult, op1=ALU.add)
                        O("act", "sqrt", [Bfs[j]], [Bfs[j]], out=s_[:SW, 5:6], in_=s_[:SW, 4:5])
                        O("dve", "reciprocal", [Bfs[j]], [Bfs[j]], out=s_[:SW, 6:7], in_=s_[:SW, 5:6])
                        O("dve", "scalar_tensor_tensor", [Bfa[j], Bfs[j], Bl], [Bfo[j]], out=fo[j][:SW, :], in0=fa[j][:SW, :], scalar=s_[:SW, 6:7],
                          in1=slw[:SW, :], op0=ALU.mult, op1=ALU.mult)
                        r0 = orow0 + q0 + si * SW
                        DMA("sp", mix_d[r0:r0 + SW, hcol[0]:hcol[0] + 128], fo[j][:SW, :], reads=[Bfo[j]])

            KTb = [bf16v(max(NTOK, NKS * 128)) for _ in range(2)]
            VAb = [bf16v(max(NTILE, NKS) * 129) for _ in range(2)]
            QTb = [bf16v(max(TQ, DEC)) for _ in range(2)]
            BKT = [Buf("KT") for _ in range(2)]
            pT = [[bf16v(512) for _ in range(2)] for _ in range(3)]
            BpT = [[Buf("pT") for _ in range(2)] for _ in range(3)]
            fs = [f32v(8) for _ in range(2)]
            Bfs = [Buf("fs") for _ in range(2)]
            fa = [f32v(128) for _ in range(2)]
            Bfa = [Buf("fa") for _ in range(2)]
            fj = f32v(128)
            Bfj = Buf("fj")
            fo = [bf16v(128) for _ in range(2)]
            Bfo = [Buf("fo") for _ in range(2)]
            hcol = [0]
            hn = 0
            for h in range(H):
                hcol[0] = h * 128
                attn(KT_d[h], VA_d[h], QT_d[h], TQ, NTILE, 128, True, 0, hn)
                hn += 1
                attn(KTs_d[h], VAs_d[h], QTs_d[h], DEC, NKS, DEC, False, TQ, hn)
                hn += 1
            P.barrier()
            top[0] = base_top

        if "E" in phases:
            def orow(ot):
                return ot * 128
            normT_pass(lambda ot: mix_d[ot * 128:ot * 128 + own_rows(ot), :], NT + 1, own_rows, D, None, mT_d, src_bf16=True)
            xr = [f32v(512) for _ in range(3)]
            Bxr = [Buf("xr") for _ in range(3)]

            def evacE(ot, rows, c0, cw, bks, n):
                i = n % 3
                src = xq[(T0 + ot) * 128:(T0 + ot) * 128 + rows, c0:c0 + cw] if ot < NT else xs[:, c0:c0 + cw]
                DMA("sp", xr[i][:rows, :cw], src, writes=[Bxr[i]])
                O("dve", "tensor_tensor", [pbuf[bks[0]], Bxr[i]], [Bxr[i]], out=xr[i][:rows, :cw], in0=bank(bks[0])[:rows, :cw],
                  in1=xr[i][:rows, :cw], op=ALU.add)
                DMA("sp", x1_d[ot * 128:ot * 128 + rows, c0:c0 + cw], xr[i][:rows, :cw], reads=[Bxr[i]])
            proj_pass(mT_d, KC, [w_out], D, 512, lambda c0, cw: list(range(NT + 1)), own_rows, evacE)
            top[0] = base_top
            normT_pass(lambda ot: x1_d[ot * 128:ot * 128 + own_rows(ot), :], NT + 1, own_rows, D, norm2_w[0:1, :], h2T_d)

        if "F" in phases:
            CWF = 256
            gs = [f32v(CWF) for _ in range(3)]
            go = [bf16v(CWF) for _ in range(3)]
            Bgs = [Buf("gs") for _ in range(3)]
            Bgo = [Buf("go") for _ in range(3)]

            def evacF1(ot, rows, c0, cw, bks, n):
                i = n % 3
                O("act", "activation", [pbuf[bks[0]]], [Bgs[i]], out=gs[i][:rows, :cw], in_=bank(bks[0])[:rows, :cw], func=AF.Silu)
                O("dve", "tensor_tensor", [Bgs[i], pbuf[bks[1]]], [Bgo[i]], out=go[i][:rows, :cw], in0=gs[i][:rows, :cw],
                  in1=bank(bks[1])[:rows, :cw], op=ALU.mult)
                DMA("sp", ff_d[ot * 128:ot * 128 + rows, c0:c0 + cw], go[i][:rows, :cw], reads=[Bgo[i]])
            proj_pass(h2T_d, KC, [w_gate, w_up], DFF, CWF, lambda c0, cw: list(range(NT + 1)), own_rows, evacF1)
            top[0] = base_top
            normT_pass(lambda ot: ff_d[ot * 128:ot * 128 + own_rows(ot), :], NT + 1, own_rows, DFF, None, ffT_d, src_bf16=True)
            xr = [f32v(CWF) for _ in range(3)]
            Bxr = [Buf("xr") for _ in range(3)]

            def evacF2(ot, rows, c0, cw, bks, n):
                i = n % 3
                DMA("sp", xr[i][:rows, :cw], x1_d[ot * 128:ot * 128 + rows, c0:c0 + cw], writes=[Bxr[i]])
                O("dve", "tensor_tensor", [pbuf[bks[0]], Bxr[i]], [Bxr[i]], out=xr[i][:rows, :cw], in0=bank(bks[0])[:rows, :cw],
                  in1=xr[i][:rows, :cw], op=ALU.add)
                DMA("sp", x2_d[ot * 128:ot * 128 + rows, c0:c0 + cw], xr[i][:rows, :cw], reads=[Bxr[i]])
            proj_pass(ffT_d, DFF // 128, [w_down], D, CWF, lambda c0, cw: list(range(NT + 1)), own_rows, evacF2)
            top[0] = base_top
            fw = f32v(D)
            Bfw = Buf("fw")
            DMA("sp", fw, final_norm_w[0:1, :].partition_broadcast(128), writes=[Bfw])
            xt = [f32v(D) for _ in range(2)]
            Bxt = [Buf("xt") for _ in range(2)]
            jk = bf16v(D)
            Bjk = Buf("jk")
            st = [f32v(4) for _ in range(2)]
            Bst = [Buf("st") for _ in range(2)]
            for ot in range(NT + 1):
                i = ot % 2
                rows = own_rows(ot)
                DMA("sp", xt[i][:rows, :], x2_d[ot * 128:ot * 128 + rows, :], writes=[Bxt[i]])
                O("pool", "memset", [], [Bst[i]], st[i], 0.0)
                O("act", "activation", [Bxt[i]], [Bjk, Bst[i]], out=jk[:rows, :], in_=xt[i][:rows, :], func=AF.Square, accum_out=st[i][:rows, 0:1])
                O("dve", "tensor_scalar", [Bst[i]], [Bst[i]], out=st[i][:rows, 1:2], in0=st[i][:rows, 0:1], scalar1=1.0 / D, scalar2=EPS,
                  op0=ALU.mult, op1=ALU.add)
                O("act", "sqrt", [Bst[i]], [Bst[i]], out=st[i][:rows, 3:4], in_=st[i][:rows, 1:2])
                O("dve", "reciprocal", [Bst[i]], [Bst[i]], out=st[i][:rows, 2:3], in_=st[i][:rows, 3:4])
                O("dve", "scalar_tensor_tensor", [Bxt[i], Bst[i], Bfw], [Bxt[i]], out=xt[i][:rows, :], in0=xt[i][:rows, :],
                  scalar=st[i][:rows, 2:3], in1=fw[:rows, :], op0=ALU.mult, op1=ALU.mult)
                dst = y_q[ot * 128:ot * 128 + rows, :] if ot < NT else y_s[:, :]
                DMA("sp", dst, xt[i][:rows, :], reads=[Bxt[i]])

        P.finish()
        P.emit()
    return nc


def host_inputs(cfg, inp):
    c = cfg
    f = lambda a: np.ascontiguousarray(np.asarray(a, dtype=np.float32))
    xp = f(inp["x_prompt"])
    B = xp.shape[0]
    ncores = B * NSLOT
    lam_in = np.concatenate([f(inp["lambda_q1"]), f(inp["lambda_k1"]), f(inp["lambda_q2"]), f(inp["lambda_k2"])], 0).reshape(1, 256)
    u = np.arange(128)
    consts = np.concatenate([np.eye(128), (u[:, None] <= u[None, :]), (u[:, None] > u[None, :]), np.ones((128, 128))], 1).astype(np.float32)
    shared = {
        "norm1_w": f(inp["norm1_w"]), "w_in": f(inp["w_in"])[0], "lam_in": lam_in,
        "subln_w": f(inp["subln_w"]), "conv_w": f(inp["conv_w"])[0], "conv_b": f(inp["conv_b"]),
        "dt_bias": f(inp["dt_bias"]), "A_log": f(inp["A_log"]), "D_skip": f(inp["D_skip"]),
        "ssd_norm_w": f(inp["ssd_norm_w"]), "w_out": f(inp["w_out"])[0], "norm2_w": f(inp["norm2_w"]),
        "w_gate": f(inp["w_gate"])[0], "w_up": f(inp["w_up"])[0], "w_down": f(inp["w_down"])[0],
        "final_norm_w": f(inp["final_norm_w"]).reshape(1, -1),
        "consts": consts,
    }
    maps = []
    for core in range(ncores):
        b, j = core // NSLOT, core % NSLOT
        xq = np.zeros((NSLOT * c.TQ, c.D), np.float32)
        valid = np.zeros((NSLOT * c.TQ, 1), np.float32)
        n = (j + 1) * c.TQ
        xq[NSLOT * c.TQ - n:] = xp[b, :n]
        valid[NSLOT * c.TQ - n:] = 1.0
        m = dict(shared)
        m.update({
            "xq": xq, "valid": valid, "xs": f(inp["x_sample"])[core],
            "ck": f(inp["cache_k"])[0, core].reshape(c.PAST, c.AW),
            "cv": f(inp["cache_v"])[0, core].reshape(c.PAST, c.AW),
            "sconv": f(inp["state_conv"])[0, core],
            "sssm": f(inp["state_ssm"])[0, core].reshape(c.SH * 64, 128),
        })
        maps.append(m)
    return maps


def assemble(cfg, res, B):
    c = cfg
    ncores = B * NSLOT
    g = lambda name: [np.asarray(res[i][name], dtype=np.float32) for i in range(ncores)]
    yq, ys, kq, vq, cp, sp_, ks, vs, cs, ss = (g(n) for n in
        ["y_q", "y_s", "k_q", "v_q", "conv_p", "ssm_p", "k_s", "v_s", "conv_s", "ssm_s"])
    cat = lambda lst, b: np.concatenate(lst[b * NSLOT:(b + 1) * NSLOT], 0)
    y_prompt = np.stack([cat(yq, b) for b in range(B)])
    k_prompt = np.stack([cat(kq, b) for b in range(B)]).reshape(1, B, c.SEQ, c.H, 128)
    v_prompt = np.stack([cat(vq, b) for b in range(B)]).reshape(1, B, c.SEQ, c.H, 128)
    conv_prompt = np.stack([cp[b * NSLOT + NSLOT - 1] for b in range(B)])[None]
    ssm_prompt = np.stack([sp_[b * NSLOT + NSLOT - 1] for b in range(B)]).reshape(1, B, c.SH, 64, 128)
    y_sample = np.stack(ys)
    k_sample = np.stack(ks).reshape(1, ncores, c.DEC, c.H, 128)
    v_sample = np.stack(vs).reshape(1, ncores, c.DEC, c.H, 128)
    conv_sample = np.stack(cs)[None]
    ssm_sample = np.stack(ss).reshape(1, ncores, c.SH, 64, 128)
    return (y_prompt, y_sample, k_prompt, v_prompt, conv_prompt, ssm_prompt,
            k_sample, v_sample, conv_sample, ssm_sample)


def run(cfg, inp, phases="ABXDCEF", dbg=()):
    B = np.asarray(inp["x_prompt"]).shape[0]
    maps = host_inputs(cfg, inp)
    nc = build(cfg, phases, dbg)
    res = run_bass_kernel_spmd(nc, maps, core_ids=list(range(len(maps))))
    if dbg:
        return assemble(cfg, res.results, B), res.results
    return assemble(cfg, res.results, B)


def kernel(**inputs):
    return run(Cfg(), inputs)
```

```python
import contextlib
import numpy as np
import concourse.bass as bass
import concourse.mybir as mybir
from concourse.bass_utils import run_bass_kernel_spmd

F32 = mybir.dt.float32
BF16 = mybir.dt.bfloat16
AF = mybir.ActivationFunctionType
ALU = mybir.AluOpType
EPS = 1e-6
NSLOT = 4
KDMA = 8


class Cfg:
    def __init__(self, D=4096, SEQ=8192, DFF=11008, PAST=2048, DEC=16, G=8):
        self.D, self.SEQ, self.DFF, self.PAST, self.DEC, self.G = D, SEQ, DFF, PAST, DEC, G
        self.AW = D // 2
        self.H = self.AW // 128
        self.SI = D - self.AW
        self.SH = self.SI // 64
        self.CD = self.SI + 2 * G * 128
        self.IN = 3 * self.AW + self.SI + self.CD + self.SH
        self.TQ = SEQ // NSLOT
        self.NT = self.TQ // 128
        self.KC = D // 128
        self.oQ, self.oK, self.oV, self.oZ = 0, self.AW, 2 * self.AW, 3 * self.AW
        self.oX = 3 * self.AW + self.SI
        self.oDT = self.oX + self.CD


class Buf:
    def __init__(self, name):
        self.name, self.w, self.r = name, None, []


class Prog:
    ENG = ["pe", "act", "dve", "pool", "sp"]

    def __init__(self, nc, stack):
        self.nc = nc
        self.ops = {e: [] for e in self.ENG}
        self.esem = {e: stack.enter_context(nc.semaphore("s_" + e)) for e in ["pe", "act", "dve", "pool"]}
        self.cnt = {e: 0 for e in self.esem}
        self.dsem = {q: [stack.enter_context(nc.semaphore("d_%s%d" % (q, i))) for i in range(KDMA)]
                     for q in ["sp", "pool", "act"]}
        self.dn = {"sp": 0, "pool": 0, "act": 0}
        self.seen = {e: {} for e in self.ENG}
        self.pend = {e: [] for e in self.ENG}

    def op(self, eng, fn, reads=(), writes=(), dma=False):
        waits = list(self.pend[eng])
        self.pend[eng] = []
        for b in reads:
            if b.w:
                waits.append(b.w)
        for b in writes:
            if b.w:
                waits.append(b.w)
            waits.extend(b.r)
        if dma:
            m = self.dn[eng]
            sem = self.dsem[eng][m % KDMA]
            prev = 16 * (m // KDMA)
            if prev:
                waits.append((sem, prev))
            ev = (sem, prev + 16)
            inc = 16
            self.dn[eng] += 1
        else:
            self.cnt[eng] += 1
            ev = (self.esem[eng], self.cnt[eng])
            inc = 1
        need = {}
        for s, v in waits:
            if eng == "pe" and s is self.esem["pe"]:
                continue
            if self.seen[eng].get(id(s), 0) >= v:
                continue
            if need.get(id(s), (s, 0))[1] < v:
                need[id(s)] = (s, v)
        for k, (s, v) in need.items():
            self.seen[eng][k] = v
        self.ops[eng].append((list(need.values()), fn, ev[0], inc))
        for b in reads:
            b.r.append(ev)
        for b in writes:
            b.w, b.r = ev, []
        return ev

    def all_events(self):
        evs = [(self.esem[e], self.cnt[e]) for e in self.esem if self.cnt[e]]
        for q in self.dsem:
            m = self.dn[q]
            for i in range(KDMA):
                n = (m - i + KDMA - 1) // KDMA if m > i else 0
                if n:
                    evs.append((self.dsem[q][i], 16 * n))
        return evs

    def barrier(self):
        evs = self.all_events()
        for e in self.ENG:
            self.pend[e] = list(evs)

    def finish(self):
        self.barrier()
        for e in self.ENG:
            need = {}
            for s, v in self.pend[e]:
                if self.seen[e].get(id(s), 0) < v:
                    need[id(s)] = (s, v)
            self.ops[e].append((list(need.values()), None, None, 0))

    def emit(self):
        nc = self.nc
        names = {"pe": "tensor", "act": "scalar", "dve": "vector", "pool": "gpsimd", "sp": "sync"}
        with nc.Block() as block:
            for e in self.ENG:
                def body(engobj, e=e):
                    for need, fn, sem, inc in self.ops[e]:
                        for s, v in need:
                            engobj.wait_ge(s, v)
                        if fn is not None:
                            fn(engobj).then_inc(sem, inc)
                getattr(block, names[e])(body)


def build(cfg, phases="ABXDCEF", dbg=()):
    c = cfg
    nc = bass.Bass("TRN2", target_bir_lowering=False)
    D, KC, IN, H, AW, SI, SH, G, CD, DFF = c.D, c.KC, c.IN, c.H, c.AW, c.SI, c.SH, c.G, c.CD, c.DFF
    TQ, NT, DEC, PAST = c.TQ, c.NT, c.DEC, c.PAST
    NTOK = NSLOT * TQ
    NTILE = NSLOT * NT
    T0 = NTILE - NT
    GB = G * 128
    NKS = PAST // 128 + 1
    LAM0 = 0.2

    def din(name, shape, dt=F32):
        return nc.dram_tensor(name, list(shape), dt, kind="ExternalInput").ap()

    def dout(name, shape, dt=F32):
        return nc.dram_tensor(name, list(shape), dt, kind="ExternalOutput").ap()

    def dscr(name, shape, dt):
        kind = "ExternalOutput" if name in dbg else "Internal"
        return nc.dram_tensor(name, list(shape), dt, kind=kind).ap()

    xq = din("xq", [NTOK, D])
    valid = din("valid", [NTOK, 1])
    xs = din("xs", [DEC, D])
    ck = din("ck", [PAST, AW])
    cv = din("cv", [PAST, AW])
    sconv = din("sconv", [3, CD])
    sssm = din("sssm", [SH * 64, 128])
    norm1_w = din("norm1_w", [1, D])
    w_in = din("w_in", [D, IN])
    lam_in = din("lam_in", [1, 256])
    subln_w = din("subln_w", [1, 128])
    conv_w = din("conv_w", [4, CD])
    conv_b = din("conv_b", [1, CD])
    dt_bias = din("dt_bias", [1, SH])
    A_log = din("A_log", [1, SH])
    D_skip = din("D_skip", [1, SH])
    ssd_norm_w = din("ssd_norm_w", [1, SI])
    w_out = din("w_out", [D, D])
    norm2_w = din("norm2_w", [1, D])
    w_gate = din("w_gate", [D, DFF])
    w_up = din("w_up", [D, DFF])
    w_down = din("w_down", [DFF, D])
    final_norm_w = din("final_norm_w", [1, D])
    consts_in = din("consts", [128, 4 * 128])

    y_q = dout("y_q", [TQ, D])
    y_s = dout("y_s", [DEC, D])
    k_q = dout("k_q", [TQ, AW])
    v_q = dout("v_q", [TQ, AW])
    conv_p = dout("conv_p", [3, CD])
    ssm_p = dout("ssm_p", [SH * 64, 128])
    k_s = dout("k_s", [DEC, AW])
    v_s = dout("v_s", [DEC, AW])
    conv_s = dout("conv_s", [3, CD])
    ssm_s = dout("ssm_s", [SH * 64, 128])

    hT_d = dscr("hT_d", [NTILE + 1, 128, KC * 128], BF16)
    pQ_d = dscr("pQ_d", [TQ + DEC, AW], F32)
    pK_d = dscr("pK_d", [NTOK + DEC, AW], F32)
    pV_d = dscr("pV_d", [NTOK + DEC, AW], F32)
    pZ_d = dscr("pZ_d", [TQ + DEC, SI], F32)
    pX_d = dscr("pX_d", [NTOK + DEC + 6, CD], F32)
    pDT_d = dscr("pDT_d", [NTOK + DEC, SH], F32)
    act_d = dscr("act_d", [NTOK + DEC, CD], F32)
    KT_d = dscr("KT_d", [H, 128, NTOK], BF16)
    QT_d = dscr("QT_d", [H, 128, TQ], BF16)
    VA_d = dscr("VA_d", [H, 128, NTILE * 129], BF16)
    KTs_d = dscr("KTs_d", [H, 128, NKS * 128], BF16)
    QTs_d = dscr("QTs_d", [H, 128, DEC], BF16)
    VAs_d = dscr("VAs_d", [H, 128, NKS * 129], BF16)
    mix_d = dscr("mix_d", [TQ + DEC, D], BF16)
    mT_d = dscr("mT_d", [NT + 1, 128, KC * 128], BF16)
    x1_d = dscr("x1_d", [TQ + DEC, D], F32)
    h2T_d = dscr("h2T_d", [NT + 1, 128, KC * 128], BF16)
    ff_d = dscr("ff_d", [TQ + DEC, DFF], BF16)
    ffT_d = dscr("ffT_d", [NT + 1, 128, DFF], BF16)
    x2_d = dscr("x2_d", [TQ + DEC, D], F32)

    with contextlib.ExitStack() as stack:
        P = Prog(nc, stack)
        ARENA = 44 * 1024
        arena = stack.enter_context(nc.sbuf_tensor("arena", [128, ARENA], F32))
        psum = stack.enter_context(nc.psum_tensor("psum", [128, 8 * 512], F32))
        top = [0]

        def f32v(words):
            o = top[0]
            top[0] += words
            assert top[0] <= ARENA, "SBUF arena overflow %d" % top[0]
            return arena[:, o:o + words]

        def bf16v(elems):
            return f32v((elems + 1) // 2).bitcast(BF16)[:, :elems]

        def bank(i):
            return psum[:, i * 512:(i + 1) * 512]

        pbuf = [Buf("ps%d" % i) for i in range(8)]

        def O(eng, method, reads, writes, *a, **kw):
            return P.op(eng, lambda e: getattr(e, method)(*a, **kw), reads, writes)

        def DMA(q, out, in_, reads=(), writes=()):
            return P.op(q, lambda e: e.dma_start(out=out, in_=in_), reads, writes, dma=True)

        def COPY(eng, out, in_, reads, writes):
            if eng == "act":
                return O("act", "copy", reads, writes, out=out, in_=in_)
            return O(eng, "tensor_copy", reads, writes, out=out, in_=in_)

        cst = f32v(512)
        Bc = Buf("consts")
        DMA("sp", cst, consts_in[:, :], writes=[Bc])
        ident_f, tri_le, tri_gt, ones_f = (cst[:, i * 128:(i + 1) * 128] for i in range(4))
        ident_b = bf16v(128)
        O("dve", "tensor_copy", [Bc], [Bc], out=ident_b, in_=ident_f)
        base_top = top[0]

        def own_rows(ot):
            return 128 if ot < NT else DEC

        def normT_pass(src_of, ntiles, rows_of, ncols, wrow, dstT, src_bf16=False):
            mark = top[0]
            nch = ncols // 128
            grp = 8 if nch % 8 == 0 else (4 if nch % 4 == 0 else (2 if nch % 2 == 0 else 1))
            Bw = Buf("nw")
            if wrow is not None:
                wb = f32v(ncols)
                DMA("sp", wb, wrow.partition_broadcast(128), writes=[Bw])
            xt = [None if src_bf16 else f32v(ncols) for _ in range(2)]
            Bxt = [Buf("xt") for _ in range(2)]
            junk = None if src_bf16 else bf16v(ncols)
            Bj = Buf("junk")
            xn = [bf16v(ncols) for _ in range(2)]
            Bxn = [Buf("xn") for _ in range(2)]
            st = [f32v(4) for _ in range(2)]
            Bst = [Buf("st") for _ in range(2)]
            hs = [bf16v(ncols) for _ in range(2)]
            Bhs = [Buf("hs") for _ in range(2)]
            for t in range(ntiles):
                i = t % 2
                rows = rows_of(t)
                if src_bf16:
                    DMA("sp", xn[i][:rows, :], src_of(t), writes=[Bxn[i]])
                else:
                    DMA("sp", xt[i][:rows, :], src_of(t), writes=[Bxt[i]])
                    if wrow is not None:
                        O("pool", "memset", [], [Bst[i]], st[i], 0.0)
                        O("act", "activation", [Bxt[i]], [Bj, Bst[i]], out=junk[:rows, :], in_=xt[i][:rows, :],
                          func=AF.Square, accum_out=st[i][:rows, 0:1])
                        O("dve", "tensor_scalar", [Bst[i]], [Bst[i]], out=st[i][:rows, 1:2], in0=st[i][:rows, 0:1],
                          scalar1=1.0 / ncols, scalar2=EPS, op0=ALU.mult, op1=ALU.add)
                        O("act", "sqrt", [Bst[i]], [Bst[i]], out=st[i][:rows, 3:4], in_=st[i][:rows, 1:2])
                        O("dve", "reciprocal", [Bst[i]], [Bst[i]], out=st[i][:rows, 2:3], in_=st[i][:rows, 3:4])
                        O("dve", "scalar_tensor_tensor", [Bxt[i], Bst[i], Bw], [Bxn[i]], out=xn[i][:rows, :],
                          in0=xt[i][:rows, :], scalar=st[i][:rows, 2:3], in1=wb[:rows, :], op0=ALU.mult, op1=ALU.mult)
                    else:
                        O("dve", "tensor_copy", [Bxt[i]], [Bxn[i]], out=xn[i][:rows, :], in_=xt[i][:rows, :])
                for g in range(nch // grp):
                    bk = g % 2
                    pv = bank(bk).bitcast(BF16)
                    for kk in range(grp):
                        k = g * grp + kk
                        O("pe", "transpose", [Bxn[i], Bc], [pbuf[bk]], out=pv[:, kk * 128:kk * 128 + rows],
                          in_=xn[i][:rows, k * 128:(k + 1) * 128], identity=ident_b[:rows, :rows])
                    COPY("act" if g % 2 == 0 else "dve", hs[i][:, g * grp * 128:(g + 1) * grp * 128],
                         pv[:, :grp * 128], [pbuf[bk]], [Bhs[i]])
                DMA("sp", dstT[t, :, :], hs[i], reads=[Bhs[i]])
            P.barrier()
            top[0] = mark

        def proj_pass(srcT, KCH, weights, NCOL, CW, tiles_of_block, rows_of, evac, wbufs=2, hbufs=3):
            mark = top[0]
            nw = len(weights)
            SUB = min(512, CW)
            wv = [[bf16v(KCH * CW) for _ in range(wbufs)] for _ in range(nw)]
            Bwv = [[Buf("wv") for _ in range(wbufs)] for _ in range(nw)]
            hb = [bf16v(KCH * 128) for _ in range(hbufs)]
            Bhb = [Buf("hb") for _ in range(hbufs)]
            nblk = (NCOL + CW - 1) // CW
            n_h = 0
            n_b = 0
            n_e = 0
            for cb in range(nblk):
                c0 = cb * CW
                cw = min(CW, NCOL - c0)
                wi = cb % wbufs
                wviews = []
                for j, w in enumerate(weights):
                    wview = wv[j][wi].rearrange("p (k c) -> p k c", k=KCH)
                    DMA("pool", wview[:, :, :cw], w.rearrange("(k p) c -> p k c", p=128)[:, :, c0:c0 + cw],
                        writes=[Bwv[j][wi]])
                    wviews.append(wview)
                for t in tiles_of_block(c0, cw):
                    rows = rows_of(t)
                    hi = n_h % hbufs
                    hview = hb[hi].rearrange("p (k n) -> p k n", k=KCH)
                    DMA("sp" if n_h % 2 == 0 else "act", hb[hi], srcT[t, :, :], writes=[Bhb[hi]])
                    n_h += 1
                    for s0 in range(0, cw, SUB):
                        sw = min(SUB, cw - s0)
                        bks = []
                        for j in range(nw):
                            bk = 2 + n_b % 6
                            n_b += 1
                            bks.append(bk)
                            for k in range(KCH):
                                O("pe", "matmul", [Bhb[hi], Bwv[j][wi]], [pbuf[bk]], bank(bk)[:rows, :sw],
                                  hview[:, k, :rows], wviews[j][:, k, s0:s0 + sw], start=(k == 0), stop=(k == KCH - 1))
                        n_e += 1
                        evac(t, rows, c0 + s0, sw, bks, n_e)
            P.barrier()
            top[0] = mark

        if "A" in phases:
            normT_pass(lambda t: xq[t * 128:(t + 1) * 128, :] if t < NTILE else xs[:, :], NTILE + 1,
                       lambda t: 128 if t < NTILE else DEC, D, norm1_w[0:1, :], hT_d)

        if "B" in phases:
            segs = [("Q", 0, AW, pQ_d, True), ("K", c.oK, AW, pK_d, False), ("V", c.oV, AW, pV_d, False),
                    ("Z", c.oZ, SI, pZ_d, True), ("X", c.oX, CD, pX_d, False), ("DT", c.oDT, SH, pDT_d, False)]
            for name, s0, sw, dst, own_only in segs:
                ob = [f32v(512) for _ in range(3)]
                Bob = [Buf("ob") for _ in range(3)]

                def evacB(t, rows, c0, cw, bks, n, dst=dst, own_only=own_only, name=name, ob=ob, Bob=Bob):
                    i = n % 3
                    COPY("act" if n % 2 == 0 else "dve", ob[i][:rows, :cw], bank(bks[0])[:rows, :cw], [pbuf[bks[0]]], [Bob[i]])
                    if own_only:
                        r0 = (t - T0) * 128
                    elif name == "X":
                        r0 = 3 + t * 128 if t < NTILE else NTOK + 6
                    else:
                        r0 = t * 128
                    DMA("sp", dst[r0:r0 + rows, c0:c0 + cw], ob[i][:rows, :cw], reads=[Bob[i]])

                tiles = list(range(T0, NTILE + 1)) if own_only else list(range(NTILE + 1))
                proj_pass(hT_d, KC, [w_in[:, s0:s0 + sw]], sw, 1024, lambda c0, cw, tiles=tiles: tiles,
                          lambda t: 128 if t < NTILE else DEC, evacB)
                top[0] = base_top
            DMA("sp", k_q[:, :], pK_d[T0 * 128:NTOK, :])
            DMA("sp", v_q[:, :], pV_d[T0 * 128:NTOK, :])
            DMA("sp", conv_p[:, :], pX_d[3 + NTOK - 3:3 + NTOK, :])
            DMA("sp", k_s[:, :], pK_d[NTOK:NTOK + DEC, :])
            DMA("sp", v_s[:, :], pV_d[NTOK:NTOK + DEC, :])
            DMA("sp", conv_s[:, :], pX_d[NTOK + 6 + DEC - 3:NTOK + 6 + DEC, :])
            zt = f32v(CD)
            Bz = Buf("z")
            O("pool", "memset", [], [Bz], zt[:3, :], 0.0)
            DMA("sp", pX_d[0:3, :], zt[:3, :], reads=[Bz])
            DMA("sp", pX_d[NTOK + 3:NTOK + 6, :], sconv[:, :])
            P.barrier()
            top[0] = base_top

        if "X" in phases:
            CC = min(1024, CD)
            for ch0 in range(0, CD, CC):
                cwt = [f32v(CC) for _ in range(5)]
                Bcw = Buf("cw")
                for i in range(4):
                    DMA("sp", cwt[i], conv_w[i:i + 1, ch0:ch0 + CC].partition_broadcast(128), writes=[Bcw])
                DMA("sp", cwt[4], conv_b[0:1, ch0:ch0 + CC].partition_broadcast(128), writes=[Bcw])
                win = [[f32v(CC) for _ in range(4)] for _ in range(2)]
                Bwin = [[Buf("win") for _ in range(4)] for _ in range(2)]
                for t in range(NTILE + 1):
                    rows = 128 if t < NTILE else DEC
                    r0 = 3 + t * 128 if t < NTILE else NTOK + 6
                    g0 = t * 128 if t < NTILE else NTOK
                    s = t % 2
                    for i in range(4):
                        DMA("sp", win[s][i][:rows, :], pX_d[r0 - 3 + i:r0 - 3 + i + rows, ch0:ch0 + CC], writes=[Bwin[s][i]])
                    for i in range(4):
                        O("pool", "tensor_tensor", [Bwin[s][i], Bcw], [Bwin[s][i]], out=win[s][i][:rows, :],
                          in0=win[s][i][:rows, :], in1=cwt[i][:rows, :], op=ALU.mult)
                    O("dve", "tensor_tensor", [Bwin[s][0], Bwin[s][1]], [Bwin[s][0]], out=win[s][0][:rows, :],
                      in0=win[s][0][:rows, :], in1=win[s][1][:rows, :], op=ALU.add)
                    O("dve", "tensor_tensor", [Bwin[s][2], Bwin[s][3]], [Bwin[s][2]], out=win[s][2][:rows, :],
                      in0=win[s][2][:rows, :], in1=win[s][3][:rows, :], op=ALU.add)
                    O("dve", "tensor_tensor", [Bwin[s][0], Bwin[s][2]], [Bwin[s][0]], out=win[s][0][:rows, :],
                      in0=win[s][0][:rows, :], in1=win[s][2][:rows, :], op=ALU.add)
                    O("dve", "tensor_tensor", [Bwin[s][0], Bcw], [Bwin[s][0]], out=win[s][0][:rows, :],
                      in0=win[s][0][:rows, :], in1=cwt[4][:rows, :], op=ALU.add)
                    O("act", "activation", [Bwin[s][0]], [Bwin[s][1]], out=win[s][1][:rows, :], in_=win[s][0][:rows, :],
                      func=AF.Silu)
                    DMA("sp", act_d[g0:g0 + rows, ch0:ch0 + CC], win[s][1][:rows, :], reads=[Bwin[s][1]])
                P.barrier()
                top[0] = base_top

        if "D" in phases:
            dtb, albc, dskb = f32v(SH), f32v(SH), f32v(SH)
            nwb = f32v(SI)
            Bk = Buf("ssdconst")
            DMA("sp", dtb, dt_bias[0:1, :].partition_broadcast(128), writes=[Bk])
            DMA("sp", albc, A_log[0:1, :].partition_broadcast(128), writes=[Bk])
            DMA("sp", dskb, D_skip[0:1, :].partition_broadcast(128), writes=[Bk])
            DMA("sp", nwb, ssd_norm_w[0:1, :].partition_broadcast(128), writes=[Bk])
            Abc = f32v(SH)
            O("act", "activation", [Bk], [Bk], out=Abc, in_=albc, func=AF.Exp)
            O("dve", "tensor_scalar", [Bk], [Bk], out=Abc, in0=Abc, scalar1=-1.0, scalar2=None, op0=ALU.mult)
            S = f32v(SI)
            BS = Buf("S")
            O("pool", "memset", [], [BS], S, 0.0)
            Sb = bf16v(SI)
            BSb = Buf("Sb")
            NB = 2
            xa = [f32v(SI) for _ in range(NB)]
            Ba = [f32v(GB) for _ in range(NB)]
            Ca = [f32v(GB) for _ in range(NB)]
            za = [f32v(SI) for _ in range(NB)]
            dtr = [f32v(SH) for _ in range(NB)]
            vl = [f32v(1) for _ in range(NB)]
            Bld = [Buf("ld") for _ in range(NB)]
            sm = [f32v(8 * SH) for _ in range(NB)]
            Bsm = [Buf("sm") for _ in range(NB)]
            xdt = [bf16v(SI) for _ in range(NB)]
            xdte = [bf16v(SI) for _ in range(NB)]
            Bb = [bf16v(GB) for _ in range(NB)]
            Cb = [bf16v(GB) for _ in range(NB)]
            Bx = [Buf("xd") for _ in range(NB)]
            Y = [f32v(SI) for _ in range(NB)]
            BY = [Buf("Y") for _ in range(NB)]
            yo = [bf16v(SI) for _ in range(NB)]
            Byo = [Buf("yo") for _ in range(NB)]
            BT, CT = bf16v(128), bf16v(128)
            BBT = Buf("BT")
            cbm = f32v(128)
            Bcbm = Buf("cbm")
            Lm = f32v(4 * 128)
            BL = Buf("L")
            Em = f32v(4 * 128)
            BE = Buf("E")
            Mm = bf16v(4 * 128)
            BM = Buf("M")
            tmpg = f32v(256)
            Btg = Buf("tg")
            gst = f32v(4 * G)
            Bgst = Buf("gst")
            tr = f32v(128)
            Btr = Buf("tr")

            def ssd_tile(t, n):
                i = n % NB
                own = t >= T0
                samp = t == NTILE
                rows = DEC if samp else 128
                g0 = NTOK if samp else t * 128
                o0 = (t - T0) * 128
                DMA("sp", xa[i][:rows, :], act_d[g0:g0 + rows, 0:SI], writes=[Bld[i]])
                DMA("sp", Ba[i][:rows, :], act_d[g0:g0 + rows, SI:SI + GB], writes=[Bld[i]])
                DMA("sp", dtr[i][:rows, :], pDT_d[g0:g0 + rows, :], writes=[Bld[i]])
                if samp:
                    O("pool", "memset", [], [Bld[i]], vl[i], 1.0)
                else:
                    DMA("sp", vl[i][:rows, :], valid[g0:g0 + rows, :], writes=[Bld[i]])
                if own:
                    DMA("sp", Ca[i][:rows, :], act_d[g0:g0 + rows, SI + GB:SI + 2 * GB], writes=[Bld[i]])
                    DMA("sp", za[i][:rows, :], pZ_d[o0:o0 + rows, :], writes=[Bld[i]])
                m = sm[i]
                dt_, dA_, w2_, eacs_, dte_, cdec_, tmp_ = (m[:, j * SH:(j + 1) * SH] for j in range(7))
                O("dve", "tensor_tensor", [Bld[i], Bk], [Bsm[i]], out=tmp_[:rows, :], in0=dtr[i][:rows, :], in1=dtb[:rows, :], op=ALU.add)
                O("act", "activation", [Bsm[i]], [Bsm[i]], out=tmp_[:rows, :], in_=tmp_[:rows, :], func=AF.Exp)
                O("act", "activation", [Bsm[i]], [Bsm[i]], out=tmp_[:rows, :], in_=tmp_[:rows, :], func=AF.Ln, bias=1.0)
                O("dve", "tensor_scalar", [Bsm[i], Bld[i]], [Bsm[i]], out=dt_[:rows, :], in0=tmp_[:rows, :],
                  scalar1=vl[i][:rows, 0:1], scalar2=None, op0=ALU.mult)
                O("dve", "tensor_tensor", [Bsm[i], Bk], [Bsm[i]], out=dA_[:rows, :], in0=dt_[:rows, :], in1=Abc[:rows, :], op=ALU.mult)
                pb = bank(0)
                O("pe", "matmul", [Bsm[i], Bc], [pbuf[0]], pb[:rows, 0:SH], tri_le[:rows, :rows], dA_[:rows, :], start=True, stop=True)
                O("pe", "matmul", [Bsm[i], Bc], [pbuf[0]], pb[:rows, SH:2 * SH], tri_gt[:rows, :rows], dA_[:rows, :], start=True, stop=True)
                O("pe", "matmul", [Bsm[i], Bc], [pbuf[0]], pb[:, 2 * SH:3 * SH], ones_f[:rows, :], dA_[:rows, :], start=True, stop=True)
                O("act", "activation", [pbuf[0]], [Bsm[i]], out=eacs_[:rows, :], in_=pb[:rows, 0:SH], func=AF.Exp)
                O("act", "activation", [pbuf[0]], [Bsm[i]], out=dte_[:rows, :], in_=pb[:rows, SH:2 * SH], func=AF.Exp)
                O("act", "activation", [pbuf[0]], [Bsm[i]], out=cdec_, in_=pb[:, 2 * SH:3 * SH], func=AF.Exp)
                O("dve", "tensor_tensor", [Bsm[i]], [Bsm[i]], out=w2_[:rows, :], in0=dt_[:rows, :], in1=dte_[:rows, :], op=ALU.mult)
                xa3 = xa[i].rearrange("p (h d) -> p h d", d=64)
                O("dve", "tensor_tensor", [Bld[i], Bsm[i]], [Bx[i]], out=xdt[i].rearrange("p (h d) -> p h d", d=64)[:rows],
                  in0=xa3[:rows], in1=dt_[:rows, :].unsqueeze(2).to_broadcast([rows, SH, 64]), op=ALU.mult)
                O("pool", "tensor_tensor", [Bld[i], Bsm[i]], [Bx[i]], out=xdte[i].rearrange("p (h d) -> p h d", d=64)[:rows],
                  in0=xa3[:rows], in1=w2_[:rows, :].unsqueeze(2).to_broadcast([rows, SH, 64]), op=ALU.mult)
                O("act", "copy", [Bld[i]], [Bx[i]], out=Bb[i][:rows, :], in_=Ba[i][:rows, :])
                if own:
                    O("act", "copy", [Bld[i]], [Bx[i]], out=Cb[i][:rows, :], in_=Ca[i][:rows, :])
                    O("pool", "tensor_copy", [BS], [BSb], out=Sb, in_=S)
                    for g in range(G):
                        pv = bank(1).bitcast(BF16)
                        O("pe", "transpose", [Bx[i], Bc], [pbuf[1]], out=pv[:, 0:rows], in_=Bb[i][:rows, g * 128:(g + 1) * 128], identity=ident_b[:rows, :rows])
                        O("pe", "transpose", [Bx[i], Bc], [pbuf[1]], out=pv[:, 128:128 + rows], in_=Cb[i][:rows, g * 128:(g + 1) * 128], identity=ident_b[:rows, :rows])
                        O("dve", "tensor_copy", [pbuf[1]], [BBT], out=BT[:, :rows], in_=pv[:, 0:rows])
                        O("dve", "tensor_copy", [pbuf[1]], [BBT], out=CT[:, :rows], in_=pv[:, 128:128 + rows])
                        O("pe", "matmul", [BBT], [pbuf[2]], bank(2)[:rows, :rows], BT[:, :rows], CT[:, :rows], start=True, stop=True)
                        O("dve", "tensor_tensor", [pbuf[2], Bc], [Bcbm], out=cbm[:rows, :rows], in0=bank(2)[:rows, :rows], in1=tri_le[:rows, :rows], op=ALU.mult)
                        for r in range(4):
                            h = 4 * g + r
                            O("pool" if r % 2 else "dve", "tensor_scalar", [Bc, Bsm[i]], [BL], out=Lm[:rows, r * 128:r * 128 + rows],
                              in0=tri_gt[:rows, :rows], scalar1=dA_[:rows, h:h + 1], scalar2=None, op0=ALU.mult)
                        for r in range(4):
                            O("pe", "matmul", [BL, Bc], [pbuf[3]], bank(3)[:rows, r * 128:r * 128 + rows], Lm[:rows, r * 128:r * 128 + rows],
                              tri_le[:rows, :rows], start=True, stop=True)
                        for r in range(4):
                            O("act", "activation", [pbuf[3]], [BE], out=Em[:rows, r * 128:r * 128 + rows], in_=bank(3)[:rows, r * 128:r * 128 + rows], func=AF.Exp)
                            O("dve", "tensor_tensor", [BE, Bcbm], [BM], out=Mm[:rows, r * 128:r * 128 + rows], in0=Em[:rows, r * 128:r * 128 + rows],
                              in1=cbm[:rows, :rows], op=ALU.mult)
                        for r in range(4):
                            h = 4 * g + r
                            O("pe", "matmul", [BM, Bx[i]], [pbuf[4]], bank(4)[:rows, r * 64:(r + 1) * 64], Mm[:rows, r * 128:r * 128 + rows],
                              xdt[i][:rows, h * 64:(h + 1) * 64], start=True, stop=True)
                        O("pe", "matmul", [BBT, BSb], [pbuf[4]], bank(4)[:rows, 256:512], CT[:, :rows], Sb[:, g * 256:(g + 1) * 256], start=True, stop=True)
                        O("dve", "tensor_tensor", [pbuf[4], Bsm[i]], [Btg], out=tmpg.rearrange("p (h d) -> p h d", d=64)[:rows],
                          in0=bank(4)[:, 256:512].rearrange("p (h d) -> p h d", d=64)[:rows],
                          in1=eacs_[:rows, 4 * g:4 * g + 4].unsqueeze(2).to_broadcast([rows, 4, 64]), op=ALU.mult)
                        O("dve", "tensor_tensor", [pbuf[4], Btg], [BY[i]], out=Y[i][:rows, g * 256:(g + 1) * 256], in0=bank(4)[:rows, 0:256],
                          in1=tmpg[:rows, :], op=ALU.add)
                O("dve", "tensor_tensor", [BS, Bsm[i], BSb], [BS], out=S.rearrange("p (h d) -> p h d", d=64),
                  in0=S.rearrange("p (h d) -> p h d", d=64), in1=cdec_.unsqueeze(2).to_broadcast([128, SH, 64]), op=ALU.mult)
                for gp in range(0, G, 2):
                    bk = 5 + (gp // 2) % 2
                    for g in (gp, gp + 1):
                        O("pe", "matmul", [Bx[i]], [pbuf[bk]], bank(bk)[:, (g - gp) * 256:(g - gp + 1) * 256], Bb[i][:rows, g * 128:(g + 1) * 128],
                          xdte[i][:rows, g * 256:(g + 1) * 256], start=True, stop=True)
                    O("dve", "tensor_tensor", [BS, pbuf[bk]], [BS], out=S[:, gp * 256:(gp + 2) * 256], in0=S[:, gp * 256:(gp + 2) * 256],
                      in1=bank(bk), op=ALU.add)
                if own:
                    O("pool", "tensor_tensor", [Bld[i], Bk, Bx[i]], [Bld[i]], out=xa3[:rows], in0=xa3[:rows],
                      in1=dskb[:rows, :].unsqueeze(2).to_broadcast([rows, SH, 64]), op=ALU.mult)
                    O("dve", "tensor_tensor", [BY[i], Bld[i]], [BY[i]], out=Y[i][:rows, :], in0=Y[i][:rows, :], in1=xa[i][:rows, :], op=ALU.add)
                    O("act", "activation", [Bld[i]], [Bld[i]], out=za[i][:rows, :], in_=za[i][:rows, :], func=AF.Silu)
                    O("dve", "tensor_tensor", [BY[i], Bld[i]], [BY[i]], out=Y[i][:rows, :], in0=Y[i][:rows, :], in1=za[i][:rows, :], op=ALU.mult)
                    O("pool", "memset", [], [Bgst], gst, 0.0)
                    for g in range(G):
                        O("act", "activation", [BY[i]], [Bld[i], Bgst], out=xa[i][:rows, g * 256:(g + 1) * 256], in_=Y[i][:rows, g * 256:(g + 1) * 256],
                          func=AF.Square, accum_out=gst[:rows, g:g + 1])
                    O("dve", "tensor_scalar", [Bgst], [Bgst], out=gst[:rows, G:2 * G], in0=gst[:rows, 0:G], scalar1=1.0 / 256, scalar2=EPS,
                      op0=ALU.mult, op1=ALU.add)
                    O("act", "sqrt", [Bgst], [Bgst], out=gst[:rows, 2 * G:3 * G], in_=gst[:rows, G:2 * G])
                    O("dve", "reciprocal", [Bgst], [Bgst], out=gst[:rows, 3 * G:4 * G], in_=gst[:rows, 2 * G:3 * G])
                    O("dve", "tensor_tensor", [BY[i], Bgst], [BY[i]], out=Y[i].rearrange("p (g d) -> p g d", d=256)[:rows],
                      in0=Y[i].rearrange("p (g d) -> p g d", d=256)[:rows],
                      in1=gst[:rows, 3 * G:4 * G].unsqueeze(2).to_broadcast([rows, G, 256]), op=ALU.mult)
                    O("dve", "tensor_tensor", [BY[i], Bk], [Byo[i]], out=yo[i][:rows, :], in0=Y[i][:rows, :], in1=nwb[:rows, :], op=ALU.mult)
                    DMA("sp", mix_d[o0:o0 + rows, AW:AW + SI], yo[i][:rows, :], reads=[Byo[i]])

            def state_out(dst):
                for j in range(SI // 128):
                    O("pe", "transpose", [BS, Bc], [pbuf[7]], out=bank(7)[:, 0:128], in_=S[:, j * 128:(j + 1) * 128], identity=ident_f)
                    O("dve", "tensor_copy", [pbuf[7]], [Btr], out=tr, in_=bank(7)[:, 0:128])
                    DMA("sp", dst[j * 128:(j + 1) * 128, :], tr, reads=[Btr])

            n = 0
            for t in range(NTILE):
                ssd_tile(t, n)
                n += 1
            state_out(ssm_p)
            for j in range(SI // 128):
                DMA("sp", tr, sssm[j * 128:(j + 1) * 128, :], writes=[Btr])
                O("pe", "transpose", [Btr, Bc], [pbuf[7]], out=bank(7)[:, 0:128], in_=tr, identity=ident_f)
                O("dve", "tensor_copy", [pbuf[7], BSb], [BS], out=S[:, j * 128:(j + 1) * 128], in_=bank(7)[:, 0:128])
            ssd_tile(NTILE, n)
            state_out(ssm_s)
            P.barrier()
            top[0] = base_top

        if "C" in phases:
            lq = f32v(256)
            lt = f32v(8)
            Bl = Buf("lam")
            DMA("sp", lq, lam_in[0:1, :].partition_broadcast(128), writes=[Bl])
            O("pool", "memset", [], [Bl], lt, 0.0)
            ljunk = f32v(64)
            O("dve", "tensor_tensor", [Bl], [Bl], out=lq[:, 0:64], in0=lq[:, 0:64], in1=lq[:, 64:128], op=ALU.mult)
            O("dve", "tensor_tensor", [Bl], [Bl], out=lq[:, 128:192], in0=lq[:, 128:192], in1=lq[:, 192:256], op=ALU.mult)
            O("act", "activation", [Bl], [Bl], out=ljunk, in_=lq[:, 0:64], func=AF.Copy, accum_out=lt[:, 0:1])
            O("act", "activation", [Bl], [Bl], out=ljunk, in_=lq[:, 128:192], func=AF.Copy, accum_out=lt[:, 1:2])
            O("act", "activation", [Bl], [Bl], out=lt[:, 2:4], in_=lt[:, 0:2], func=AF.Exp)
            O("dve", "tensor_tensor", [Bl], [Bl], out=lt[:, 4:5], in0=lt[:, 2:3], in1=lt[:, 3:4], op=ALU.subtract)
            O("dve", "tensor_scalar", [Bl], [Bl], out=lt[:, 5:6], in0=lt[:, 4:5], scalar1=LAM0, scalar2=-1.0, op0=ALU.add, op1=ALU.mult)
            nlam = lt[:, 5:6]
            slw = f32v(128)
            DMA("sp", slw, subln_w[0:1, :].partition_broadcast(128), writes=[Bl])
            O("dve", "tensor_scalar", [Bl], [Bl], out=slw, in0=slw, scalar1=1.0 - LAM0, scalar2=None, op0=ALU.mult)
            c_top = top[0]

            def prep(ksrc, vsrc, qsrc, vlsrc, rows, kt, KTd, VAd, QTd, qcol, n):
                i = n % 2
                kf, vf, qf = pk[i], pvv[i], pq[i]
                DMA("sp", kf[:rows, :], ksrc, writes=[Bpk[i]])
                DMA("sp", vf[:rows, :], vsrc, writes=[Bpk[i]])
                kb_, vb_ = pkb[i], pvb[i]
                O("act", "copy", [Bpk[i]], [Bpb[i]], out=kb_[:rows, :], in_=kf[:rows, :])
                vb3 = vb_.rearrange("p (h d) -> p h d", d=129)
                O("dve", "tensor_copy", [Bpk[i]], [Bpb[i]], out=vb3[:rows, :, 0:128], in_=vf.rearrange("p (h d) -> p h d", d=128)[:rows])
                if vlsrc is None:
                    O("pool", "memset", [], [Bpb[i]], vb3[:rows, :, 128:129], 1.0)
                else:
                    DMA("sp", pvl[i][:rows, :], vlsrc, writes=[Bpk[i]])
                    O("pool", "tensor_copy", [Bpk[i]], [Bpb[i]], out=vb3[:rows, :, 128:129],
                      in_=pvl[i][:rows, 0:1].unsqueeze(1).to_broadcast([rows, H, 1]))
                DMA("sp", VAd.rearrange("h p (t d) -> p h t d", d=129)[:rows, :, kt, :], vb3[:rows], reads=[Bpb[i]])
                srcs = [(kb_, KTd, kt * 128)]
                if qsrc is not None:
                    DMA("sp", qf[:rows, :], qsrc, writes=[Bpk[i]])
                    O("act", "activation", [Bpk[i]], [Bpb[i]], out=pqb[i][:rows, :], in_=qf[:rows, :], func=AF.Copy, scale=0.125)
                    srcs.append((pqb[i], QTd, qcol))
                for sb_, dstd, col in srcs:
                    for g in range((H + 7) // 8):
                        hh = min(8, H - g * 8)
                        pv_ = bank(7).bitcast(BF16)
                        for j in range(hh):
                            h = g * 8 + j
                            O("pe", "transpose", [Bpb[i], Bc], [pbuf[7]], out=pv_[:, j * 128:j * 128 + rows], in_=sb_[:rows, h * 128:(h + 1) * 128],
                              identity=ident_b[:rows, :rows])
                        O("dve", "tensor_copy", [pbuf[7]], [Bpt], out=ptt[:, :hh * 128], in_=pv_[:, :hh * 128])
                        DMA("sp", dstd.rearrange("h p n -> p h n")[:, g * 8:g * 8 + hh, col:col + rows],
                            ptt.rearrange("p (h n) -> p h n", n=128)[:, :hh, :rows], reads=[Bpt])

            pk = [f32v(AW) for _ in range(2)]
            pvv = [f32v(AW) for _ in range(2)]
            pq = [f32v(AW) for _ in range(2)]
            pvl = [f32v(1) for _ in range(2)]
            Bpk = [Buf("pk") for _ in range(2)]
            pkb = [bf16v(AW) for _ in range(2)]
            pqb = [bf16v(AW) for _ in range(2)]
            pvb = [bf16v(H * 129) for _ in range(2)]
            Bpb = [Buf("pb") for _ in range(2)]
            ptt = bf16v(1024)
            Bpt = Buf("pt")
            n = 0
            for t in range(NTILE):
                own = t >= T0
                prep(pK_d[t * 128:(t + 1) * 128, :], pV_d[t * 128:(t + 1) * 128, :],
                     pQ_d[(t - T0) * 128:(t - T0 + 1) * 128, :] if own else None, valid[t * 128:(t + 1) * 128, :],
                     128, t, KT_d, VA_d, QT_d, (t - T0) * 128, n)
                n += 1
            for kt in range(NKS - 1):
                prep(ck[kt * 128:(kt + 1) * 128, :], cv[kt * 128:(kt + 1) * 128, :], None, None, 128, kt, KTs_d, VAs_d, None, 0, n)
                n += 1
            prep(pK_d[NTOK:NTOK + DEC, :], pV_d[NTOK:NTOK + DEC, :], pQ_d[TQ:TQ + DEC, :], None, DEC, NKS - 1, KTs_d, VAs_d, QTs_d, 0, n)
            P.barrier()
            top[0] = c_top

            def attn(KTd, VAd, QTd, nq, nkt, rows_last, causal, orow0, hn):
                QB = min(512, nq)
                SW = min(128, QB)
                sub = QB // SW
                i = hn % 2
                kts = (nkt - 1) * 128 + rows_last
                DMA("sp", KTb[i][:, :kts], KTd[:, :kts], writes=[BKT[i]])
                DMA("sp", VAb[i][:, :nkt * 129], VAd[:, :nkt * 129], writes=[BKT[i]])
                DMA("sp", QTb[i][:, :nq], QTd[:, :nq], writes=[BKT[i]])
                VA3 = VAb[i].rearrange("p (t d) -> p t d", d=129)
                npair = 0
                for qb in range(nq // QB):
                    q0 = qb * QB
                    kt_last = (T0 + qb * sub + sub - 1) if causal else nkt - 1
                    d0 = (T0 + qb * sub) if causal else nkt

                    def acc(m, si):
                        if si < 3:
                            return 4 + m, si * 129
                        return 6, m * 129
                    for kt in range(kt_last + 1):
                        rk = rows_last if kt == nkt - 1 else 128
                        di = kt - d0 if kt >= d0 else -1
                        qlo = di * SW if di >= 0 else 0
                        ps = npair % 3
                        npair += 1
                        for m in range(2):
                            sbk = (npair % 2) * 2 + m
                            O("pe", "matmul", [BKT[i]], [pbuf[sbk]], bank(sbk)[:rk, qlo:QB], KTb[i][m * 64:(m + 1) * 64, kt * 128:kt * 128 + rk],
                              QTb[i][m * 64:(m + 1) * 64, q0 + qlo:q0 + QB], start=True, stop=True)
                            O("act", "activation", [pbuf[sbk]], [BpT[ps][m]], out=pT[ps][m][:rk, qlo:QB], in_=bank(sbk)[:rk, qlo:QB], func=AF.Exp)
                            if di >= 0:
                                O("pool", "memset", [], [BpT[ps][m]], pT[ps][m][64:128, qlo:qlo + 64], 0.0)
                        for si in range(max(di, 0), sub):
                            last_for_si = (d0 + si) if causal else nkt - 1
                            for m in range(2):
                                ab, ao = acc(m, si)
                                first_in_bank = (kt == 0) and ao == 0 and (ab != 6 or m == 0)
                                O("pe", "matmul", [BpT[ps][m], BKT[i]], [pbuf[ab]], bank(ab)[:SW, ao:ao + 129], pT[ps][m][:rk, si * SW:(si + 1) * SW],
                                  VA3[:rk, kt, :], start=first_in_bank, stop=(kt == last_for_si), skip_group_check=True)
                    for si in range(sub):
                        a1b, a1o = acc(0, si)
                        a2b, a2o = acc(1, si)
                        o1 = bank(a1b)[:SW, a1o:a1o + 129]
                        o2 = bank(a2b)[:SW, a2o:a2o + 129]
                        j = si % 2
                        s_ = fs[j]
                        O("dve", "reciprocal", [pbuf[a1b]], [Bfs[j]], out=s_[:SW, 0:1], in_=o1[:, 128:129])
                        O("dve", "reciprocal", [pbuf[a2b]], [Bfs[j]], out=s_[:SW, 1:2], in_=o2[:, 128:129])
                        O("dve", "tensor_tensor", [Bfs[j], Bl], [Bfs[j]], out=s_[:SW, 2:3], in0=s_[:SW, 1:2], in1=nlam[:SW, :], op=ALU.mult)
                        O("act", "activation", [pbuf[a1b], Bfs[j]], [Bfa[j]], out=fa[j][:SW, :], in_=o1[:, 0:128], func=AF.Copy, scale=s_[:SW, 0:1])
                        O("dve", "scalar_tensor_tensor", [pbuf[a2b], Bfs[j], Bfa[j]], [Bfa[j]], out=fa[j][:SW, :], in0=o2[:, 0:128],
                          scalar=s_[:SW, 2:3], in1=fa[j][:SW, :], op0=ALU.mult, op1=ALU.add)
                        O("pool", "memset", [], [Bfs[j]], s_[:, 3:4], 0.0)
                        O("act", "activation", [Bfa[j]], [Bfj, Bfs[j]], out=fj[:SW, :], in_=fa[j][:SW, :], func=AF.Square, accum_out=s_[:SW, 3:4])
                        O("dve", "tensor_scalar", [Bfs[j]], [Bfs[j]], out=s_[:SW, 4:5], in0=s_[:SW, 3:4], scalar1=1.0 / 128, scalar2=EPS,
                          op0=ALU.mult, op1=ALU.add)
                        O("act", "sqrt", [Bfs[j]], [Bfs[j]], out=s_[:SW, 5:6], in_=s_[:SW, 4:5])
                        O("dve", "reciprocal", [Bfs[j]], [Bfs[j]], out=s_[:SW, 6:7], in_=s_[:SW, 5:6])
                        O("dve", "scalar_tensor_tensor", [Bfa[j], Bfs[j], Bl], [Bfo[j]], out=fo[j][:SW, :], in0=fa[j][:SW, :], scalar=s_[:SW, 6:7],
                          in1=slw[:SW, :], op0=ALU.mult, op1=ALU.mult)
                        r0 = orow0 + q0 + si * SW
                        DMA("sp", mix_d[r0:r0 + SW, hcol[0]:hcol[0] + 128], fo[j][:SW, :], reads=[Bfo[j]])

            KTb = [bf16v(max(NTOK, NKS * 128)) for _ in range(2)]
            VAb = [bf16v(max(NTILE, NKS) * 129) for _ in range(2)]
            QTb = [bf16v(max(TQ, DEC)) for _ in range(2)]
            BKT = [Buf("KT") for _ in range(2)]
            pT = [[bf16v(512) for _ in range(2)] for _ in range(3)]
            BpT = [[Buf("pT") for _ in range(2)] for _ in range(3)]
            fs = [f32v(8) for _ in range(2)]
            Bfs = [Buf("fs") for _ in range(2)]
            fa = [f32v(128) for _ in range(2)]
            Bfa = [Buf("fa") for _ in range(2)]
            fj = f32v(128)
            Bfj = Buf("fj")
            fo = [bf16v(128) for _ in range(2)]
            Bfo = [Buf("fo") for _ in range(2)]
            hcol = [0]
            hn = 0
            for h in range(H):
                hcol[0] = h * 128
                attn(KT_d[h], VA_d[h], QT_d[h], TQ, NTILE, 128, True, 0, hn)
                hn += 1
                attn(KTs_d[h], VAs_d[h], QTs_d[h], DEC, NKS, DEC, False, TQ, hn)
                hn += 1
            P.barrier()
            top[0] = base_top

        if "E" in phases:
            def orow(ot):
                return ot * 128
            normT_pass(lambda ot: mix_d[ot * 128:ot * 128 + own_rows(ot), :], NT + 1, own_rows, D, None, mT_d, src_bf16=True)
            xr = [f32v(512) for _ in range(3)]
            Bxr = [Buf("xr") for _ in range(3)]

            def evacE(ot, rows, c0, cw, bks, n):
                i = n % 3
                src = xq[(T0 + ot) * 128:(T0 + ot) * 128 + rows, c0:c0 + cw] if ot < NT else xs[:, c0:c0 + cw]
                DMA("sp", xr[i][:rows, :cw], src, writes=[Bxr[i]])
                O("dve", "tensor_tensor", [pbuf[bks[0]], Bxr[i]], [Bxr[i]], out=xr[i][:rows, :cw], in0=bank(bks[0])[:rows, :cw],
                  in1=xr[i][:rows, :cw], op=ALU.add)
                DMA("sp", x1_d[ot * 128:ot * 128 + rows, c0:c0 + cw], xr[i][:rows, :cw], reads=[Bxr[i]])
            proj_pass(mT_d, KC, [w_out], D, 1024, lambda c0, cw: list(range(NT + 1)), own_rows, evacE)
            top[0] = base_top
            normT_pass(lambda ot: x1_d[ot * 128:ot * 128 + own_rows(ot), :], NT + 1, own_rows, D, norm2_w[0:1, :], h2T_d)

        if "F" in phases:
            CWF = 512
            gs = [f32v(CWF) for _ in range(3)]
            go = [bf16v(CWF) for _ in range(3)]
            Bgs = [Buf("gs") for _ in range(3)]
            Bgo = [Buf("go") for _ in range(3)]

            def evacF1(ot, rows, c0, cw, bks, n):
                i = n % 3
                O("act", "activation", [pbuf[bks[0]]], [Bgs[i]], out=gs[i][:rows, :cw], in_=bank(bks[0])[:rows, :cw], func=AF.Silu)
                O("dve", "tensor_tensor", [Bgs[i], pbuf[bks[1]]], [Bgo[i]], out=go[i][:rows, :cw], in0=gs[i][:rows, :cw],
                  in1=bank(bks[1])[:rows, :cw], op=ALU.mult)
                DMA("sp", ff_d[ot * 128:ot * 128 + rows, c0:c0 + cw], go[i][:rows, :cw], reads=[Bgo[i]])
            proj_pass(h2T_d, KC, [w_gate, w_up], DFF, CWF, lambda c0, cw: list(range(NT + 1)), own_rows, evacF1)
            top[0] = base_top
            normT_pass(lambda ot: ff_d[ot * 128:ot * 128 + own_rows(ot), :], NT + 1, own_rows, DFF, None, ffT_d, src_bf16=True)
            xr = [f32v(CWF) for _ in range(3)]
            Bxr = [Buf("xr") for _ in range(3)]

            def evacF2(ot, rows, c0, cw, bks, n):
                i = n % 3
                DMA("sp", xr[i][:rows, :cw], x1_d[ot * 128:ot * 128 + rows, c0:c0 + cw], writes=[Bxr[i]])
                O("dve", "tensor_tensor", [pbuf[bks[0]], Bxr[i]], [Bxr[i]], out=xr[i][:rows, :cw], in0=bank(bks[0])[:rows, :cw],
                  in1=xr[i][:rows, :cw], op=ALU.add)
                DMA("sp", x2_d[ot * 128:ot * 128 + rows, c0:c0 + cw], xr[i][:rows, :cw], reads=[Bxr[i]])
            proj_pass(ffT_d, DFF // 128, [w_down], D, CWF, lambda c0, cw: list(range(NT + 1)), own_rows, evacF2, wbufs=1)
            top[0] = base_top
            fw = f32v(D)
            Bfw = Buf("fw")
            DMA("sp", fw, final_norm_w[0:1, :].partition_broadcast(128), writes=[Bfw])
            xt = [f32v(D) for _ in range(2)]
            Bxt = [Buf("xt") for _ in range(2)]
            jk = bf16v(D)
            Bjk = Buf("jk")
            st = [f32v(4) for _ in range(2)]
            Bst = [Buf("st") for _ in range(2)]
            for ot in range(NT + 1):
                i = ot % 2
                rows = own_rows(ot)
                DMA("sp", xt[i][:rows, :], x2_d[ot * 128:ot * 128 + rows, :], writes=[Bxt[i]])
                O("pool", "memset", [], [Bst[i]], st[i], 0.0)
                O("act", "activation", [Bxt[i]], [Bjk, Bst[i]], out=jk[:rows, :], in_=xt[i][:rows, :], func=AF.Square, accum_out=st[i][:rows, 0:1])
                O("dve", "tensor_scalar", [Bst[i]], [Bst[i]], out=st[i][:rows, 1:2], in0=st[i][:rows, 0:1], scalar1=1.0 / D, scalar2=EPS,
                  op0=ALU.mult, op1=ALU.add)
                O("act", "sqrt", [Bst[i]], [Bst[i]], out=st[i][:rows, 3:4], in_=st[i][:rows, 1:2])
                O("dve", "reciprocal", [Bst[i]], [Bst[i]], out=st[i][:rows, 2:3], in_=st[i][:rows, 3:4])
                O("dve", "scalar_tensor_tensor", [Bxt[i], Bst[i], Bfw], [Bxt[i]], out=xt[i][:rows, :], in0=xt[i][:rows, :],
                  scalar=st[i][:rows, 2:3], in1=fw[:rows, :], op0=ALU.mult, op1=ALU.mult)
                dst = y_q[ot * 128:ot * 128 + rows, :] if ot < NT else y_s[:, :]
                DMA("sp", dst, xt[i][:rows, :], reads=[Bxt[i]])

        P.finish()
        P.emit()
    return nc


def host_inputs(cfg, inp):
    c = cfg
    f = lambda a: np.ascontiguousarray(np.asarray(a, dtype=np.float32))
    xp = f(inp["x_prompt"])
    B = xp.shape[0]
    ncores = B * NSLOT
    lam_in = np.concatenate([f(inp["lambda_q1"]), f(inp["lambda_k1"]), f(inp["lambda_q2"]), f(inp["lambda_k2"])], 0).reshape(1, 256)
    u = np.arange(128)
    consts = np.concatenate([np.eye(128), (u[:, None] <= u[None, :]), (u[:, None] > u[None, :]), np.ones((128, 128))], 1).astype(np.float32)
    shared = {
        "norm1_w": f(inp["norm1_w"]), "w_in": f(inp["w_in"])[0], "lam_in": lam_in,
        "subln_w": f(inp["subln_w"]), "conv_w": f(inp["conv_w"])[0], "conv_b": f(inp["conv_b"]),
        "dt_bias": f(inp["dt_bias"]), "A_log": f(inp["A_log"]), "D_skip": f(inp["D_skip"]),
        "ssd_norm_w": f(inp["ssd_norm_w"]), "w_out": f(inp["w_out"])[0], "norm2_w": f(inp["norm2_w"]),
        "w_gate": f(inp["w_gate"])[0], "w_up": f(inp["w_up"])[0], "w_down": f(inp["w_down"])[0],
        "final_norm_w": f(inp["final_norm_w"]).reshape(1, -1),
        "consts": consts,
    }
    maps = []
    for core in range(ncores):
        b, j = core // NSLOT, core % NSLOT
        xq = np.zeros((NSLOT * c.TQ, c.D), np.float32)
        valid = np.zeros((NSLOT * c.TQ, 1), np.float32)
        n = (j + 1) * c.TQ
        xq[NSLOT * c.TQ - n:] = xp[b, :n]
        valid[NSLOT * c.TQ - n:] = 1.0
        m = dict(shared)
        m.update({
            "xq": xq, "valid": valid, "xs": f(inp["x_sample"])[core],
            "ck": f(inp["cache_k"])[0, core].reshape(c.PAST, c.AW),
            "cv": f(inp["cache_v"])[0, core].reshape(c.PAST, c.AW),
            "sconv": f(inp["state_conv"])[0, core],
            "sssm": f(inp["state_ssm"])[0, core].reshape(c.SH * 64, 128),
        })
        maps.append(m)
    return maps


def assemble(cfg, res, B):
    c = cfg
    ncores = B * NSLOT
    g = lambda name: [np.asarray(res[i][name], dtype=np.float32) for i in range(ncores)]
    yq, ys, kq, vq, cp, sp_, ks, vs, cs, ss = (g(n) for n in
        ["y_q", "y_s", "k_q", "v_q", "conv_p", "ssm_p", "k_s", "v_s", "conv_s", "ssm_s"])
    cat = lambda lst, b: np.concatenate(lst[b * NSLOT:(b + 1) * NSLOT], 0)
    y_prompt = np.stack([cat(yq, b) for b in range(B)])
    k_prompt = np.stack([cat(kq, b) for b in range(B)]).reshape(1, B, c.SEQ, c.H, 128)
    v_prompt = np.stack([cat(vq, b) for b in range(B)]).reshape(1, B, c.SEQ, c.H, 128)
    conv_prompt = np.stack([cp[b * NSLOT + NSLOT - 1] for b in range(B)])[None]
    ssm_prompt = np.stack([sp_[b * NSLOT + NSLOT - 1] for b in range(B)]).reshape(1, B, c.SH, 64, 128)
    y_sample = np.stack(ys)
    k_sample = np.stack(ks).reshape(1, ncores, c.DEC, c.H, 128)
    v_sample = np.stack(vs).reshape(1, ncores, c.DEC, c.H, 128)
    conv_sample = np.stack(cs)[None]
    ssm_sample = np.stack(ss).reshape(1, ncores, c.SH, 64, 128)
    return (y_prompt, y_sample, k_prompt, v_prompt, conv_prompt, ssm_prompt,
            k_sample, v_sample, conv_sample, ssm_sample)


def run(cfg, inp, phases="ABXDCEF", dbg=()):
    B = np.asarray(inp["x_prompt"]).shape[0]
    maps = host_inputs(cfg, inp)
    nc = build(cfg, phases, dbg)
    res = run_bass_kernel_spmd(nc, maps, core_ids=list(range(len(maps))))
    if dbg:
        return assemble(cfg, res.results, B), res.results
    return assemble(cfg, res.results, B)


def kernel(**inputs):
    return run(Cfg(), inputs)
```

```python
import contextlib
import numpy as np
import concourse.bass as bass
import concourse.mybir as mybir
from concourse.bass_utils import run_bass_kernel_spmd

F32 = mybir.dt.float32
BF16 = mybir.dt.bfloat16
AF = mybir.ActivationFunctionType
ALU = mybir.AluOpType
EPS = 1e-6
NSLOT = 4
KDMA = 8


class Cfg:
    def __init__(self, D=4096, SEQ=8192, DFF=11008, PAST=2048, DEC=16, G=8):
        self.D, self.SEQ, self.DFF, self.PAST, self.DEC, self.G = D, SEQ, DFF, PAST, DEC, G
        self.AW = D // 2
        self.H = self.AW // 128
        self.SI = D - self.AW
        self.SH = self.SI // 64
        self.CD = self.SI + 2 * G * 128
        self.IN = 3 * self.AW + self.SI + self.CD + self.SH
        self.TQ = SEQ // NSLOT
        self.NT = self.TQ // 128
        self.KC = D // 128
        self.oQ, self.oK, self.oV, self.oZ = 0, self.AW, 2 * self.AW, 3 * self.AW
        self.oX = 3 * self.AW + self.SI
        self.oDT = self.oX + self.CD


class Buf:
    def __init__(self, name):
        self.name, self.w, self.r = name, None, []


class Prog:
    ENG = ["pe", "act", "dve", "pool", "sp"]

    def __init__(self, nc, stack):
        self.nc = nc
        self.ops = {e: [] for e in self.ENG}
        self.esem = {e: stack.enter_context(nc.semaphore("s_" + e)) for e in ["pe", "act", "dve", "pool"]}
        self.cnt = {e: 0 for e in self.esem}
        self.dsem = {q: [stack.enter_context(nc.semaphore("d_%s%d" % (q, i))) for i in range(KDMA)]
                     for q in ["sp", "pool", "act"]}
        self.dn = {"sp": 0, "pool": 0, "act": 0}
        self.seen = {e: {} for e in self.ENG}
        self.pend = {e: [] for e in self.ENG}

    def op(self, eng, fn, reads=(), writes=(), dma=False):
        waits = list(self.pend[eng])
        self.pend[eng] = []
        for b in reads:
            if b.w:
                waits.append(b.w)
        for b in writes:
            if b.w:
                waits.append(b.w)
            waits.extend(b.r)
        if dma:
            m = self.dn[eng]
            sem = self.dsem[eng][m % KDMA]
            prev = 16 * (m // KDMA)
            if prev:
                waits.append((sem, prev))
            ev = (sem, prev + 16)
            inc = 16
            self.dn[eng] += 1
        else:
            self.cnt[eng] += 1
            ev = (self.esem[eng], self.cnt[eng])
            inc = 1
        need = {}
        for s, v in waits:
            if eng == "pe" and s is self.esem["pe"]:
                continue
            if self.seen[eng].get(id(s), 0) >= v:
                continue
            if need.get(id(s), (s, 0))[1] < v:
                need[id(s)] = (s, v)
        for k, (s, v) in need.items():
            self.seen[eng][k] = v
        self.ops[eng].append((list(need.values()), fn, ev[0], inc))
        for b in reads:
            b.r.append(ev)
        for b in writes:
            b.w, b.r = ev, []
        return ev

    def all_events(self):
        evs = [(self.esem[e], self.cnt[e]) for e in self.esem if self.cnt[e]]
        for q in self.dsem:
            m = self.dn[q]
            for i in range(KDMA):
                n = (m - i + KDMA - 1) // KDMA if m > i else 0
                if n:
                    evs.append((self.dsem[q][i], 16 * n))
        return evs

    def barrier(self):
        evs = self.all_events()
        for e in self.ENG:
            self.pend[e] = list(evs)

    def finish(self):
        self.barrier()
        for e in self.ENG:
            need = {}
            for s, v in self.pend[e]:
                if self.seen[e].get(id(s), 0) < v:
                    need[id(s)] = (s, v)
            self.ops[e].append((list(need.values()), None, None, 0))

    def emit(self):
        nc = self.nc
        names = {"pe": "tensor", "act": "scalar", "dve": "vector", "pool": "gpsimd", "sp": "sync"}
        with nc.Block() as block:
            for e in self.ENG:
                def body(engobj, e=e):
                    for need, fn, sem, inc in self.ops[e]:
                        for s, v in need:
                            engobj.wait_ge(s, v)
                        if fn is not None:
                            fn(engobj).then_inc(sem, inc)
                getattr(block, names[e])(body)


def build(cfg, phases="ABXDCEF", dbg=()):
    c = cfg
    nc = bass.Bass("TRN2", target_bir_lowering=False)
    D, KC, IN, H, AW, SI, SH, G, CD, DFF = c.D, c.KC, c.IN, c.H, c.AW, c.SI, c.SH, c.G, c.CD, c.DFF
    TQ, NT, DEC, PAST = c.TQ, c.NT, c.DEC, c.PAST
    NTOK = NSLOT * TQ
    NTILE = NSLOT * NT
    T0 = NTILE - NT
    GB = G * 128
    NKS = PAST // 128 + 1
    LAM0 = 0.2

    def din(name, shape, dt=F32):
        return nc.dram_tensor(name, list(shape), dt, kind="ExternalInput").ap()

    def dout(name, shape, dt=F32):
        return nc.dram_tensor(name, list(shape), dt, kind="ExternalOutput").ap()

    def dscr(name, shape, dt):
        kind = "ExternalOutput" if name in dbg else "Internal"
        return nc.dram_tensor(name, list(shape), dt, kind=kind).ap()

    xq = din("xq", [NTOK, D])
    valid = din("valid", [NTOK, 1])
    xs = din("xs", [DEC, D])
    ck = din("ck", [PAST, AW])
    cv = din("cv", [PAST, AW])
    sconv = din("sconv", [3, CD])
    sssm = din("sssm", [SH * 64, 128])
    norm1_w = din("norm1_w", [1, D])
    w_in = din("w_in", [D, IN])
    lam_in = din("lam_in", [1, 256])
    subln_w = din("subln_w", [1, 128])
    conv_w = din("conv_w", [4, CD])
    conv_b = din("conv_b", [1, CD])
    dt_bias = din("dt_bias", [1, SH])
    A_log = din("A_log", [1, SH])
    D_skip = din("D_skip", [1, SH])
    ssd_norm_w = din("ssd_norm_w", [1, SI])
    w_out = din("w_out", [D, D])
    norm2_w = din("norm2_w", [1, D])
    w_gate = din("w_gate", [D, DFF])
    w_up = din("w_up", [D, DFF])
    w_down = din("w_down", [DFF, D])
    final_norm_w = din("final_norm_w", [1, D])
    consts_in = din("consts", [128, 4 * 128])

    y_q = dout("y_q", [TQ, D])
    y_s = dout("y_s", [DEC, D])
    k_q = dout("k_q", [TQ, AW])
    v_q = dout("v_q", [TQ, AW])
    conv_p = dout("conv_p", [3, CD])
    ssm_p = dout("ssm_p", [SH * 64, 128])
    k_s = dout("k_s", [DEC, AW])
    v_s = dout("v_s", [DEC, AW])
    conv_s = dout("conv_s", [3, CD])
    ssm_s = dout("ssm_s", [SH * 64, 128])

    hT_d = dscr("hT_d", [NTILE + 1, 128, KC * 128], BF16)
    pQ_d = dscr("pQ_d", [TQ + DEC, AW], F32)
    pK_d = dscr("pK_d", [NTOK + DEC, AW], F32)
    pV_d = dscr("pV_d", [NTOK + DEC, AW], F32)
    pZ_d = dscr("pZ_d", [TQ + DEC, SI], F32)
    pX_d = dscr("pX_d", [NTOK + DEC + 6, CD], F32)
    pDT_d = dscr("pDT_d", [NTOK + DEC, SH], F32)
    act_d = dscr("act_d", [NTOK + DEC, CD], F32)
    KT_d = dscr("KT_d", [H, 128, NTOK], BF16)
    QT_d = dscr("QT_d", [H, 128, TQ], BF16)
    VA_d = dscr("VA_d", [H, 128, NTILE * 129], BF16)
    KTs_d = dscr("KTs_d", [H, 128, NKS * 128], BF16)
    QTs_d = dscr("QTs_d", [H, 128, DEC], BF16)
    VAs_d = dscr("VAs_d", [H, 128, NKS * 129], BF16)
    mix_d = dscr("mix_d", [TQ + DEC, D], BF16)
    mT_d = dscr("mT_d", [NT + 1, 128, KC * 128], BF16)
    x1_d = dscr("x1_d", [TQ + DEC, D], F32)
    h2T_d = dscr("h2T_d", [NT + 1, 128, KC * 128], BF16)
    ff_d = dscr("ff_d", [TQ + DEC, DFF], BF16)
    ffT_d = dscr("ffT_d", [NT + 1, 128, DFF], BF16)
    x2_d = dscr("x2_d", [TQ + DEC, D], F32)

    with contextlib.ExitStack() as stack:
        P = Prog(nc, stack)
        ARENA = 44 * 1024
        arena = stack.enter_context(nc.sbuf_tensor("arena", [128, ARENA], F32))
        psum = stack.enter_context(nc.psum_tensor("psum", [128, 8 * 512], F32))
        top = [0]

        def f32v(words):
            o = top[0]
            top[0] += words
            assert top[0] <= ARENA, "SBUF arena overflow %d" % top[0]
            return arena[:, o:o + words]

        def bf16v(elems):
            return f32v((elems + 1) // 2).bitcast(BF16)[:, :elems]

        def bank(i):
            return psum[:, i * 512:(i + 1) * 512]

        pbuf = [Buf("ps%d" % i) for i in range(8)]

        def O(eng, method, reads, writes, *a, **kw):
            return P.op(eng, lambda e: getattr(e, method)(*a, **kw), reads, writes)

        def DMA(q, out, in_, reads=(), writes=()):
            return P.op(q, lambda e: e.dma_start(out=out, in_=in_), reads, writes, dma=True)

        def COPY(eng, out, in_, reads, writes):
            if eng == "act":
                return O("act", "copy", reads, writes, out=out, in_=in_)
            return O(eng, "tensor_copy", reads, writes, out=out, in_=in_)

        cst = f32v(512)
        Bc = Buf("consts")
        DMA("sp", cst, consts_in[:, :], writes=[Bc])
        ident_f, tri_le, tri_gt, ones_f = (cst[:, i * 128:(i + 1) * 128] for i in range(4))
        ident_b = bf16v(128)
        O("dve", "tensor_copy", [Bc], [Bc], out=ident_b, in_=ident_f)
        base_top = top[0]

        def own_rows(ot):
            return 128 if ot < NT else DEC

        def normT_pass(src_of, ntiles, rows_of, ncols, wrow, dstT, src_bf16=False):
            mark = top[0]
            nch = ncols // 128
            grp = 8 if nch % 8 == 0 else (4 if nch % 4 == 0 else (2 if nch % 2 == 0 else 1))
            Bw = Buf("nw")
            if wrow is not None:
                wb = f32v(ncols)
                DMA("sp", wb, wrow.partition_broadcast(128), writes=[Bw])
            xt = [None if src_bf16 else f32v(ncols) for _ in range(2)]
            Bxt = [Buf("xt") for _ in range(2)]
            junk = None if src_bf16 else bf16v(ncols)
            Bj = Buf("junk")
            xn = [bf16v(ncols) for _ in range(2)]
            Bxn = [Buf("xn") for _ in range(2)]
            st = [f32v(4) for _ in range(2)]
            Bst = [Buf("st") for _ in range(2)]
            hs = [bf16v(ncols) for _ in range(2)]
            Bhs = [Buf("hs") for _ in range(2)]
            for t in range(ntiles):
                i = t % 2
                rows = rows_of(t)
                if src_bf16:
                    DMA("sp", xn[i][:rows, :], src_of(t), writes=[Bxn[i]])
                else:
                    DMA("sp", xt[i][:rows, :], src_of(t), writes=[Bxt[i]])
                    if wrow is not None:
                        O("pool", "memset", [], [Bst[i]], st[i], 0.0)
                        O("act", "activation", [Bxt[i]], [Bj, Bst[i]], out=junk[:rows, :], in_=xt[i][:rows, :],
                          func=AF.Square, accum_out=st[i][:rows, 0:1])
                        O("dve", "tensor_scalar", [Bst[i]], [Bst[i]], out=st[i][:rows, 1:2], in0=st[i][:rows, 0:1],
                          scalar1=1.0 / ncols, scalar2=EPS, op0=ALU.mult, op1=ALU.add)
                        O("act", "sqrt", [Bst[i]], [Bst[i]], out=st[i][:rows, 3:4], in_=st[i][:rows, 1:2])
                        O("dve", "reciprocal", [Bst[i]], [Bst[i]], out=st[i][:rows, 2:3], in_=st[i][:rows, 3:4])
                        O("dve", "scalar_tensor_tensor", [Bxt[i], Bst[i], Bw], [Bxn[i]], out=xn[i][:rows, :],
                          in0=xt[i][:rows, :], scalar=st[i][:rows, 2:3], in1=wb[:rows, :], op0=ALU.mult, op1=ALU.mult)
                    else:
                        O("dve", "tensor_copy", [Bxt[i]], [Bxn[i]], out=xn[i][:rows, :], in_=xt[i][:rows, :])
                for g in range(nch // grp):
                    bk = g % 2
                    pv = bank(bk).bitcast(BF16)
                    for kk in range(grp):
                        k = g * grp + kk
                        O("pe", "transpose", [Bxn[i], Bc], [pbuf[bk]], out=pv[:, kk * 128:kk * 128 + rows],
                          in_=xn[i][:rows, k * 128:(k + 1) * 128], identity=ident_b[:rows, :rows])
                    COPY("act" if g % 2 == 0 else "dve", hs[i][:, g * grp * 128:(g + 1) * grp * 128],
                         pv[:, :grp * 128], [pbuf[bk]], [Bhs[i]])
                DMA("sp", dstT[t, :, :], hs[i], reads=[Bhs[i]])
            P.barrier()
            top[0] = mark

        def proj_pass(srcT, KCH, weights, NCOL, CW, tiles_of_block, rows_of, evac, wbufs=2, hbufs=3):
            mark = top[0]
            nw = len(weights)
            SUB = min(512, CW)
            wv = [[bf16v(KCH * CW) for _ in range(wbufs)] for _ in range(nw)]
            Bwv = [[Buf("wv") for _ in range(wbufs)] for _ in range(nw)]
            hb = [bf16v(KCH * 128) for _ in range(hbufs)]
            Bhb = [Buf("hb") for _ in range(hbufs)]
            nblk = (NCOL + CW - 1) // CW
            n_h = 0
            n_b = 0
            n_e = 0
            for cb in range(nblk):
                c0 = cb * CW
                cw = min(CW, NCOL - c0)
                wi = cb % wbufs
                wviews = []
                for j, w in enumerate(weights):
                    wview = wv[j][wi].rearrange("p (k c) -> p k c", k=KCH)
                    DMA("pool", wview[:, :, :cw], w.rearrange("(k p) c -> p k c", p=128)[:, :, c0:c0 + cw],
                        writes=[Bwv[j][wi]])
                    wviews.append(wview)
                for t in tiles_of_block(c0, cw):
                    rows = rows_of(t)
                    hi = n_h % hbufs
                    hview = hb[hi].rearrange("p (k n) -> p k n", k=KCH)
                    DMA("sp" if n_h % 2 == 0 else "act", hb[hi], srcT[t, :, :], writes=[Bhb[hi]])
                    n_h += 1
                    for s0 in range(0, cw, SUB):
                        sw = min(SUB, cw - s0)
                        bks = []
                        for j in range(nw):
                            bk = 2 + n_b % 6
                            n_b += 1
                            bks.append(bk)
                            for k in range(KCH):
                                O("pe", "matmul", [Bhb[hi], Bwv[j][wi]], [pbuf[bk]], bank(bk)[:rows, :sw],
                                  hview[:, k, :rows], wviews[j][:, k, s0:s0 + sw], start=(k == 0), stop=(k == KCH - 1))
                        n_e += 1
                        evac(t, rows, c0 + s0, sw, bks, n_e)
            P.barrier()
            top[0] = mark

        if "A" in phases:
            normT_pass(lambda t: xq[t * 128:(t + 1) * 128, :] if t < NTILE else xs[:, :], NTILE + 1,
                       lambda t: 128 if t < NTILE else DEC, D, norm1_w[0:1, :], hT_d)

        if "B" in phases:
            segs = [("Q", 0, AW, pQ_d, True), ("K", c.oK, AW, pK_d, False), ("V", c.oV, AW, pV_d, False),
                    ("Z", c.oZ, SI, pZ_d, True), ("X", c.oX, CD, pX_d, False), ("DT", c.oDT, SH, pDT_d, False)]
            for name, s0, sw, dst, own_only in segs:
                ob = [f32v(512) for _ in range(3)]
                Bob = [Buf("ob") for _ in range(3)]

                def evacB(t, rows, c0, cw, bks, n, dst=dst, own_only=own_only, name=name, ob=ob, Bob=Bob):
                    i = n % 3
                    COPY("act" if n % 2 == 0 else "dve", ob[i][:rows, :cw], bank(bks[0])[:rows, :cw], [pbuf[bks[0]]], [Bob[i]])
                    if own_only:
                        r0 = (t - T0) * 128
                    elif name == "X":
                        r0 = 3 + t * 128 if t < NTILE else NTOK + 6
                    else:
                        r0 = t * 128
                    DMA("pool", dst[r0:r0 + rows, c0:c0 + cw], ob[i][:rows, :cw], reads=[Bob[i]])

                tiles = list(range(T0, NTILE + 1)) if own_only else list(range(NTILE + 1))
                proj_pass(hT_d, KC, [w_in[:, s0:s0 + sw]], sw, 1024, lambda c0, cw, tiles=tiles: tiles,
                          lambda t: 128 if t < NTILE else DEC, evacB)
                top[0] = base_top
            DMA("sp", k_q[:, :], pK_d[T0 * 128:NTOK, :])
            DMA("sp", v_q[:, :], pV_d[T0 * 128:NTOK, :])
            DMA("sp", conv_p[:, :], pX_d[3 + NTOK - 3:3 + NTOK, :])
            DMA("sp", k_s[:, :], pK_d[NTOK:NTOK + DEC, :])
            DMA("sp", v_s[:, :], pV_d[NTOK:NTOK + DEC, :])
            DMA("sp", conv_s[:, :], pX_d[NTOK + 6 + DEC - 3:NTOK + 6 + DEC, :])
            zt = f32v(CD)
            Bz = Buf("z")
            O("pool", "memset", [], [Bz], zt[:3, :], 0.0)
            DMA("sp", pX_d[0:3, :], zt[:3, :], reads=[Bz])
            DMA("sp", pX_d[NTOK + 3:NTOK + 6, :], sconv[:, :])
            P.barrier()
            top[0] = base_top

        if "X" in phases:
            CC = min(1024, CD)
            for ch0 in range(0, CD, CC):
                cwt = [f32v(CC) for _ in range(5)]
                Bcw = Buf("cw")
                for i in range(4):
                    DMA("sp", cwt[i], conv_w[i:i + 1, ch0:ch0 + CC].partition_broadcast(128), writes=[Bcw])
                DMA("sp", cwt[4], conv_b[0:1, ch0:ch0 + CC].partition_broadcast(128), writes=[Bcw])
                win = [[f32v(CC) for _ in range(4)] for _ in range(2)]
                Bwin = [[Buf("win") for _ in range(4)] for _ in range(2)]
                for t in range(NTILE + 1):
                    rows = 128 if t < NTILE else DEC
                    r0 = 3 + t * 128 if t < NTILE else NTOK + 6
                    g0 = t * 128 if t < NTILE else NTOK
                    s = t % 2
                    for i in range(4):
                        DMA("sp" if i % 2 == 0 else "act", win[s][i][:rows, :], pX_d[r0 - 3 + i:r0 - 3 + i + rows, ch0:ch0 + CC], writes=[Bwin[s][i]])
                    for i in range(4):
                        O("pool", "tensor_tensor", [Bwin[s][i], Bcw], [Bwin[s][i]], out=win[s][i][:rows, :],
                          in0=win[s][i][:rows, :], in1=cwt[i][:rows, :], op=ALU.mult)
                    O("dve", "tensor_tensor", [Bwin[s][0], Bwin[s][1]], [Bwin[s][0]], out=win[s][0][:rows, :],
                      in0=win[s][0][:rows, :], in1=win[s][1][:rows, :], op=ALU.add)
                    O("dve", "tensor_tensor", [Bwin[s][2], Bwin[s][3]], [Bwin[s][2]], out=win[s][2][:rows, :],
                      in0=win[s][2][:rows, :], in1=win[s][3][:rows, :], op=ALU.add)
                    O("dve", "tensor_tensor", [Bwin[s][0], Bwin[s][2]], [Bwin[s][0]], out=win[s][0][:rows, :],
                      in0=win[s][0][:rows, :], in1=win[s][2][:rows, :], op=ALU.add)
                    O("dve", "tensor_tensor", [Bwin[s][0], Bcw], [Bwin[s][0]], out=win[s][0][:rows, :],
                      in0=win[s][0][:rows, :], in1=cwt[4][:rows, :], op=ALU.add)
                    O("act", "activation", [Bwin[s][0]], [Bwin[s][1]], out=win[s][1][:rows, :], in_=win[s][0][:rows, :],
                      func=AF.Silu)
                    DMA("sp", act_d[g0:g0 + rows, ch0:ch0 + CC], win[s][1][:rows, :], reads=[Bwin[s][1]])
                P.barrier()
                top[0] = base_top

        if "D" in phases:
            dtb, albc, dskb = f32v(SH), f32v(SH), f32v(SH)
            nwb = f32v(SI)
            Bk = Buf("ssdconst")
            DMA("sp", dtb, dt_bias[0:1, :].partition_broadcast(128), writes=[Bk])
            DMA("sp", albc, A_log[0:1, :].partition_broadcast(128), writes=[Bk])
            DMA("sp", dskb, D_skip[0:1, :].partition_broadcast(128), writes=[Bk])
            DMA("sp", nwb, ssd_norm_w[0:1, :].partition_broadcast(128), writes=[Bk])
            Abc = f32v(SH)
            O("act", "activation", [Bk], [Bk], out=Abc, in_=albc, func=AF.Exp)
            O("dve", "tensor_scalar", [Bk], [Bk], out=Abc, in0=Abc, scalar1=-1.0, scalar2=None, op0=ALU.mult)
            S = f32v(SI)
            BS = Buf("S")
            O("pool", "memset", [], [BS], S, 0.0)
            Sb = bf16v(SI)
            BSb = Buf("Sb")
            NB = 2
            xa = [f32v(SI) for _ in range(NB)]
            Ba = [f32v(GB) for _ in range(NB)]
            Ca = [f32v(GB) for _ in range(NB)]
            za = [f32v(SI) for _ in range(NB)]
            dtr = [f32v(SH) for _ in range(NB)]
            vl = [f32v(1) for _ in range(NB)]
            Bld = [Buf("ld") for _ in range(NB)]
            sm = [f32v(8 * SH) for _ in range(NB)]
            Bsm = [Buf("sm") for _ in range(NB)]
            xdt = [bf16v(SI) for _ in range(NB)]
            xdte = [bf16v(SI) for _ in range(NB)]
            Bb = [bf16v(GB) for _ in range(NB)]
            Cb = [bf16v(GB) for _ in range(NB)]
            Bx = [Buf("xd") for _ in range(NB)]
            Y = [f32v(SI) for _ in range(NB)]
            BY = [Buf("Y") for _ in range(NB)]
            yo = [bf16v(SI) for _ in range(NB)]
            Byo = [Buf("yo") for _ in range(NB)]
            BT, CT = bf16v(128), bf16v(128)
            BBT = Buf("BT")
            cbm = f32v(128)
            Bcbm = Buf("cbm")
            Lm = f32v(4 * 128)
            BL = Buf("L")
            Em = f32v(4 * 128)
            BE = Buf("E")
            Mm = bf16v(4 * 128)
            BM = Buf("M")
            tmpg = f32v(256)
            Btg = Buf("tg")
            gst = f32v(4 * G)
            Bgst = Buf("gst")
            tr = f32v(128)
            Btr = Buf("tr")

            def ssd_tile(t, n):
                i = n % NB
                own = t >= T0
                samp = t == NTILE
                rows = DEC if samp else 128
                g0 = NTOK if samp else t * 128
                o0 = (t - T0) * 128
                DMA("sp", xa[i][:rows, :], act_d[g0:g0 + rows, 0:SI], writes=[Bld[i]])
                DMA("sp", Ba[i][:rows, :], act_d[g0:g0 + rows, SI:SI + GB], writes=[Bld[i]])
                DMA("sp", dtr[i][:rows, :], pDT_d[g0:g0 + rows, :], writes=[Bld[i]])
                if samp:
                    O("pool", "memset", [], [Bld[i]], vl[i], 1.0)
                else:
                    DMA("sp", vl[i][:rows, :], valid[g0:g0 + rows, :], writes=[Bld[i]])
                if own:
                    DMA("sp", Ca[i][:rows, :], act_d[g0:g0 + rows, SI + GB:SI + 2 * GB], writes=[Bld[i]])
                    DMA("sp", za[i][:rows, :], pZ_d[o0:o0 + rows, :], writes=[Bld[i]])
                m = sm[i]
                dt_, dA_, w2_, eacs_, dte_, cdec_, tmp_ = (m[:, j * SH:(j + 1) * SH] for j in range(7))
                O("dve", "tensor_tensor", [Bld[i], Bk], [Bsm[i]], out=tmp_[:rows, :], in0=dtr[i][:rows, :], in1=dtb[:rows, :], op=ALU.add)
                O("act", "activation", [Bsm[i]], [Bsm[i]], out=tmp_[:rows, :], in_=tmp_[:rows, :], func=AF.Exp)
                O("act", "activation", [Bsm[i]], [Bsm[i]], out=tmp_[:rows, :], in_=tmp_[:rows, :], func=AF.Ln, bias=1.0)
                O("dve", "tensor_scalar", [Bsm[i], Bld[i]], [Bsm[i]], out=dt_[:rows, :], in0=tmp_[:rows, :],
                  scalar1=vl[i][:rows, 0:1], scalar2=None, op0=ALU.mult)
                O("dve", "tensor_tensor", [Bsm[i], Bk], [Bsm[i]], out=dA_[:rows, :], in0=dt_[:rows, :], in1=Abc[:rows, :], op=ALU.mult)
                pb = bank(0)
                O("pe", "matmul", [Bsm[i], Bc], [pbuf[0]], pb[:rows, 0:SH], tri_le[:rows, :rows], dA_[:rows, :], start=True, stop=True)
                O("pe", "matmul", [Bsm[i], Bc], [pbuf[0]], pb[:rows, SH:2 * SH], tri_gt[:rows, :rows], dA_[:rows, :], start=True, stop=True)
                O("pe", "matmul", [Bsm[i], Bc], [pbuf[0]], pb[:, 2 * SH:3 * SH], ones_f[:rows, :], dA_[:rows, :], start=True, stop=True)
                O("act", "activation", [pbuf[0]], [Bsm[i]], out=eacs_[:rows, :], in_=pb[:rows, 0:SH], func=AF.Exp)
                O("act", "activation", [pbuf[0]], [Bsm[i]], out=dte_[:rows, :], in_=pb[:rows, SH:2 * SH], func=AF.Exp)
                O("act", "activation", [pbuf[0]], [Bsm[i]], out=cdec_, in_=pb[:, 2 * SH:3 * SH], func=AF.Exp)
                O("dve", "tensor_tensor", [Bsm[i]], [Bsm[i]], out=w2_[:rows, :], in0=dt_[:rows, :], in1=dte_[:rows, :], op=ALU.mult)
                xa3 = xa[i].rearrange("p (h d) -> p h d", d=64)
                O("dve", "tensor_tensor", [Bld[i], Bsm[i]], [Bx[i]], out=xdt[i].rearrange("p (h d) -> p h d", d=64)[:rows],
                  in0=xa3[:rows], in1=dt_[:rows, :].unsqueeze(2).to_broadcast([rows, SH, 64]), op=ALU.mult)
                O("pool", "tensor_tensor", [Bld[i], Bsm[i]], [Bx[i]], out=xdte[i].rearrange("p (h d) -> p h d", d=64)[:rows],
                  in0=xa3[:rows], in1=w2_[:rows, :].unsqueeze(2).to_broadcast([rows, SH, 64]), op=ALU.mult)
                O("act", "copy", [Bld[i]], [Bx[i]], out=Bb[i][:rows, :], in_=Ba[i][:rows, :])
                if own:
                    O("act", "copy", [Bld[i]], [Bx[i]], out=Cb[i][:rows, :], in_=Ca[i][:rows, :])
                    O("pool", "tensor_copy", [BS], [BSb], out=Sb, in_=S)
                    for g in range(G):
                        pv = bank(1).bitcast(BF16)
                        O("pe", "transpose", [Bx[i], Bc], [pbuf[1]], out=pv[:, 0:rows], in_=Bb[i][:rows, g * 128:(g + 1) * 128], identity=ident_b[:rows, :rows])
                        O("pe", "transpose", [Bx[i], Bc], [pbuf[1]], out=pv[:, 128:128 + rows], in_=Cb[i][:rows, g * 128:(g + 1) * 128], identity=ident_b[:rows, :rows])
                        O("dve", "tensor_copy", [pbuf[1]], [BBT], out=BT[:, :rows], in_=pv[:, 0:rows])
                        O("dve", "tensor_copy", [pbuf[1]], [BBT], out=CT[:, :rows], in_=pv[:, 128:128 + rows])
                        O("pe", "matmul", [BBT], [pbuf[2]], bank(2)[:rows, :rows], BT[:, :rows], CT[:, :rows], start=True, stop=True)
                        O("dve", "tensor_tensor", [pbuf[2], Bc], [Bcbm], out=cbm[:rows, :rows], in0=bank(2)[:rows, :rows], in1=tri_le[:rows, :rows], op=ALU.mult)
                        for r in range(4):
                            h = 4 * g + r
                            O("pool" if r % 2 else "dve", "tensor_scalar", [Bc, Bsm[i]], [BL], out=Lm[:rows, r * 128:r * 128 + rows],
                              in0=tri_gt[:rows, :rows], scalar1=dA_[:rows, h:h + 1], scalar2=None, op0=ALU.mult)
                        for r in range(4):
                            O("pe", "matmul", [BL, Bc], [pbuf[3]], bank(3)[:rows, r * 128:r * 128 + rows], Lm[:rows, r * 128:r * 128 + rows],
                              tri_le[:rows, :rows], start=True, stop=True)
                        for r in range(4):
                            O("act", "activation", [pbuf[3]], [BE], out=Em[:rows, r * 128:r * 128 + rows], in_=bank(3)[:rows, r * 128:r * 128 + rows], func=AF.Exp)
                            O("dve", "tensor_tensor", [BE, Bcbm], [BM], out=Mm[:rows, r * 128:r * 128 + rows], in0=Em[:rows, r * 128:r * 128 + rows],
                              in1=cbm[:rows, :rows], op=ALU.mult)
                        for r in range(4):
                            h = 4 * g + r
                            O("pe", "matmul", [BM, Bx[i]], [pbuf[4]], bank(4)[:rows, r * 64:(r + 1) * 64], Mm[:rows, r * 128:r * 128 + rows],
                              xdt[i][:rows, h * 64:(h + 1) * 64], start=True, stop=True)
                        O("pe", "matmul", [BBT, BSb], [pbuf[4]], bank(4)[:rows, 256:512], CT[:, :rows], Sb[:, g * 256:(g + 1) * 256], start=True, stop=True)
                        O("dve", "tensor_tensor", [pbuf[4], Bsm[i]], [Btg], out=tmpg.rearrange("p (h d) -> p h d", d=64)[:rows],
                          in0=bank(4)[:, 256:512].rearrange("p (h d) -> p h d", d=64)[:rows],
                          in1=eacs_[:rows, 4 * g:4 * g + 4].unsqueeze(2).to_broadcast([rows, 4, 64]), op=ALU.mult)
                        O("dve", "tensor_tensor", [pbuf[4], Btg], [BY[i]], out=Y[i][:rows, g * 256:(g + 1) * 256], in0=bank(4)[:rows, 0:256],
                          in1=tmpg[:rows, :], op=ALU.add)
                O("dve", "tensor_tensor", [BS, Bsm[i], BSb], [BS], out=S.rearrange("p (h d) -> p h d", d=64),
                  in0=S.rearrange("p (h d) -> p h d", d=64), in1=cdec_.unsqueeze(2).to_broadcast([128, SH, 64]), op=ALU.mult)
                for gp in range(0, G, 2):
                    bk = 5 + (gp // 2) % 2
                    for g in (gp, gp + 1):
                        O("pe", "matmul", [Bx[i]], [pbuf[bk]], bank(bk)[:, (g - gp) * 256:(g - gp + 1) * 256], Bb[i][:rows, g * 128:(g + 1) * 128],
                          xdte[i][:rows, g * 256:(g + 1) * 256], start=True, stop=True)
                    O("dve", "tensor_tensor", [BS, pbuf[bk]], [BS], out=S[:, gp * 256:(gp + 2) * 256], in0=S[:, gp * 256:(gp + 2) * 256],
                      in1=bank(bk), op=ALU.add)
                if own:
                    O("pool", "tensor_tensor", [Bld[i], Bk, Bx[i]], [Bld[i]], out=xa3[:rows], in0=xa3[:rows],
                      in1=dskb[:rows, :].unsqueeze(2).to_broadcast([rows, SH, 64]), op=ALU.mult)
                    O("dve", "tensor_tensor", [BY[i], Bld[i]], [BY[i]], out=Y[i][:rows, :], in0=Y[i][:rows, :], in1=xa[i][:rows, :], op=ALU.add)
                    O("act", "activation", [Bld[i]], [Bld[i]], out=za[i][:rows, :], in_=za[i][:rows, :], func=AF.Silu)
                    O("dve", "tensor_tensor", [BY[i], Bld[i]], [BY[i]], out=Y[i][:rows, :], in0=Y[i][:rows, :], in1=za[i][:rows, :], op=ALU.mult)
                    O("pool", "memset", [], [Bgst], gst, 0.0)
                    for g in range(G):
                        O("act", "activation", [BY[i]], [Bld[i], Bgst], out=xa[i][:rows, g * 256:(g + 1) * 256], in_=Y[i][:rows, g * 256:(g + 1) * 256],
                          func=AF.Square, accum_out=gst[:rows, g:g + 1])
                    O("dve", "tensor_scalar", [Bgst], [Bgst], out=gst[:rows, G:2 * G], in0=gst[:rows, 0:G], scalar1=1.0 / 256, scalar2=EPS,
                      op0=ALU.mult, op1=ALU.add)
                    O("act", "sqrt", [Bgst], [Bgst], out=gst[:rows, 2 * G:3 * G], in_=gst[:rows, G:2 * G])
                    O("dve", "reciprocal", [Bgst], [Bgst], out=gst[:rows, 3 * G:4 * G], in_=gst[:rows, 2 * G:3 * G])
                    O("dve", "tensor_tensor", [BY[i], Bgst], [BY[i]], out=Y[i].rearrange("p (g d) -> p g d", d=256)[:rows],
                      in0=Y[i].rearrange("p (g d) -> p g d", d=256)[:rows],
                      in1=gst[:rows, 3 * G:4 * G].unsqueeze(2).to_broadcast([rows, G, 256]), op=ALU.mult)
                    O("dve", "tensor_tensor", [BY[i], Bk], [Byo[i]], out=yo[i][:rows, :], in0=Y[i][:rows, :], in1=nwb[:rows, :], op=ALU.mult)
                    DMA("sp", mix_d[o0:o0 + rows, AW:AW + SI], yo[i][:rows, :], reads=[Byo[i]])

            def state_out(dst):
                for j in range(SI // 128):
                    O("pe", "transpose", [BS, Bc], [pbuf[7]], out=bank(7)[:, 0:128], in_=S[:, j * 128:(j + 1) * 128], identity=ident_f)
                    O("dve", "tensor_copy", [pbuf[7]], [Btr], out=tr, in_=bank(7)[:, 0:128])
                    DMA("sp", dst[j * 128:(j + 1) * 128, :], tr, reads=[Btr])

            n = 0
            for t in range(NTILE):
                ssd_tile(t, n)
                n += 1
            state_out(ssm_p)
            for j in range(SI // 128):
                DMA("sp", tr, sssm[j * 128:(j + 1) * 128, :], writes=[Btr])
                O("pe", "transpose", [Btr, Bc], [pbuf[7]], out=bank(7)[:, 0:128], in_=tr, identity=ident_f)
                O("dve", "tensor_copy", [pbuf[7], BSb], [BS], out=S[:, j * 128:(j + 1) * 128], in_=bank(7)[:, 0:128])
            ssd_tile(NTILE, n)
            state_out(ssm_s)
            P.barrier()
            top[0] = base_top

        if "C" in phases:
            lq = f32v(256)
            lt = f32v(8)
            Bl = Buf("lam")
            DMA("sp", lq, lam_in[0:1, :].partition_broadcast(128), writes=[Bl])
            O("pool", "memset", [], [Bl], lt, 0.0)
            ljunk = f32v(64)
            O("dve", "tensor_tensor", [Bl], [Bl], out=lq[:, 0:64], in0=lq[:, 0:64], in1=lq[:, 64:128], op=ALU.mult)
            O("dve", "tensor_tensor", [Bl], [Bl], out=lq[:, 128:192], in0=lq[:, 128:192], in1=lq[:, 192:256], op=ALU.mult)
            O("act", "activation", [Bl], [Bl], out=ljunk, in_=lq[:, 0:64], func=AF.Copy, accum_out=lt[:, 0:1])
            O("act", "activation", [Bl], [Bl], out=ljunk, in_=lq[:, 128:192], func=AF.Copy, accum_out=lt[:, 1:2])
            O("act", "activation", [Bl], [Bl], out=lt[:, 2:4], in_=lt[:, 0:2], func=AF.Exp)
            O("dve", "tensor_tensor", [Bl], [Bl], out=lt[:, 4:5], in0=lt[:, 2:3], in1=lt[:, 3:4], op=ALU.subtract)
            O("dve", "tensor_scalar", [Bl], [Bl], out=lt[:, 5:6], in0=lt[:, 4:5], scalar1=LAM0, scalar2=-1.0, op0=ALU.add, op1=ALU.mult)
            nlam = lt[:, 5:6]
            slw = f32v(128)
            DMA("sp", slw, subln_w[0:1, :].partition_broadcast(128), writes=[Bl])
            O("dve", "tensor_scalar", [Bl], [Bl], out=slw, in0=slw, scalar1=1.0 - LAM0, scalar2=None, op0=ALU.mult)
            c_top = top[0]

            def prep(ksrc, vsrc, qsrc, vlsrc, rows, kt, KTd, VAd, QTd, qcol, n):
                i = n % 2
                kf, vf, qf = pk[i], pvv[i], pq[i]
                DMA("sp", kf[:rows, :], ksrc, writes=[Bpk[i]])
                DMA("sp", vf[:rows, :], vsrc, writes=[Bpk[i]])
                kb_, vb_ = pkb[i], pvb[i]
                O("act", "copy", [Bpk[i]], [Bpb[i]], out=kb_[:rows, :], in_=kf[:rows, :])
                vb3 = vb_.rearrange("p (h d) -> p h d", d=129)
                O("dve", "tensor_copy", [Bpk[i]], [Bpb[i]], out=vb3[:rows, :, 0:128], in_=vf.rearrange("p (h d) -> p h d", d=128)[:rows])
                if vlsrc is None:
                    O("pool", "memset", [], [Bpb[i]], vb3[:rows, :, 128:129], 1.0)
                else:
                    DMA("sp", pvl[i][:rows, :], vlsrc, writes=[Bpk[i]])
                    O("pool", "tensor_copy", [Bpk[i]], [Bpb[i]], out=vb3[:rows, :, 128:129],
                      in_=pvl[i][:rows, 0:1].unsqueeze(1).to_broadcast([rows, H, 1]))
                DMA("sp", VAd.rearrange("h p (t d) -> p h t d", d=129)[:rows, :, kt, :], vb3[:rows], reads=[Bpb[i]])
                srcs = [(kb_, KTd, kt * 128)]
                if qsrc is not None:
                    DMA("sp", qf[:rows, :], qsrc, writes=[Bpk[i]])
                    O("act", "activation", [Bpk[i]], [Bpb[i]], out=pqb[i][:rows, :], in_=qf[:rows, :], func=AF.Copy, scale=0.125)
                    srcs.append((pqb[i], QTd, qcol))
                for sb_, dstd, col in srcs:
                    for g in range((H + 7) // 8):
                        hh = min(8, H - g * 8)
                        pv_ = bank(7).bitcast(BF16)
                        for j in range(hh):
                            h = g * 8 + j
                            O("pe", "transpose", [Bpb[i], Bc], [pbuf[7]], out=pv_[:, j * 128:j * 128 + rows], in_=sb_[:rows, h * 128:(h + 1) * 128],
                              identity=ident_b[:rows, :rows])
                        O("dve", "tensor_copy", [pbuf[7]], [Bpt], out=ptt[:, :hh * 128], in_=pv_[:, :hh * 128])
                        DMA("sp", dstd.rearrange("h p n -> p h n")[:, g * 8:g * 8 + hh, col:col + rows],
                            ptt.rearrange("p (h n) -> p h n", n=128)[:, :hh, :rows], reads=[Bpt])

            pk = [f32v(AW) for _ in range(2)]
            pvv = [f32v(AW) for _ in range(2)]
            pq = [f32v(AW) for _ in range(2)]
            pvl = [f32v(1) for _ in range(2)]
            Bpk = [Buf("pk") for _ in range(2)]
            pkb = [bf16v(AW) for _ in range(2)]
            pqb = [bf16v(AW) for _ in range(2)]
            pvb = [bf16v(H * 129) for _ in range(2)]
            Bpb = [Buf("pb") for _ in range(2)]
            ptt = bf16v(1024)
            Bpt = Buf("pt")
            n = 0
            for t in range(NTILE):
                own = t >= T0
                prep(pK_d[t * 128:(t + 1) * 128, :], pV_d[t * 128:(t + 1) * 128, :],
                     pQ_d[(t - T0) * 128:(t - T0 + 1) * 128, :] if own else None, valid[t * 128:(t + 1) * 128, :],
                     128, t, KT_d, VA_d, QT_d, (t - T0) * 128, n)
                n += 1
            for kt in range(NKS - 1):
                prep(ck[kt * 128:(kt + 1) * 128, :], cv[kt * 128:(kt + 1) * 128, :], None, None, 128, kt, KTs_d, VAs_d, None, 0, n)
                n += 1
            prep(pK_d[NTOK:NTOK + DEC, :], pV_d[NTOK:NTOK + DEC, :], pQ_d[TQ:TQ + DEC, :], None, DEC, NKS - 1, KTs_d, VAs_d, QTs_d, 0, n)
            P.barrier()
            top[0] = c_top

            def attn(KTd, VAd, QTd, nq, nkt, rows_last, causal, orow0, hn):
                QB = min(512, nq)
                SW = min(128, QB)
                sub = QB // SW
                i = hn % 2
                kts = (nkt - 1) * 128 + rows_last
                DMA("sp", KTb[i][:, :kts], KTd[:, :kts], writes=[BKT[i]])
                DMA("sp", VAb[i][:, :nkt * 129], VAd[:, :nkt * 129], writes=[BKT[i]])
                DMA("sp", QTb[i][:, :nq], QTd[:, :nq], writes=[BKT[i]])
                VA3 = VAb[i].rearrange("p (t d) -> p t d", d=129)
                def acc(m, si):
                    if si < 3:
                        return 4 + m, si * 129
                    return 6, m * 129

                pairs = []
                for qb in range(nq // QB):
                    kt_last = (T0 + qb * sub + sub - 1) if causal else nkt - 1
                    d0 = (T0 + qb * sub) if causal else nkt
                    for kt in range(kt_last + 1):
                        pairs.append((qb, kt, d0, kt == kt_last))

                def emit_scores(n):
                    qb, kt, d0, _ = pairs[n]
                    q0 = qb * QB
                    rk = rows_last if kt == nkt - 1 else 128
                    di = kt - d0 if kt >= d0 else -1
                    qlo = di * SW if di >= 0 else 0
                    ps = n % 3
                    for m in range(2):
                        sbk = (n % 2) * 2 + m
                        O("pe", "matmul", [BKT[i]], [pbuf[sbk]], bank(sbk)[:rk, qlo:QB], KTb[i][m * 64:(m + 1) * 64, kt * 128:kt * 128 + rk],
                          QTb[i][m * 64:(m + 1) * 64, q0 + qlo:q0 + QB], start=True, stop=True)
                        O("act", "activation", [pbuf[sbk]], [BpT[ps][m]], out=pT[ps][m][:rk, qlo:QB], in_=bank(sbk)[:rk, qlo:QB], func=AF.Exp)
                        if di >= 0:
                            O("pool", "memset", [], [BpT[ps][m]], pT[ps][m][64:128, qlo:qlo + 64], 0.0)

                def emit_pv(n):
                    qb, kt, d0, last = pairs[n]
                    q0 = qb * QB
                    rk = rows_last if kt == nkt - 1 else 128
                    di = kt - d0 if kt >= d0 else -1
                    ps = n % 3
                    for si in range(max(di, 0), sub):
                        last_for_si = (d0 + si) if causal else nkt - 1
                        for m in range(2):
                            ab, ao = acc(m, si)
                            first_in_bank = (kt == 0) and ao == 0 and (ab != 6 or m == 0)
                            O("pe", "matmul", [BpT[ps][m], BKT[i]], [pbuf[ab]], bank(ab)[:SW, ao:ao + 129], pT[ps][m][:rk, si * SW:(si + 1) * SW],
                              VA3[:rk, kt, :], start=first_in_bank, stop=(kt == last_for_si), skip_group_check=True)
                    if not last:
                        return
                    for si in range(sub):
                        a1b, a1o = acc(0, si)
                        a2b, a2o = acc(1, si)
                        o1 = bank(a1b)[:SW, a1o:a1o + 129]
                        o2 = bank(a2b)[:SW, a2o:a2o + 129]
                        j = si % 2
                        s_ = fs[j]
                        O("dve", "reciprocal", [pbuf[a1b]], [Bfs[j]], out=s_[:SW, 0:1], in_=o1[:, 128:129])
                        O("dve", "reciprocal", [pbuf[a2b]], [Bfs[j]], out=s_[:SW, 1:2], in_=o2[:, 128:129])
                        O("dve", "tensor_tensor", [Bfs[j], Bl], [Bfs[j]], out=s_[:SW, 2:3], in0=s_[:SW, 1:2], in1=nlam[:SW, :], op=ALU.mult)
                        O("act", "activation", [pbuf[a1b], Bfs[j]], [Bfa[j]], out=fa[j][:SW, :], in_=o1[:, 0:128], func=AF.Copy, scale=s_[:SW, 0:1])
                        O("dve", "scalar_tensor_tensor", [pbuf[a2b], Bfs[j], Bfa[j]], [Bfa[j]], out=fa[j][:SW, :], in0=o2[:, 0:128],
                          scalar=s_[:SW, 2:3], in1=fa[j][:SW, :], op0=ALU.mult, op1=ALU.add)
                        O("pool", "memset", [], [Bfs[j]], s_[:, 3:4], 0.0)
                        O("act", "activation", [Bfa[j]], [Bfj, Bfs[j]], out=fj[:SW, :], in_=fa[j][:SW, :], func=AF.Square, accum_out=s_[:SW, 3:4])
                        O("dve", "tensor_scalar", [Bfs[j]], [Bfs[j]], out=s_[:SW, 4:5], in0=s_[:SW, 3:4], scalar1=1.0 / 128, scalar2=EPS,
                          op0=ALU.mult, op1=ALU.add)
                        O("act", "sqrt", [Bfs[j]], [Bfs[j]], out=s_[:SW, 5:6], in_=s_[:SW, 4:5])
                        O("dve", "reciprocal", [Bfs[j]], [Bfs[j]], out=s_[:SW, 6:7], in_=s_[:SW, 5:6])
                        O("dve", "scalar_tensor_tensor", [Bfa[j], Bfs[j], Bl], [Bfo[j]], out=fo[j][:SW, :], in0=fa[j][:SW, :], scalar=s_[:SW, 6:7],
                          in1=slw[:SW, :], op0=ALU.mult, op1=ALU.mult)
                        r0 = orow0 + q0 + si * SW
                        DMA("sp", mix_d[r0:r0 + SW, hcol[0]:hcol[0] + 128], fo[j][:SW, :], reads=[Bfo[j]])

                for n in range(len(pairs) + 1):
                    if n < len(pairs):
                        emit_scores(n)
                    if n >= 1:
                        emit_pv(n - 1)

            KTb = [bf16v(max(NTOK, NKS * 128)) for _ in range(2)]
            VAb = [bf16v(max(NTILE, NKS) * 129) for _ in range(2)]
            QTb = [bf16v(max(TQ, DEC)) for _ in range(2)]
            BKT = [Buf("KT") for _ in range(2)]
            pT = [[bf16v(512) for _ in range(2)] for _ in range(3)]
            BpT = [[Buf("pT") for _ in range(2)] for _ in range(3)]
            fs = [f32v(8) for _ in range(2)]
            Bfs = [Buf("fs") for _ in range(2)]
            fa = [f32v(128) for _ in range(2)]
            Bfa = [Buf("fa") for _ in range(2)]
            fj = f32v(128)
            Bfj = Buf("fj")
            fo = [bf16v(128) for _ in range(2)]
            Bfo = [Buf("fo") for _ in range(2)]
            hcol = [0]
            hn = 0
            for h in range(H):
                hcol[0] = h * 128
                attn(KT_d[h], VA_d[h], QT_d[h], TQ, NTILE, 128, True, 0, hn)
                hn += 1
                attn(KTs_d[h], VAs_d[h], QTs_d[h], DEC, NKS, DEC, False, TQ, hn)
                hn += 1
            P.barrier()
            top[0] = base_top

        if "E" in phases:
            def orow(ot):
                return ot * 128
            normT_pass(lambda ot: mix_d[ot * 128:ot * 128 + own_rows(ot), :], NT + 1, own_rows, D, None, mT_d, src_bf16=True)
            xr = [f32v(512) for _ in range(3)]
            Bxr = [Buf("xr") for _ in range(3)]

            def evacE(ot, rows, c0, cw, bks, n):
                i = n % 3
                src = xq[(T0 + ot) * 128:(T0 + ot) * 128 + rows, c0:c0 + cw] if ot < NT else xs[:, c0:c0 + cw]
                DMA("sp", xr[i][:rows, :cw], src, writes=[Bxr[i]])
                O("dve", "tensor_tensor", [pbuf[bks[0]], Bxr[i]], [Bxr[i]], out=xr[i][:rows, :cw], in0=bank(bks[0])[:rows, :cw],
                  in1=xr[i][:rows, :cw], op=ALU.add)
                DMA("pool", x1_d[ot * 128:ot * 128 + rows, c0:c0 + cw], xr[i][:rows, :cw], reads=[Bxr[i]])
            proj_pass(mT_d, KC, [w_out], D, 1024, lambda c0, cw: list(range(NT + 1)), own_rows, evacE)
            top[0] = base_top
            normT_pass(lambda ot: x1_d[ot * 128:ot * 128 + own_rows(ot), :], NT + 1, own_rows, D, norm2_w[0:1, :], h2T_d)

        if "F" in phases:
            CWF = 512
            gs = [f32v(CWF) for _ in range(3)]
            go = [bf16v(CWF) for _ in range(3)]
            Bgs = [Buf("gs") for _ in range(3)]
            Bgo = [Buf("go") for _ in range(3)]

            def evacF1(ot, rows, c0, cw, bks, n):
                i = n % 3
                O("act", "activation", [pbuf[bks[0]]], [Bgs[i]], out=gs[i][:rows, :cw], in_=bank(bks[0])[:rows, :cw], func=AF.Silu)
                O("dve", "tensor_tensor", [Bgs[i], pbuf[bks[1]]], [Bgo[i]], out=go[i][:rows, :cw], in0=gs[i][:rows, :cw],
                  in1=bank(bks[1])[:rows, :cw], op=ALU.mult)
                DMA("pool", ff_d[ot * 128:ot * 128 + rows, c0:c0 + cw], go[i][:rows, :cw], reads=[Bgo[i]])
            proj_pass(h2T_d, KC, [w_gate, w_up], DFF, CWF, lambda c0, cw: list(range(NT + 1)), own_rows, evacF1)
            top[0] = base_top
            normT_pass(lambda ot: ff_d[ot * 128:ot * 128 + own_rows(ot), :], NT + 1, own_rows, DFF, None, ffT_d, src_bf16=True)
            xr = [f32v(CWF) for _ in range(3)]
            Bxr = [Buf("xr") for _ in range(3)]

            def evacF2(ot, rows, c0, cw, bks, n):
                i = n % 3
                DMA("sp", xr[i][:rows, :cw], x1_d[ot * 128:ot * 128 + rows, c0:c0 + cw], writes=[Bxr[i]])
                O("dve", "tensor_tensor", [pbuf[bks[0]], Bxr[i]], [Bxr[i]], out=xr[i][:rows, :cw], in0=bank(bks[0])[:rows, :cw],
                  in1=xr[i][:rows, :cw], op=ALU.add)
                DMA("pool", x2_d[ot * 128:ot * 128 + rows, c0:c0 + cw], xr[i][:rows, :cw], reads=[Bxr[i]])
            proj_pass(ffT_d, DFF // 128, [w_down], D, CWF, lambda c0, cw: list(range(NT + 1)), own_rows, evacF2, wbufs=1)
            top[0] = base_top
            fw = f32v(D)
            Bfw = Buf("fw")
            DMA("sp", fw, final_norm_w[0:1, :].partition_broadcast(128), writes=[Bfw])
            xt = [f32v(D) for _ in range(2)]
            Bxt = [Buf("xt") for _ in range(2)]
            jk = bf16v(D)
            Bjk = Buf("jk")
            st = [f32v(4) for _ in range(2)]
            Bst = [Buf("st") for _ in range(2)]
            for ot in range(NT + 1):
                i = ot % 2
                rows = own_rows(ot)
                DMA("sp", xt[i][:rows, :], x2_d[ot * 128:ot * 128 + rows, :], writes=[Bxt[i]])
                O("pool", "memset", [], [Bst[i]], st[i], 0.0)
                O("act", "activation", [Bxt[i]], [Bjk, Bst[i]], out=jk[:rows, :], in_=xt[i][:rows, :], func=AF.Square, accum_out=st[i][:rows, 0:1])
                O("dve", "tensor_scalar", [Bst[i]], [Bst[i]], out=st[i][:rows, 1:2], in0=st[i][:rows, 0:1], scalar1=1.0 / D, scalar2=EPS,
                  op0=ALU.mult, op1=ALU.add)
                O("act", "sqrt", [Bst[i]], [Bst[i]], out=st[i][:rows, 3:4], in_=st[i][:rows, 1:2])
                O("dve", "reciprocal", [Bst[i]], [Bst[i]], out=st[i][:rows, 2:3], in_=st[i][:rows, 3:4])
                O("dve", "scalar_tensor_tensor", [Bxt[i], Bst[i], Bfw], [Bxt[i]], out=xt[i][:rows, :], in0=xt[i][:rows, :],
                  scalar=st[i][:rows, 2:3], in1=fw[:rows, :], op0=ALU.mult, op1=ALU.mult)
                dst = y_q[ot * 128:ot * 128 + rows, :] if ot < NT else y_s[:, :]
                DMA("sp", dst, xt[i][:rows, :], reads=[Bxt[i]])

        P.finish()
        P.emit()
    return nc


def host_inputs(cfg, inp):
    c = cfg
    f = lambda a: np.ascontiguousarray(np.asarray(a, dtype=np.float32))
    xp = f(inp["x_prompt"])
    B = xp.shape[0]
    ncores = B * NSLOT
    lam_in = np.concatenate([f(inp["lambda_q1"]), f(inp["lambda_k1"]), f(inp["lambda_q2"]), f(inp["lambda_k2"])], 0).reshape(1, 256)
    u = np.arange(128)
    consts = np.concatenate([np.eye(128), (u[:, None] <= u[None, :]), (u[:, None] > u[None, :]), np.ones((128, 128))], 1).astype(np.float32)
    shared = {
        "norm1_w": f(inp["norm1_w"]), "w_in": f(inp["w_in"])[0], "lam_in": lam_in,
        "subln_w": f(inp["subln_w"]), "conv_w": f(inp["conv_w"])[0], "conv_b": f(inp["conv_b"]),
        "dt_bias": f(inp["dt_bias"]), "A_log": f(inp["A_log"]), "D_skip": f(inp["D_skip"]),
        "ssd_norm_w": f(inp["ssd_norm_w"]), "w_out": f(inp["w_out"])[0], "norm2_w": f(inp["norm2_w"]),
        "w_gate": f(inp["w_gate"])[0], "w_up": f(inp["w_up"])[0], "w_down": f(inp["w_down"])[0],
        "final_norm_w": f(inp["final_norm_w"]).reshape(1, -1),
        "consts": consts,
    }
    maps = []
    for core in range(ncores):
        b, j = core // NSLOT, core % NSLOT
        xq = np.zeros((NSLOT * c.TQ, c.D), np.float32)
        valid = np.zeros((NSLOT * c.TQ, 1), np.float32)
        n = (j + 1) * c.TQ
        xq[NSLOT * c.TQ - n:] = xp[b, :n]
        valid[NSLOT * c.TQ - n:] = 1.0
        m = dict(shared)
        m.update({
            "xq": xq, "valid": valid, "xs": f(inp["x_sample"])[core],
            "ck": f(inp["cache_k"])[0, core].reshape(c.PAST, c.AW),
            "cv": f(inp["cache_v"])[0, core].reshape(c.PAST, c.AW),
            "sconv": f(inp["state_conv"])[0, core],
            "sssm": f(inp["state_ssm"])[0, core].reshape(c.SH * 64, 128),
        })
        maps.append(m)
    return maps


def assemble(cfg, res, B):
    c = cfg
    ncores = B * NSLOT
    g = lambda name: [np.asarray(res[i][name], dtype=np.float32) for i in range(ncores)]
    yq, ys, kq, vq, cp, sp_, ks, vs, cs, ss = (g(n) for n in
        ["y_q", "y_s", "k_q", "v_q", "conv_p", "ssm_p", "k_s", "v_s", "conv_s", "ssm_s"])
    cat = lambda lst, b: np.concatenate(lst[b * NSLOT:(b + 1) * NSLOT], 0)
    y_prompt = np.stack([cat(yq, b) for b in range(B)])
    k_prompt = np.stack([cat(kq, b) for b in range(B)]).reshape(1, B, c.SEQ, c.H, 128)
    v_prompt = np.stack([cat(vq, b) for b in range(B)]).reshape(1, B, c.SEQ, c.H, 128)
    conv_prompt = np.stack([cp[b * NSLOT + NSLOT - 1] for b in range(B)])[None]
    ssm_prompt = np.stack([sp_[b * NSLOT + NSLOT - 1] for b in range(B)]).reshape(1, B, c.SH, 64, 128)
    y_sample = np.stack(ys)
    k_sample = np.stack(ks).reshape(1, ncores, c.DEC, c.H, 128)
    v_sample = np.stack(vs).reshape(1, ncores, c.DEC, c.H, 128)
    conv_sample = np.stack(cs)[None]
    ssm_sample = np.stack(ss).reshape(1, ncores, c.SH, 64, 128)
    return (y_prompt, y_sample, k_prompt, v_prompt, conv_prompt, ssm_prompt,
            k_sample, v_sample, conv_sample, ssm_sample)


def run(cfg, inp, phases="ABXDCEF", dbg=()):
    B = np.asarray(inp["x_prompt"]).shape[0]
    maps = host_inputs(cfg, inp)
    nc = build(cfg, phases, dbg)
    res = run_bass_kernel_spmd(nc, maps, core_ids=list(range(len(maps))))
    if dbg:
        return assemble(cfg, res.results, B), res.results
    return assemble(cfg, res.results, B)


def kernel(**inputs):
    return run(Cfg(), inputs)
```

```python
import contextlib
import numpy as np
import concourse.bass as bass
import concourse.mybir as mybir
from concourse.bass_utils import run_bass_kernel_spmd

F32 = mybir.dt.float32
BF16 = mybir.dt.bfloat16
AF = mybir.ActivationFunctionType
ALU = mybir.AluOpType
EPS = 1e-6
NSLOT = 4
KDMA = 8


class Cfg:
    def __init__(self, D=4096, SEQ=8192, DFF=11008, PAST=2048, DEC=16, G=8):
        self.D, self.SEQ, self.DFF, self.PAST, self.DEC, self.G = D, SEQ, DFF, PAST, DEC, G
        self.AW = D // 2
        self.H = self.AW // 128
        self.SI = D - self.AW
        self.SH = self.SI // 64
        self.CD = self.SI + 2 * G * 128
        self.IN = 3 * self.AW + self.SI + self.CD + self.SH
        self.TQ = SEQ // NSLOT
        self.NT = self.TQ // 128
        self.KC = D // 128
        self.oQ, self.oK, self.oV, self.oZ = 0, self.AW, 2 * self.AW, 3 * self.AW
        self.oX = 3 * self.AW + self.SI
        self.oDT = self.oX + self.CD


class Buf:
    def __init__(self, name):
        self.name, self.w, self.r = name, None, []


class Prog:
    ENG = ["pe", "act", "dve", "pool", "sp"]

    def __init__(self, nc, stack):
        self.nc = nc
        self.ops = {e: [] for e in self.ENG}
        self.esem = {e: stack.enter_context(nc.semaphore("s_" + e)) for e in ["pe", "act", "dve", "pool"]}
        self.cnt = {e: 0 for e in self.esem}
        self.dsem = {q: [stack.enter_context(nc.semaphore("d_%s%d" % (q, i))) for i in range(KDMA)]
                     for q in ["sp", "pool", "act"]}
        self.dn = {"sp": 0, "pool": 0, "act": 0}
        self.seen = {e: {} for e in self.ENG}
        self.pend = {e: [] for e in self.ENG}

    def op(self, eng, fn, reads=(), writes=(), dma=False):
        waits = list(self.pend[eng])
        self.pend[eng] = []
        for b in reads:
            if b.w:
                waits.append(b.w)
        for b in writes:
            if b.w:
                waits.append(b.w)
            waits.extend(b.r)
        if dma:
            m = self.dn[eng]
            sem = self.dsem[eng][m % KDMA]
            prev = 16 * (m // KDMA)
            if prev:
                waits.append((sem, prev))
            ev = (sem, prev + 16)
            inc = 16
            self.dn[eng] += 1
        else:
            self.cnt[eng] += 1
            ev = (self.esem[eng], self.cnt[eng])
            inc = 1
        need = {}
        for s, v in waits:
            if eng == "pe" and s is self.esem["pe"]:
                continue
            if self.seen[eng].get(id(s), 0) >= v:
                continue
            if need.get(id(s), (s, 0))[1] < v:
                need[id(s)] = (s, v)
        for k, (s, v) in need.items():
            self.seen[eng][k] = v
        self.ops[eng].append((list(need.values()), fn, ev[0], inc))
        for b in reads:
            b.r.append(ev)
        for b in writes:
            b.w, b.r = ev, []
        return ev

    def all_events(self):
        evs = [(self.esem[e], self.cnt[e]) for e in self.esem if self.cnt[e]]
        for q in self.dsem:
            m = self.dn[q]
            for i in range(KDMA):
                n = (m - i + KDMA - 1) // KDMA if m > i else 0
                if n:
                    evs.append((self.dsem[q][i], 16 * n))
        return evs

    def barrier(self):
        evs = self.all_events()
        for e in self.ENG:
            self.pend[e] = list(evs)

    def finish(self):
        self.barrier()
        for e in self.ENG:
            need = {}
            for s, v in self.pend[e]:
                if self.seen[e].get(id(s), 0) < v:
                    need[id(s)] = (s, v)
            self.ops[e].append((list(need.values()), None, None, 0))

    def emit(self):
        nc = self.nc
        names = {"pe": "tensor", "act": "scalar", "dve": "vector", "pool": "gpsimd", "sp": "sync"}
        with nc.Block() as block:
            for e in self.ENG:
                def body(engobj, e=e):
                    for need, fn, sem, inc in self.ops[e]:
                        for s, v in need:
                            engobj.wait_ge(s, v)
                        if fn is not None:
                            fn(engobj).then_inc(sem, inc)
                getattr(block, names[e])(body)


def build(cfg, phases="ABXDCEF", dbg=()):
    c = cfg
    nc = bass.Bass("TRN2", target_bir_lowering=False)
    D, KC, IN, H, AW, SI, SH, G, CD, DFF = c.D, c.KC, c.IN, c.H, c.AW, c.SI, c.SH, c.G, c.CD, c.DFF
    TQ, NT, DEC, PAST = c.TQ, c.NT, c.DEC, c.PAST
    NTOK = NSLOT * TQ
    NTILE = NSLOT * NT
    T0 = NTILE - NT
    GB = G * 128
    NKS = PAST // 128 + 1
    LAM0 = 0.2

    def din(name, shape, dt=F32):
        return nc.dram_tensor(name, list(shape), dt, kind="ExternalInput").ap()

    def dout(name, shape, dt=F32):
        return nc.dram_tensor(name, list(shape), dt, kind="ExternalOutput").ap()

    def dscr(name, shape, dt):
        kind = "ExternalOutput" if name in dbg else "Internal"
        return nc.dram_tensor(name, list(shape), dt, kind=kind).ap()

    xq = din("xq", [NTOK, D])
    valid = din("valid", [NTOK, 1])
    xs = din("xs", [DEC, D])
    ck = din("ck", [PAST, AW])
    cv = din("cv", [PAST, AW])
    sconv = din("sconv", [3, CD])
    sssm = din("sssm", [SH * 64, 128])
    norm1_w = din("norm1_w", [1, D])
    w_in = din("w_in", [D, IN])
    lam_in = din("lam_in", [1, 256])
    subln_w = din("subln_w", [1, 128])
    conv_w = din("conv_w", [4, CD])
    conv_b = din("conv_b", [1, CD])
    dt_bias = din("dt_bias", [1, SH])
    A_log = din("A_log", [1, SH])
    D_skip = din("D_skip", [1, SH])
    ssd_norm_w = din("ssd_norm_w", [1, SI])
    w_out = din("w_out", [D, D])
    norm2_w = din("norm2_w", [1, D])
    w_gate = din("w_gate", [D, DFF])
    w_up = din("w_up", [D, DFF])
    w_down = din("w_down", [DFF, D])
    final_norm_w = din("final_norm_w", [1, D])
    consts_in = din("consts", [128, 4 * 128])

    y_q = dout("y_q", [TQ, D])
    y_s = dout("y_s", [DEC, D])
    k_q = dout("k_q", [TQ, AW])
    v_q = dout("v_q", [TQ, AW])
    conv_p = dout("conv_p", [3, CD])
    ssm_p = dout("ssm_p", [SH * 64, 128])
    k_s = dout("k_s", [DEC, AW])
    v_s = dout("v_s", [DEC, AW])
    conv_s = dout("conv_s", [3, CD])
    ssm_s = dout("ssm_s", [SH * 64, 128])

    hT_d = dscr("hT_d", [NTILE + 1, 128, KC * 128], BF16)
    pQ_d = dscr("pQ_d", [TQ + DEC, AW], F32)
    pK_d = dscr("pK_d", [NTOK + DEC, AW], F32)
    pV_d = dscr("pV_d", [NTOK + DEC, AW], F32)
    pZ_d = dscr("pZ_d", [TQ + DEC, SI], F32)
    pX_d = dscr("pX_d", [NTOK + DEC + 6, CD], F32)
    pDT_d = dscr("pDT_d", [NTOK + DEC, SH], F32)
    act_d = dscr("act_d", [NTOK + DEC, CD], F32)
    KT_d = dscr("KT_d", [H, 128, NTOK], BF16)
    QT_d = dscr("QT_d", [H, 128, TQ], BF16)
    VA_d = dscr("VA_d", [H, 128, NTILE * 129], BF16)
    KTs_d = dscr("KTs_d", [H, 128, NKS * 128], BF16)
    QTs_d = dscr("QTs_d", [H, 128, DEC], BF16)
    VAs_d = dscr("VAs_d", [H, 128, NKS * 129], BF16)
    mix_d = dscr("mix_d", [TQ + DEC, D], BF16)
    mT_d = dscr("mT_d", [NT + 1, 128, KC * 128], BF16)
    x1_d = dscr("x1_d", [TQ + DEC, D], F32)
    h2T_d = dscr("h2T_d", [NT + 1, 128, KC * 128], BF16)
    ff_d = dscr("ff_d", [TQ + DEC, DFF], BF16)
    ffT_d = dscr("ffT_d", [NT + 1, 128, DFF], BF16)
    x2_d = dscr("x2_d", [TQ + DEC, D], F32)

    with contextlib.ExitStack() as stack:
        P = Prog(nc, stack)
        ARENA = 44 * 1024
        arena = stack.enter_context(nc.sbuf_tensor("arena", [128, ARENA], F32))
        psum = stack.enter_context(nc.psum_tensor("psum", [128, 8 * 512], F32))
        top = [0]

        def f32v(words):
            o = top[0]
            top[0] += words
            assert top[0] <= ARENA, "SBUF arena overflow %d" % top[0]
            return arena[:, o:o + words]

        def bf16v(elems):
            return f32v((elems + 1) // 2).bitcast(BF16)[:, :elems]

        def bank(i):
            return psum[:, i * 512:(i + 1) * 512]

        pbuf = [Buf("ps%d" % i) for i in range(8)]

        def O(eng, method, reads, writes, *a, **kw):
            return P.op(eng, lambda e: getattr(e, method)(*a, **kw), reads, writes)

        def DMA(q, out, in_, reads=(), writes=()):
            return P.op(q, lambda e: e.dma_start(out=out, in_=in_), reads, writes, dma=True)

        def COPY(eng, out, in_, reads, writes):
            if eng == "act":
                return O("act", "copy", reads, writes, out=out, in_=in_)
            return O(eng, "tensor_copy", reads, writes, out=out, in_=in_)

        cst = f32v(512)
        Bc = Buf("consts")
        DMA("sp", cst, consts_in[:, :], writes=[Bc])
        ident_f, tri_le, tri_gt, ones_f = (cst[:, i * 128:(i + 1) * 128] for i in range(4))
        ident_b = bf16v(128)
        O("dve", "tensor_copy", [Bc], [Bc], out=ident_b, in_=ident_f)
        base_top = top[0]

        def own_rows(ot):
            return 128 if ot < NT else DEC

        def normT_pass(src_of, ntiles, rows_of, ncols, wrow, dstT, src_bf16=False):
            mark = top[0]
            nch = ncols // 128
            grp = 8 if nch % 8 == 0 else (4 if nch % 4 == 0 else (2 if nch % 2 == 0 else 1))
            Bw = Buf("nw")
            if wrow is not None:
                wb = f32v(ncols)
                DMA("sp", wb, wrow.partition_broadcast(128), writes=[Bw])
            xt = [None if src_bf16 else f32v(ncols) for _ in range(2)]
            Bxt = [Buf("xt") for _ in range(2)]
            junk = None if src_bf16 else bf16v(ncols)
            Bj = Buf("junk")
            xn = [bf16v(ncols) for _ in range(2)]
            Bxn = [Buf("xn") for _ in range(2)]
            st = [f32v(4) for _ in range(2)]
            Bst = [Buf("st") for _ in range(2)]
            hs = [bf16v(ncols) for _ in range(2)]
            Bhs = [Buf("hs") for _ in range(2)]
            for t in range(ntiles):
                i = t % 2
                rows = rows_of(t)
                if src_bf16:
                    DMA("sp", xn[i][:rows, :], src_of(t), writes=[Bxn[i]])
                else:
                    DMA("sp", xt[i][:rows, :], src_of(t), writes=[Bxt[i]])
                    if wrow is not None:
                        O("pool", "memset", [], [Bst[i]], st[i], 0.0)
                        O("act", "activation", [Bxt[i]], [Bj, Bst[i]], out=junk[:rows, :], in_=xt[i][:rows, :],
                          func=AF.Square, accum_out=st[i][:rows, 0:1])
                        O("dve", "tensor_scalar", [Bst[i]], [Bst[i]], out=st[i][:rows, 1:2], in0=st[i][:rows, 0:1],
                          scalar1=1.0 / ncols, scalar2=EPS, op0=ALU.mult, op1=ALU.add)
                        O("act", "sqrt", [Bst[i]], [Bst[i]], out=st[i][:rows, 3:4], in_=st[i][:rows, 1:2])
                        O("dve", "reciprocal", [Bst[i]], [Bst[i]], out=st[i][:rows, 2:3], in_=st[i][:rows, 3:4])
                        O("dve", "scalar_tensor_tensor", [Bxt[i], Bst[i], Bw], [Bxn[i]], out=xn[i][:rows, :],
                          in0=xt[i][:rows, :], scalar=st[i][:rows, 2:3], in1=wb[:rows, :], op0=ALU.mult, op1=ALU.mult)
                    else:
                        O("dve", "tensor_copy", [Bxt[i]], [Bxn[i]], out=xn[i][:rows, :], in_=xt[i][:rows, :])
                for g in range(nch // grp):
                    bk = g % 2
                    pv = bank(bk).bitcast(BF16)
                    for kk in range(grp):
                        k = g * grp + kk
                        O("pe", "transpose", [Bxn[i], Bc], [pbuf[bk]], out=pv[:, kk * 128:kk * 128 + rows],
                          in_=xn[i][:rows, k * 128:(k + 1) * 128], identity=ident_b[:rows, :rows])
                    COPY("act" if g % 2 == 0 else "dve", hs[i][:, g * grp * 128:(g + 1) * grp * 128],
                         pv[:, :grp * 128], [pbuf[bk]], [Bhs[i]])
                DMA("sp", dstT[t, :, :], hs[i], reads=[Bhs[i]])
            P.barrier()
            top[0] = mark

        def proj_pass(srcT, KCH, weights, NCOL, CW, tiles_of_block, rows_of, evac, wbufs=2, hbufs=3, hq=("sp",)):
            mark = top[0]
            nw = len(weights)
            SUB = min(512, CW)
            wv = [[bf16v(KCH * CW) for _ in range(wbufs)] for _ in range(nw)]
            Bwv = [[Buf("wv") for _ in range(wbufs)] for _ in range(nw)]
            hb = [bf16v(KCH * 128) for _ in range(hbufs)]
            Bhb = [Buf("hb") for _ in range(hbufs)]
            nblk = (NCOL + CW - 1) // CW
            n_h = 0
            n_b = 0
            n_e = 0
            for cb in range(nblk):
                c0 = cb * CW
                cw = min(CW, NCOL - c0)
                wi = cb % wbufs
                wviews = []
                for j, w in enumerate(weights):
                    wview = wv[j][wi].rearrange("p (k c) -> p k c", k=KCH)
                    DMA("pool", wview[:, :, :cw], w.rearrange("(k p) c -> p k c", p=128)[:, :, c0:c0 + cw],
                        writes=[Bwv[j][wi]])
                    wviews.append(wview)
                for t in tiles_of_block(c0, cw):
                    rows = rows_of(t)
                    hi = n_h % hbufs
                    hview = hb[hi].rearrange("p (k n) -> p k n", k=KCH)
                    DMA(hq[n_h % len(hq)], hb[hi], srcT[t, :, :], writes=[Bhb[hi]])
                    n_h += 1
                    for s0 in range(0, cw, SUB):
                        sw = min(SUB, cw - s0)
                        bks = []
                        for j in range(nw):
                            bk = 2 + n_b % 6
                            n_b += 1
                            bks.append(bk)
                            for k in range(KCH):
                                O("pe", "matmul", [Bhb[hi], Bwv[j][wi]], [pbuf[bk]], bank(bk)[:rows, :sw],
                                  hview[:, k, :rows], wviews[j][:, k, s0:s0 + sw], start=(k == 0), stop=(k == KCH - 1))
                        n_e += 1
                        evac(t, rows, c0 + s0, sw, bks, n_e)
            P.barrier()
            top[0] = mark

        if "A" in phases:
            normT_pass(lambda t: xq[t * 128:(t + 1) * 128, :] if t < NTILE else xs[:, :], NTILE + 1,
                       lambda t: 128 if t < NTILE else DEC, D, norm1_w[0:1, :], hT_d)

        if "B" in phases:
            segs = [("Q", 0, AW, pQ_d, True), ("K", c.oK, AW, pK_d, False), ("V", c.oV, AW, pV_d, False),
                    ("Z", c.oZ, SI, pZ_d, True), ("X", c.oX, CD, pX_d, False), ("DT", c.oDT, SH, pDT_d, False)]
            for name, s0, sw, dst, own_only in segs:
                ob = [f32v(512) for _ in range(3)]
                Bob = [Buf("ob") for _ in range(3)]

                def evacB(t, rows, c0, cw, bks, n, dst=dst, own_only=own_only, name=name, ob=ob, Bob=Bob):
                    i = n % 3
                    COPY("dve", ob[i][:rows, :cw], bank(bks[0])[:rows, :cw], [pbuf[bks[0]]], [Bob[i]])
                    if own_only:
                        r0 = (t - T0) * 128
                    elif name == "X":
                        r0 = 3 + t * 128 if t < NTILE else NTOK + 6
                    else:
                        r0 = t * 128
                    DMA("pool", dst[r0:r0 + rows, c0:c0 + cw], ob[i][:rows, :cw], reads=[Bob[i]])

                tiles = list(range(T0, NTILE + 1)) if own_only else list(range(NTILE + 1))
                proj_pass(hT_d, KC, [w_in[:, s0:s0 + sw]], sw, 1024, lambda c0, cw, tiles=tiles: tiles,
                          lambda t: 128 if t < NTILE else DEC, evacB, hq=("sp", "act"))
                top[0] = base_top
            DMA("sp", k_q[:, :], pK_d[T0 * 128:NTOK, :])
            DMA("sp", v_q[:, :], pV_d[T0 * 128:NTOK, :])
            DMA("sp", conv_p[:, :], pX_d[3 + NTOK - 3:3 + NTOK, :])
            DMA("sp", k_s[:, :], pK_d[NTOK:NTOK + DEC, :])
            DMA("sp", v_s[:, :], pV_d[NTOK:NTOK + DEC, :])
            DMA("sp", conv_s[:, :], pX_d[NTOK + 6 + DEC - 3:NTOK + 6 + DEC, :])
            zt = f32v(CD)
            Bz = Buf("z")
            O("pool", "memset", [], [Bz], zt[:3, :], 0.0)
            DMA("sp", pX_d[0:3, :], zt[:3, :], reads=[Bz])
            DMA("sp", pX_d[NTOK + 3:NTOK + 6, :], sconv[:, :])
            P.barrier()
            top[0] = base_top

        if "X" in phases:
            CC = min(1024, CD)
            for ch0 in range(0, CD, CC):
                cwt = [f32v(CC) for _ in range(5)]
                Bcw = Buf("cw")
                for i in range(4):
                    DMA("sp", cwt[i], conv_w[i:i + 1, ch0:ch0 + CC].partition_broadcast(128), writes=[Bcw])
                DMA("sp", cwt[4], conv_b[0:1, ch0:ch0 + CC].partition_broadcast(128), writes=[Bcw])
                NS = 3
                win = [[f32v(CC) for _ in range(4)] for _ in range(NS)]
                Bwin = [[Buf("win") for _ in range(4)] for _ in range(NS)]
                only_own = ch0 >= SI + GB
                nx = 0
                for t in range(NTILE + 1):
                    if only_own and t < T0:
                        continue
                    rows = 128 if t < NTILE else DEC
                    r0 = 3 + t * 128 if t < NTILE else NTOK + 6
                    g0 = t * 128 if t < NTILE else NTOK
                    s = nx % NS
                    nx += 1
                    W_, B_ = win[s], Bwin[s]
                    for i in range(4):
                        DMA("sp" if i % 2 == 0 else "act", W_[i][:rows, :], pX_d[r0 - 3 + i:r0 - 3 + i + rows, ch0:ch0 + CC], writes=[B_[i]])

                    def TT(eng, a, b, bb, op):
                        O(eng, "tensor_tensor", [B_[a], bb], [B_[a]], out=W_[a][:rows, :], in0=W_[a][:rows, :], in1=b[:rows, :], op=op)
                    TT("pool", 0, cwt[0], Bcw, ALU.mult)
                    TT("pool", 1, cwt[1], Bcw, ALU.mult)
                    TT("dve", 2, cwt[2], Bcw, ALU.mult)
                    TT("dve", 3, cwt[3], Bcw, ALU.mult)
                    TT("pool", 0, W_[1], B_[1], ALU.add)
                    TT("dve", 2, W_[3], B_[3], ALU.add)
                    TT("dve", 2, cwt[4], Bcw, ALU.add)
                    TT("dve", 0, W_[2], B_[2], ALU.add)
                    O("act", "activation", [B_[0]], [B_[1]], out=W_[1][:rows, :], in_=W_[0][:rows, :], func=AF.Silu)
                    DMA("sp", act_d[g0:g0 + rows, ch0:ch0 + CC], W_[1][:rows, :], reads=[B_[1]])
                P.barrier()
                top[0] = base_top

        if "D" in phases:
            dtb, albc, dskb = f32v(SH), f32v(SH), f32v(SH)
            nwb = f32v(SI)
            Bk = Buf("ssdconst")
            DMA("sp", dtb, dt_bias[0:1, :].partition_broadcast(128), writes=[Bk])
            DMA("sp", albc, A_log[0:1, :].partition_broadcast(128), writes=[Bk])
            DMA("sp", dskb, D_skip[0:1, :].partition_broadcast(128), writes=[Bk])
            DMA("sp", nwb, ssd_norm_w[0:1, :].partition_broadcast(128), writes=[Bk])
            Abc = f32v(SH)
            O("act", "activation", [Bk], [Bk], out=Abc, in_=albc, func=AF.Exp)
            O("dve", "tensor_scalar", [Bk], [Bk], out=Abc, in0=Abc, scalar1=-1.0, scalar2=None, op0=ALU.mult)
            S = f32v(SI)
            BS = Buf("S")
            O("pool", "memset", [], [BS], S, 0.0)
            Sb = bf16v(SI)
            BSb = Buf("Sb")
            NB = 2
            xa = [f32v(SI) for _ in range(NB)]
            Ba = [f32v(GB) for _ in range(NB)]
            Ca = [f32v(GB) for _ in range(NB)]
            za = [f32v(SI) for _ in range(NB)]
            dtr = [f32v(SH) for _ in range(NB)]
            vl = [f32v(1) for _ in range(NB)]
            Bld = [Buf("ld") for _ in range(NB)]
            sm = [f32v(8 * SH) for _ in range(NB)]
            Bsm = [Buf("sm") for _ in range(NB)]
            xdt = [bf16v(SI) for _ in range(NB)]
            xdte = [bf16v(SI) for _ in range(NB)]
            Bb = [bf16v(GB) for _ in range(NB)]
            Cb = [bf16v(GB) for _ in range(NB)]
            Bx = [Buf("xd") for _ in range(NB)]
            Y = [f32v(SI) for _ in range(NB)]
            BY = [Buf("Y") for _ in range(NB)]
            yo = [bf16v(SI) for _ in range(NB)]
            Byo = [Buf("yo") for _ in range(NB)]
            BT, CT = bf16v(128), bf16v(128)
            BBT = Buf("BT")
            cbm = f32v(128)
            Bcbm = Buf("cbm")
            Lm = f32v(4 * 128)
            BL = Buf("L")
            Em = f32v(4 * 128)
            BE = Buf("E")
            Mm = bf16v(4 * 128)
            BM = Buf("M")
            tmpg = f32v(256)
            Btg = Buf("tg")
            gst = f32v(4 * G)
            Bgst = Buf("gst")
            tr = f32v(128)
            Btr = Buf("tr")

            def ssd_tile(t, n):
                i = n % NB
                own = t >= T0
                samp = t == NTILE
                rows = DEC if samp else 128
                g0 = NTOK if samp else t * 128
                o0 = (t - T0) * 128
                DMA("sp", xa[i][:rows, :], act_d[g0:g0 + rows, 0:SI], writes=[Bld[i]])
                DMA("sp", Ba[i][:rows, :], act_d[g0:g0 + rows, SI:SI + GB], writes=[Bld[i]])
                DMA("sp", dtr[i][:rows, :], pDT_d[g0:g0 + rows, :], writes=[Bld[i]])
                if samp:
                    O("pool", "memset", [], [Bld[i]], vl[i], 1.0)
                else:
                    DMA("sp", vl[i][:rows, :], valid[g0:g0 + rows, :], writes=[Bld[i]])
                if own:
                    DMA("sp", Ca[i][:rows, :], act_d[g0:g0 + rows, SI + GB:SI + 2 * GB], writes=[Bld[i]])
                    DMA("sp", za[i][:rows, :], pZ_d[o0:o0 + rows, :], writes=[Bld[i]])
                m = sm[i]
                dt_, dA_, w2_, eacs_, dte_, cdec_, tmp_ = (m[:, j * SH:(j + 1) * SH] for j in range(7))
                O("dve", "tensor_tensor", [Bld[i], Bk], [Bsm[i]], out=tmp_[:rows, :], in0=dtr[i][:rows, :], in1=dtb[:rows, :], op=ALU.add)
                O("act", "activation", [Bsm[i]], [Bsm[i]], out=tmp_[:rows, :], in_=tmp_[:rows, :], func=AF.Exp)
                O("act", "activation", [Bsm[i]], [Bsm[i]], out=tmp_[:rows, :], in_=tmp_[:rows, :], func=AF.Ln, bias=1.0)
                O("dve", "tensor_scalar", [Bsm[i], Bld[i]], [Bsm[i]], out=dt_[:rows, :], in0=tmp_[:rows, :],
                  scalar1=vl[i][:rows, 0:1], scalar2=None, op0=ALU.mult)
                O("dve", "tensor_tensor", [Bsm[i], Bk], [Bsm[i]], out=dA_[:rows, :], in0=dt_[:rows, :], in1=Abc[:rows, :], op=ALU.mult)
                pb = bank(0)
                O("pe", "matmul", [Bsm[i], Bc], [pbuf[0]], pb[:rows, 0:SH], tri_le[:rows, :rows], dA_[:rows, :], start=True, stop=True)
                O("pe", "matmul", [Bsm[i], Bc], [pbuf[0]], pb[:rows, SH:2 * SH], tri_gt[:rows, :rows], dA_[:rows, :], start=True, stop=True)
                O("pe", "matmul", [Bsm[i], Bc], [pbuf[0]], pb[:, 2 * SH:3 * SH], ones_f[:rows, :], dA_[:rows, :], start=True, stop=True)
                O("act", "activation", [pbuf[0]], [Bsm[i]], out=eacs_[:rows, :], in_=pb[:rows, 0:SH], func=AF.Exp)
                O("act", "activation", [pbuf[0]], [Bsm[i]], out=dte_[:rows, :], in_=pb[:rows, SH:2 * SH], func=AF.Exp)
                O("act", "activation", [pbuf[0]], [Bsm[i]], out=cdec_, in_=pb[:, 2 * SH:3 * SH], func=AF.Exp)
                O("dve", "tensor_tensor", [Bsm[i]], [Bsm[i]], out=w2_[:rows, :], in0=dt_[:rows, :], in1=dte_[:rows, :], op=ALU.mult)
                xa3 = xa[i].rearrange("p (h d) -> p h d", d=64)
                O("dve", "tensor_tensor", [Bld[i], Bsm[i]], [Bx[i]], out=xdt[i].rearrange("p (h d) -> p h d", d=64)[:rows],
                  in0=xa3[:rows], in1=dt_[:rows, :].unsqueeze(2).to_broadcast([rows, SH, 64]), op=ALU.mult)
                O("pool", "tensor_tensor", [Bld[i], Bsm[i]], [Bx[i]], out=xdte[i].rearrange("p (h d) -> p h d", d=64)[:rows],
                  in0=xa3[:rows], in1=w2_[:rows, :].unsqueeze(2).to_broadcast([rows, SH, 64]), op=ALU.mult)
                O("act", "copy", [Bld[i]], [Bx[i]], out=Bb[i][:rows, :], in_=Ba[i][:rows, :])
                if own:
                    O("act", "copy", [Bld[i]], [Bx[i]], out=Cb[i][:rows, :], in_=Ca[i][:rows, :])
                    O("pool", "tensor_copy", [BS], [BSb], out=Sb, in_=S)
                    for g in range(G):
                        pv = bank(1).bitcast(BF16)
                        O("pe", "transpose", [Bx[i], Bc], [pbuf[1]], out=pv[:, 0:rows], in_=Bb[i][:rows, g * 128:(g + 1) * 128], identity=ident_b[:rows, :rows])
                        O("pe", "transpose", [Bx[i], Bc], [pbuf[1]], out=pv[:, 128:128 + rows], in_=Cb[i][:rows, g * 128:(g + 1) * 128], identity=ident_b[:rows, :rows])
                        O("dve", "tensor_copy", [pbuf[1]], [BBT], out=BT[:, :rows], in_=pv[:, 0:rows])
                        O("dve", "tensor_copy", [pbuf[1]], [BBT], out=CT[:, :rows], in_=pv[:, 128:128 + rows])
                        O("pe", "matmul", [BBT], [pbuf[2]], bank(2)[:rows, :rows], BT[:, :rows], CT[:, :rows], start=True, stop=True)
                        O("dve", "tensor_tensor", [pbuf[2], Bc], [Bcbm], out=cbm[:rows, :rows], in0=bank(2)[:rows, :rows], in1=tri_le[:rows, :rows], op=ALU.mult)
                        for r in range(4):
                            h = 4 * g + r
                            O("pool" if r % 2 else "dve", "tensor_scalar", [Bc, Bsm[i]], [BL], out=Lm[:rows, r * 128:r * 128 + rows],
                              in0=tri_gt[:rows, :rows], scalar1=dA_[:rows, h:h + 1], scalar2=None, op0=ALU.mult)
                        for r in range(4):
                            O("pe", "matmul", [BL, Bc], [pbuf[3]], bank(3)[:rows, r * 128:r * 128 + rows], Lm[:rows, r * 128:r * 128 + rows],
                              tri_le[:rows, :rows], start=True, stop=True)
                        for r in range(4):
                            O("act", "activation", [pbuf[3]], [BE], out=Em[:rows, r * 128:r * 128 + rows], in_=bank(3)[:rows, r * 128:r * 128 + rows], func=AF.Exp)
                            O("dve", "tensor_tensor", [BE, Bcbm], [BM], out=Mm[:rows, r * 128:r * 128 + rows], in0=Em[:rows, r * 128:r * 128 + rows],
                              in1=cbm[:rows, :rows], op=ALU.mult)
                        for r in range(4):
                            h = 4 * g + r
                            O("pe", "matmul", [BM, Bx[i]], [pbuf[4]], bank(4)[:rows, r * 64:(r + 1) * 64], Mm[:rows, r * 128:r * 128 + rows],
                              xdt[i][:rows, h * 64:(h + 1) * 64], start=True, stop=True)
                        O("pe", "matmul", [BBT, BSb], [pbuf[4]], bank(4)[:rows, 256:512], CT[:, :rows], Sb[:, g * 256:(g + 1) * 256], start=True, stop=True)
                        O("dve", "tensor_tensor", [pbuf[4], Bsm[i]], [Btg], out=tmpg.rearrange("p (h d) -> p h d", d=64)[:rows],
                          in0=bank(4)[:, 256:512].rearrange("p (h d) -> p h d", d=64)[:rows],
                          in1=eacs_[:rows, 4 * g:4 * g + 4].unsqueeze(2).to_broadcast([rows, 4, 64]), op=ALU.mult)
                        O("dve", "tensor_tensor", [pbuf[4], Btg], [BY[i]], out=Y[i][:rows, g * 256:(g + 1) * 256], in0=bank(4)[:rows, 0:256],
                          in1=tmpg[:rows, :], op=ALU.add)
                O("dve", "tensor_tensor", [BS, Bsm[i], BSb], [BS], out=S.rearrange("p (h d) -> p h d", d=64),
                  in0=S.rearrange("p (h d) -> p h d", d=64), in1=cdec_.unsqueeze(2).to_broadcast([128, SH, 64]), op=ALU.mult)
                for gp in range(0, G, 2):
                    bk = 5 + (gp // 2) % 2
                    for g in (gp, gp + 1):
                        O("pe", "matmul", [Bx[i]], [pbuf[bk]], bank(bk)[:, (g - gp) * 256:(g - gp + 1) * 256], Bb[i][:rows, g * 128:(g + 1) * 128],
                          xdte[i][:rows, g * 256:(g + 1) * 256], start=True, stop=True)
                    O("dve", "tensor_tensor", [BS, pbuf[bk]], [BS], out=S[:, gp * 256:(gp + 2) * 256], in0=S[:, gp * 256:(gp + 2) * 256],
                      in1=bank(bk), op=ALU.add)
                if own:
                    O("pool", "tensor_tensor", [Bld[i], Bk, Bx[i]], [Bld[i]], out=xa3[:rows], in0=xa3[:rows],
                      in1=dskb[:rows, :].unsqueeze(2).to_broadcast([rows, SH, 64]), op=ALU.mult)
                    O("dve", "tensor_tensor", [BY[i], Bld[i]], [BY[i]], out=Y[i][:rows, :], in0=Y[i][:rows, :], in1=xa[i][:rows, :], op=ALU.add)
                    O("act", "activation", [Bld[i]], [Bld[i]], out=za[i][:rows, :], in_=za[i][:rows, :], func=AF.Silu)
                    O("dve", "tensor_tensor", [BY[i], Bld[i]], [BY[i]], out=Y[i][:rows, :], in0=Y[i][:rows, :], in1=za[i][:rows, :], op=ALU.mult)
                    O("pool", "memset", [], [Bgst], gst, 0.0)
                    for g in range(G):
                        O("act", "activation", [BY[i]], [Bld[i], Bgst], out=xa[i][:rows, g * 256:(g + 1) * 256], in_=Y[i][:rows, g * 256:(g + 1) * 256],
                          func=AF.Square, accum_out=gst[:rows, g:g + 1])
                    O("dve", "tensor_scalar", [Bgst], [Bgst], out=gst[:rows, G:2 * G], in0=gst[:rows, 0:G], scalar1=1.0 / 256, scalar2=EPS,
                      op0=ALU.mult, op1=ALU.add)
                    O("act", "sqrt", [Bgst], [Bgst], out=gst[:rows, 2 * G:3 * G], in_=gst[:rows, G:2 * G])
                    O("dve", "reciprocal", [Bgst], [Bgst], out=gst[:rows, 3 * G:4 * G], in_=gst[:rows, 2 * G:3 * G])
                    O("dve", "tensor_tensor", [BY[i], Bgst], [BY[i]], out=Y[i].rearrange("p (g d) -> p g d", d=256)[:rows],
                      in0=Y[i].rearrange("p (g d) -> p g d", d=256)[:rows],
                      in1=gst[:rows, 3 * G:4 * G].unsqueeze(2).to_broadcast([rows, G, 256]), op=ALU.mult)
                    O("dve", "tensor_tensor", [BY[i], Bk], [Byo[i]], out=yo[i][:rows, :], in0=Y[i][:rows, :], in1=nwb[:rows, :], op=ALU.mult)
                    DMA("sp", mix_d[o0:o0 + rows, AW:AW + SI], yo[i][:rows, :], reads=[Byo[i]])

            def state_out(dst):
                for j in range(SI // 128):
                    O("pe", "transpose", [BS, Bc], [pbuf[7]], out=bank(7)[:, 0:128], in_=S[:, j * 128:(j + 1) * 128], identity=ident_f)
                    O("dve", "tensor_copy", [pbuf[7]], [Btr], out=tr, in_=bank(7)[:, 0:128])
                    DMA("sp", dst[j * 128:(j + 1) * 128, :], tr, reads=[Btr])

            n = 0
            for t in range(NTILE):
                ssd_tile(t, n)
                n += 1
            state_out(ssm_p)
            for j in range(SI // 128):
                DMA("sp", tr, sssm[j * 128:(j + 1) * 128, :], writes=[Btr])
                O("pe", "transpose", [Btr, Bc], [pbuf[7]], out=bank(7)[:, 0:128], in_=tr, identity=ident_f)
                O("dve", "tensor_copy", [pbuf[7], BSb], [BS], out=S[:, j * 128:(j + 1) * 128], in_=bank(7)[:, 0:128])
            ssd_tile(NTILE, n)
            state_out(ssm_s)
            P.barrier()
            top[0] = base_top

        if "C" in phases:
            lq = f32v(256)
            lt = f32v(8)
            Bl = Buf("lam")
            DMA("sp", lq, lam_in[0:1, :].partition_broadcast(128), writes=[Bl])
            O("pool", "memset", [], [Bl], lt, 0.0)
            ljunk = f32v(64)
            O("dve", "tensor_tensor", [Bl], [Bl], out=lq[:, 0:64], in0=lq[:, 0:64], in1=lq[:, 64:128], op=ALU.mult)
            O("dve", "tensor_tensor", [Bl], [Bl], out=lq[:, 128:192], in0=lq[:, 128:192], in1=lq[:, 192:256], op=ALU.mult)
            O("act", "activation", [Bl], [Bl], out=ljunk, in_=lq[:, 0:64], func=AF.Copy, accum_out=lt[:, 0:1])
            O("act", "activation", [Bl], [Bl], out=ljunk, in_=lq[:, 128:192], func=AF.Copy, accum_out=lt[:, 1:2])
            O("act", "activation", [Bl], [Bl], out=lt[:, 2:4], in_=lt[:, 0:2], func=AF.Exp)
            O("dve", "tensor_tensor", [Bl], [Bl], out=lt[:, 4:5], in0=lt[:, 2:3], in1=lt[:, 3:4], op=ALU.subtract)
            O("dve", "tensor_scalar", [Bl], [Bl], out=lt[:, 5:6], in0=lt[:, 4:5], scalar1=LAM0, scalar2=-1.0, op0=ALU.add, op1=ALU.mult)
            nlam = lt[:, 5:6]
            slw = f32v(128)
            DMA("sp", slw, subln_w[0:1, :].partition_broadcast(128), writes=[Bl])
            O("dve", "tensor_scalar", [Bl], [Bl], out=slw, in0=slw, scalar1=1.0 - LAM0, scalar2=None, op0=ALU.mult)
            c_top = top[0]

            def prep(ksrc, vsrc, qsrc, vlsrc, rows, kt, KTd, VAd, QTd, qcol, n):
                i = n % 2
                kf, vf, qf = pk[i], pvv[i], pq[i]
                DMA("sp", kf[:rows, :], ksrc, writes=[Bpk[i]])
                DMA("sp", vf[:rows, :], vsrc, writes=[Bpk[i]])
                kb_, vb_ = pkb[i], pvb[i]
                O("act", "copy", [Bpk[i]], [Bpb[i]], out=kb_[:rows, :], in_=kf[:rows, :])
                vb3 = vb_.rearrange("p (h d) -> p h d", d=129)
                O("dve", "tensor_copy", [Bpk[i]], [Bpb[i]], out=vb3[:rows, :, 0:128], in_=vf.rearrange("p (h d) -> p h d", d=128)[:rows])
                if vlsrc is None:
                    O("pool", "memset", [], [Bpb[i]], vb3[:rows, :, 128:129], 1.0)
                else:
                    DMA("sp", pvl[i][:rows, :], vlsrc, writes=[Bpk[i]])
                    O("pool", "tensor_copy", [Bpk[i]], [Bpb[i]], out=vb3[:rows, :, 128:129],
                      in_=pvl[i][:rows, 0:1].unsqueeze(1).to_broadcast([rows, H, 1]))
                DMA("sp", VAd.rearrange("h p (t d) -> p h t d", d=129)[:rows, :, kt, :], vb3[:rows], reads=[Bpb[i]])
                srcs = [(kb_, KTd, kt * 128)]
                if qsrc is not None:
                    DMA("sp", qf[:rows, :], qsrc, writes=[Bpk[i]])
                    O("act", "activation", [Bpk[i]], [Bpb[i]], out=pqb[i][:rows, :], in_=qf[:rows, :], func=AF.Copy, scale=0.125)
                    srcs.append((pqb[i], QTd, qcol))
                for sb_, dstd, col in srcs:
                    for g in range((H + 7) // 8):
                        hh = min(8, H - g * 8)
                        pv_ = bank(7).bitcast(BF16)
                        for j in range(hh):
                            h = g * 8 + j
                            O("pe", "transpose", [Bpb[i], Bc], [pbuf[7]], out=pv_[:, j * 128:j * 128 + rows], in_=sb_[:rows, h * 128:(h + 1) * 128],
                              identity=ident_b[:rows, :rows])
                        O("dve", "tensor_copy", [pbuf[7]], [Bpt], out=ptt[:, :hh * 128], in_=pv_[:, :hh * 128])
                        DMA("sp", dstd.rearrange("h p n -> p h n")[:, g * 8:g * 8 + hh, col:col + rows],
                            ptt.rearrange("p (h n) -> p h n", n=128)[:, :hh, :rows], reads=[Bpt])

            pk = [f32v(AW) for _ in range(2)]
            pvv = [f32v(AW) for _ in range(2)]
            pq = [f32v(AW) for _ in range(2)]
            pvl = [f32v(1) for _ in range(2)]
            Bpk = [Buf("pk") for _ in range(2)]
            pkb = [bf16v(AW) for _ in range(2)]
            pqb = [bf16v(AW) for _ in range(2)]
            pvb = [bf16v(H * 129) for _ in range(2)]
            Bpb = [Buf("pb") for _ in range(2)]
            ptt = bf16v(1024)
            Bpt = Buf("pt")
            n = 0
            for t in range(NTILE):
                own = t >= T0
                prep(pK_d[t * 128:(t + 1) * 128, :], pV_d[t * 128:(t + 1) * 128, :],
                     pQ_d[(t - T0) * 128:(t - T0 + 1) * 128, :] if own else None, valid[t * 128:(t + 1) * 128, :],
                     128, t, KT_d, VA_d, QT_d, (t - T0) * 128, n)
                n += 1
            for kt in range(NKS - 1):
                prep(ck[kt * 128:(kt + 1) * 128, :], cv[kt * 128:(kt + 1) * 128, :], None, None, 128, kt, KTs_d, VAs_d, None, 0, n)
                n += 1
            prep(pK_d[NTOK:NTOK + DEC, :], pV_d[NTOK:NTOK + DEC, :], pQ_d[TQ:TQ + DEC, :], None, DEC, NKS - 1, KTs_d, VAs_d, QTs_d, 0, n)
            P.barrier()
            top[0] = c_top

            def attn(KTd, VAd, QTd, nq, nkt, rows_last, causal, orow0, hn):
                QB = min(512, nq)
                SW = min(128, QB)
                sub = QB // SW
                i = hn % 2
                kts = (nkt - 1) * 128 + rows_last
                DMA("sp", KTb[i][:, :kts], KTd[:, :kts], writes=[BKT[i]])
                DMA("sp", VAb[i][:, :nkt * 129], VAd[:, :nkt * 129], writes=[BKT[i]])
                DMA("sp", QTb[i][:, :nq], QTd[:, :nq], writes=[BKT[i]])
                VA3 = VAb[i].rearrange("p (t d) -> p t d", d=129)
                def acc(m, si):
                    if si < 3:
                        return 4 + m, si * 129
                    return 6, m * 129

                pairs = []
                for qb in range(nq // QB):
                    kt_last = (T0 + qb * sub + sub - 1) if causal else nkt - 1
                    d0 = (T0 + qb * sub) if causal else nkt
                    for kt in range(kt_last + 1):
                        pairs.append((qb, kt, d0, kt == kt_last))

                def emit_scores(n):
                    qb, kt, d0, _ = pairs[n]
                    q0 = qb * QB
                    rk = rows_last if kt == nkt - 1 else 128
                    di = kt - d0 if kt >= d0 else -1
                    qlo = di * SW if di >= 0 else 0
                    ps = n % 3
                    st_ = n % 2
                    for m in range(2):
                        sbk = st_ * 2 + m
                        O("pe", "matmul", [BKT[i]], [pbuf[sbk]], bank(sbk)[:rk, qlo:QB], KTb[i][m * 64:(m + 1) * 64, kt * 128:kt * 128 + rk],
                          QTb[i][m * 64:(m + 1) * 64, q0 + qlo:q0 + QB], start=True, stop=True)
                    O("act", "activation", [pbuf[st_ * 2], pbuf[st_ * 2 + 1]], [BpT[ps][0], BpT[ps][1]],
                      out=pTT[ps].rearrange("p (m q) -> p m q", m=2)[:rk, :, qlo:QB],
                      in_=psum[:, st_ * 1024:(st_ + 1) * 1024].rearrange("p (m q) -> p m q", m=2)[:rk, :, qlo:QB], func=AF.Exp)
                    if di >= 0:
                        for m in range(2):
                            O("pool", "memset", [], [BpT[ps][m]], pT[ps][m][64:128, qlo:qlo + 64], 0.0)

                def emit_pv(n):
                    qb, kt, d0, last = pairs[n]
                    q0 = qb * QB
                    rk = rows_last if kt == nkt - 1 else 128
                    di = kt - d0 if kt >= d0 else -1
                    ps = n % 3
                    for si in range(max(di, 0), sub):
                        last_for_si = (d0 + si) if causal else nkt - 1
                        for m in range(2):
                            ab, ao = acc(m, si)
                            first_in_bank = (kt == 0) and ao == 0 and (ab != 6 or m == 0)
                            O("pe", "matmul", [BpT[ps][m], BKT[i]], [pbuf[ab]], bank(ab)[:SW, ao:ao + 129], pT[ps][m][:rk, si * SW:(si + 1) * SW],
                              VA3[:rk, kt, :], start=first_in_bank, stop=(kt == last_for_si), skip_group_check=True)
                    if not last:
                        return
                    for si in range(sub):
                        a1b, a1o = acc(0, si)
                        a2b, a2o = acc(1, si)
                        o1 = bank(a1b)[:SW, a1o:a1o + 129]
                        o2 = bank(a2b)[:SW, a2o:a2o + 129]
                        j = si % 2
                        s_ = fs[j]
                        O("dve", "reciprocal", [pbuf[a1b]], [Bfs[j]], out=s_[:SW, 0:1], in_=o1[:, 128:129])
                        O("dve", "reciprocal", [pbuf[a2b]], [Bfs[j]], out=s_[:SW, 1:2], in_=o2[:, 128:129])
                        O("dve", "tensor_tensor", [Bfs[j], Bl], [Bfs[j]], out=s_[:SW, 2:3], in0=s_[:SW, 1:2], in1=nlam[:SW, :], op=ALU.mult)
                        O("act", "activation", [pbuf[a1b], Bfs[j]], [Bfa[j]], out=fa[j][:SW, :], in_=o1[:, 0:128], func=AF.Copy, scale=s_[:SW, 0:1])
                        O("dve", "scalar_tensor_tensor", [pbuf[a2b], Bfs[j], Bfa[j]], [Bfa[j]], out=fa[j][:SW, :], in0=o2[:, 0:128],
                          scalar=s_[:SW, 2:3], in1=fa[j][:SW, :], op0=ALU.mult, op1=ALU.add)
                        O("pool", "memset", [], [Bfs[j]], s_[:, 3:4], 0.0)
                        O("act", "activation", [Bfa[j]], [Bfj, Bfs[j]], out=fj[:SW, :], in_=fa[j][:SW, :], func=AF.Square, accum_out=s_[:SW, 3:4])
                        O("dve", "tensor_scalar", [Bfs[j]], [Bfs[j]], out=s_[:SW, 4:5], in0=s_[:SW, 3:4], scalar1=1.0 / 128, scalar2=EPS,
                          op0=ALU.mult, op1=ALU.add)
                        O("act", "sqrt", [Bfs[j]], [Bfs[j]], out=s_[:SW, 5:6], in_=s_[:SW, 4:5])
                        O("dve", "reciprocal", [Bfs[j]], [Bfs[j]], out=s_[:SW, 6:7], in_=s_[:SW, 5:6])
                        O("dve", "scalar_tensor_tensor", [Bfa[j], Bfs[j], Bl], [Bfo[j]], out=fo[j][:SW, :], in0=fa[j][:SW, :], scalar=s_[:SW, 6:7],
                          in1=slw[:SW, :], op0=ALU.mult, op1=ALU.mult)
                        r0 = orow0 + q0 + si * SW
                        DMA("sp", mix_d[r0:r0 + SW, hcol[0]:hcol[0] + 128], fo[j][:SW, :], reads=[Bfo[j]])

                for n in range(len(pairs) + 1):
                    if n < len(pairs):
                        emit_scores(n)
                    if n >= 1:
                        emit_pv(n - 1)

            KTb = [bf16v(max(NTOK, NKS * 128)) for _ in range(2)]
            VAb = [bf16v(max(NTILE, NKS) * 129) for _ in range(2)]
            QTb = [bf16v(max(TQ, DEC)) for _ in range(2)]
            BKT = [Buf("KT") for _ in range(2)]
            pTT = [bf16v(1024) for _ in range(3)]
            pT = [[pTT[a][:, m * 512:(m + 1) * 512] for m in range(2)] for a in range(3)]
            BpT = [[Buf("pT") for _ in range(2)] for _ in range(3)]
            fs = [f32v(8) for _ in range(2)]
            Bfs = [Buf("fs") for _ in range(2)]
            fa = [f32v(128) for _ in range(2)]
            Bfa = [Buf("fa") for _ in range(2)]
            fj = f32v(128)
            Bfj = Buf("fj")
            fo = [bf16v(128) for _ in range(2)]
            Bfo = [Buf("fo") for _ in range(2)]
            hcol = [0]
            hn = 0
            for h in range(H):
                hcol[0] = h * 128
                attn(KT_d[h], VA_d[h], QT_d[h], TQ, NTILE, 128, True, 0, hn)
                hn += 1
                attn(KTs_d[h], VAs_d[h], QTs_d[h], DEC, NKS, DEC, False, TQ, hn)
                hn += 1
            P.barrier()
            top[0] = base_top

        if "E" in phases:
            def orow(ot):
                return ot * 128
            normT_pass(lambda ot: mix_d[ot * 128:ot * 128 + own_rows(ot), :], NT + 1, own_rows, D, None, mT_d, src_bf16=True)
            xr = [f32v(512) for _ in range(3)]
            Bxr = [Buf("xr") for _ in range(3)]

            def evacE(ot, rows, c0, cw, bks, n):
                i = n % 3
                src = xq[(T0 + ot) * 128:(T0 + ot) * 128 + rows, c0:c0 + cw] if ot < NT else xs[:, c0:c0 + cw]
                DMA("sp", xr[i][:rows, :cw], src, writes=[Bxr[i]])
                O("dve", "tensor_tensor", [pbuf[bks[0]], Bxr[i]], [Bxr[i]], out=xr[i][:rows, :cw], in0=bank(bks[0])[:rows, :cw],
                  in1=xr[i][:rows, :cw], op=ALU.add)
                DMA("pool", x1_d[ot * 128:ot * 128 + rows, c0:c0 + cw], xr[i][:rows, :cw], reads=[Bxr[i]])
            proj_pass(mT_d, KC, [w_out], D, 1024, lambda c0, cw: list(range(NT + 1)), own_rows, evacE, hq=("sp", "act"))
            top[0] = base_top
            normT_pass(lambda ot: x1_d[ot * 128:ot * 128 + own_rows(ot), :], NT + 1, own_rows, D, norm2_w[0:1, :], h2T_d)

        if "F" in phases:
            CWF = 512
            gs = [f32v(CWF) for _ in range(3)]
            go = [bf16v(CWF) for _ in range(3)]
            Bgs = [Buf("gs") for _ in range(3)]
            Bgo = [Buf("go") for _ in range(3)]

            def evacF1(ot, rows, c0, cw, bks, n):
                i = n % 3
                O("act", "activation", [pbuf[bks[0]]], [Bgs[i]], out=gs[i][:rows, :cw], in_=bank(bks[0])[:rows, :cw], func=AF.Silu)
                O("dve", "tensor_tensor", [Bgs[i], pbuf[bks[1]]], [Bgo[i]], out=go[i][:rows, :cw], in0=gs[i][:rows, :cw],
                  in1=bank(bks[1])[:rows, :cw], op=ALU.mult)
                DMA("pool", ff_d[ot * 128:ot * 128 + rows, c0:c0 + cw], go[i][:rows, :cw], reads=[Bgo[i]])
            proj_pass(h2T_d, KC, [w_gate, w_up], DFF, CWF, lambda c0, cw: list(range(NT + 1)), own_rows, evacF1)
            top[0] = base_top
            normT_pass(lambda ot: ff_d[ot * 128:ot * 128 + own_rows(ot), :], NT + 1, own_rows, DFF, None, ffT_d, src_bf16=True)
            xr = [f32v(CWF) for _ in range(3)]
            Bxr = [Buf("xr") for _ in range(3)]

            def evacF2(ot, rows, c0, cw, bks, n):
                i = n % 3
                DMA("sp", xr[i][:rows, :cw], x1_d[ot * 128:ot * 128 + rows, c0:c0 + cw], writes=[Bxr[i]])
                O("dve", "tensor_tensor", [pbuf[bks[0]], Bxr[i]], [Bxr[i]], out=xr[i][:rows, :cw], in0=bank(bks[0])[:rows, :cw],
                  in1=xr[i][:rows, :cw], op=ALU.add)
                DMA("pool", x2_d[ot * 128:ot * 128 + rows, c0:c0 + cw], xr[i][:rows, :cw], reads=[Bxr[i]])
            proj_pass(ffT_d, DFF // 128, [w_down], D, CWF, lambda c0, cw: list(range(NT + 1)), own_rows, evacF2, wbufs=1, hq=("sp", "act"))
            top[0] = base_top
            fw = f32v(D)
            Bfw = Buf("fw")
            DMA("sp", fw, final_norm_w[0:1, :].partition_broadcast(128), writes=[Bfw])
            xt = [f32v(D) for _ in range(2)]
            Bxt = [Buf("xt") for _ in range(2)]
            jk = bf16v(D)
            Bjk = Buf("jk")
            st = [f32v(4) for _ in range(2)]
            Bst = [Buf("st") for _ in range(2)]
            for ot in range(NT + 1):
                i = ot % 2
                rows = own_rows(ot)
                DMA("sp", xt[i][:rows, :], x2_d[ot * 128:ot * 128 + rows, :], writes=[Bxt[i]])
                O("pool", "memset", [], [Bst[i]], st[i], 0.0)
                O("act", "activation", [Bxt[i]], [Bjk, Bst[i]], out=jk[:rows, :], in_=xt[i][:rows, :], func=AF.Square, accum_out=st[i][:rows, 0:1])
                O("dve", "tensor_scalar", [Bst[i]], [Bst[i]], out=st[i][:rows, 1:2], in0=st[i][:rows, 0:1], scalar1=1.0 / D, scalar2=EPS,
                  op0=ALU.mult, op1=ALU.add)
                O("act", "sqrt", [Bst[i]], [Bst[i]], out=st[i][:rows, 3:4], in_=st[i][:rows, 1:2])
                O("dve", "reciprocal", [Bst[i]], [Bst[i]], out=st[i][:rows, 2:3], in_=st[i][:rows, 3:4])
                O("dve", "scalar_tensor_tensor", [Bxt[i], Bst[i], Bfw], [Bxt[i]], out=xt[i][:rows, :], in0=xt[i][:rows, :],
                  scalar=st[i][:rows, 2:3], in1=fw[:rows, :], op0=ALU.mult, op1=ALU.mult)
                dst = y_q[ot * 128:ot * 128 + rows, :] if ot < NT else y_s[:, :]
                DMA("sp", dst, xt[i][:rows, :], reads=[Bxt[i]])

        P.finish()
        P.emit()
    return nc


def host_inputs(cfg, inp):
    c = cfg
    f = lambda a: np.ascontiguousarray(np.asarray(a, dtype=np.float32))
    xp = f(inp["x_prompt"])
    B = xp.shape[0]
    ncores = B * NSLOT
    lam_in = np.concatenate([f(inp["lambda_q1"]), f(inp["lambda_k1"]), f(inp["lambda_q2"]), f(inp["lambda_k2"])], 0).reshape(1, 256)
    u = np.arange(128)
    consts = np.concatenate([np.eye(128), (u[:, None] <= u[None, :]), (u[:, None] > u[None, :]), np.ones((128, 128))], 1).astype(np.float32)
    shared = {
        "norm1_w": f(inp["norm1_w"]), "w_in": f(inp["w_in"])[0], "lam_in": lam_in,
        "subln_w": f(inp["subln_w"]), "conv_w": f(inp["conv_w"])[0], "conv_b": f(inp["conv_b"]),
        "dt_bias": f(inp["dt_bias"]), "A_log": f(inp["A_log"]), "D_skip": f(inp["D_skip"]),
        "ssd_norm_w": f(inp["ssd_norm_w"]), "w_out": f(inp["w_out"])[0], "norm2_w": f(inp["norm2_w"]),
        "w_gate": f(inp["w_gate"])[0], "w_up": f(inp["w_up"])[0], "w_down": f(inp["w_down"])[0],
        "final_norm_w": f(inp["final_norm_w"]).reshape(1, -1),
        "consts": consts,
    }
    maps = []
    for core in range(ncores):
        b, j = core // NSLOT, core % NSLOT
        xq = np.zeros((NSLOT * c.TQ, c.D), np.float32)
        valid = np.zeros((NSLOT * c.TQ, 1), np.float32)
        n = (j + 1) * c.TQ
        xq[NSLOT * c.TQ - n:] = xp[b, :n]
        valid[NSLOT * c.TQ - n:] = 1.0
        m = dict(shared)
        m.update({
            "xq": xq, "valid": valid, "xs": f(inp["x_sample"])[core],
            "ck": f(inp["cache_k"])[0, core].reshape(c.PAST, c.AW),
            "cv": f(inp["cache_v"])[0, core].reshape(c.PAST, c.AW),
            "sconv": f(inp["state_conv"])[0, core],
            "sssm": f(inp["state_ssm"])[0, core].reshape(c.SH * 64, 128),
        })
        maps.append(m)
    return maps


def assemble(cfg, res, B):
    c = cfg
    ncores = B * NSLOT
    g = lambda name: [np.asarray(res[i][name], dtype=np.float32) for i in range(ncores)]
    yq, ys, kq, vq, cp, sp_, ks, vs, cs, ss = (g(n) for n in
        ["y_q", "y_s", "k_q", "v_q", "conv_p", "ssm_p", "k_s", "v_s", "conv_s", "ssm_s"])
    cat = lambda lst, b: np.concatenate(lst[b * NSLOT:(b + 1) * NSLOT], 0)
    y_prompt = np.stack([cat(yq, b) for b in range(B)])
    k_prompt = np.stack([cat(kq, b) for b in range(B)]).reshape(1, B, c.SEQ, c.H, 128)
    v_prompt = np.stack([cat(vq, b) for b in range(B)]).reshape(1, B, c.SEQ, c.H, 128)
    conv_prompt = np.stack([cp[b * NSLOT + NSLOT - 1] for b in range(B)])[None]
    ssm_prompt = np.stack([sp_[b * NSLOT + NSLOT - 1] for b in range(B)]).reshape(1, B, c.SH, 64, 128)
    y_sample = np.stack(ys)
    k_sample = np.stack(ks).reshape(1, ncores, c.DEC, c.H, 128)
    v_sample = np.stack(vs).reshape(1, ncores, c.DEC, c.H, 128)
    conv_sample = np.stack(cs)[None]
    ssm_sample = np.stack(ss).reshape(1, ncores, c.SH, 64, 128)
    return (y_prompt, y_sample, k_prompt, v_prompt, conv_prompt, ssm_prompt,
            k_sample, v_sample, conv_sample, ssm_sample)


def run(cfg, inp, phases="ABXDCEF", dbg=()):
    B = np.asarray(inp["x_prompt"]).shape[0]
    maps = host_inputs(cfg, inp)
    nc = build(cfg, phases, dbg)
    res = run_bass_kernel_spmd(nc, maps, core_ids=list(range(len(maps))))
    if dbg:
        return assemble(cfg, res.results, B), res.results
    return assemble(cfg, res.results, B)


def kernel(**inputs):
    return run(Cfg(), inputs)
```

```python
import contextlib
import numpy as np
import concourse.bass as bass
import concourse.mybir as mybir
from concourse.bass_utils import run_bass_kernel_spmd

F32 = mybir.dt.float32
BF16 = mybir.dt.bfloat16
AF = mybir.ActivationFunctionType
ALU = mybir.AluOpType
EPS = 1e-6
NSLOT = 4
KDMA = 8


class Cfg:
    def __init__(self, D=4096, SEQ=8192, DFF=11008, PAST=2048, DEC=16, G=8):
        self.D, self.SEQ, self.DFF, self.PAST, self.DEC, self.G = D, SEQ, DFF, PAST, DEC, G
        self.AW = D // 2
        self.H = self.AW // 128
        self.SI = D - self.AW
        self.SH = self.SI // 64
        self.CD = self.SI + 2 * G * 128
        self.IN = 3 * self.AW + self.SI + self.CD + self.SH
        self.TQ = SEQ // NSLOT
        self.NT = self.TQ // 128
        self.KC = D // 128
        self.oQ, self.oK, self.oV, self.oZ = 0, self.AW, 2 * self.AW, 3 * self.AW
        self.oX = 3 * self.AW + self.SI
        self.oDT = self.oX + self.CD


class Buf:
    def __init__(self, name):
        self.name, self.w, self.r = name, None, []


class Prog:
    ENG = ["pe", "act", "dve", "pool", "sp"]

    def __init__(self, nc, stack):
        self.nc = nc
        self.ops = {e: [] for e in self.ENG}
        self.esem = {e: stack.enter_context(nc.semaphore("s_" + e)) for e in ["pe", "act", "dve", "pool"]}
        self.cnt = {e: 0 for e in self.esem}
        self.dsem = {q: [stack.enter_context(nc.semaphore("d_%s%d" % (q, i))) for i in range(KDMA)]
                     for q in ["sp", "pool", "act"]}
        self.dn = {"sp": 0, "pool": 0, "act": 0}
        self.seen = {e: {} for e in self.ENG}
        self.pend = {e: [] for e in self.ENG}

    def op(self, eng, fn, reads=(), writes=(), dma=False):
        waits = list(self.pend[eng])
        self.pend[eng] = []
        for b in reads:
            if b.w:
                waits.append(b.w)
        for b in writes:
            if b.w:
                waits.append(b.w)
            waits.extend(b.r)
        if dma:
            m = self.dn[eng]
            sem = self.dsem[eng][m % KDMA]
            prev = 16 * (m // KDMA)
            if prev:
                waits.append((sem, prev))
            ev = (sem, prev + 16)
            inc = 16
            self.dn[eng] += 1
        else:
            self.cnt[eng] += 1
            ev = (self.esem[eng], self.cnt[eng])
            inc = 1
        need = {}
        for s, v in waits:
            if eng == "pe" and s is self.esem["pe"]:
                continue
            if self.seen[eng].get(id(s), 0) >= v:
                continue
            if need.get(id(s), (s, 0))[1] < v:
                need[id(s)] = (s, v)
        for k, (s, v) in need.items():
            self.seen[eng][k] = v
        self.ops[eng].append((list(need.values()), fn, ev[0], inc))
        for b in reads:
            b.r.append(ev)
        for b in writes:
            b.w, b.r = ev, []
        return ev

    def all_events(self):
        evs = [(self.esem[e], self.cnt[e]) for e in self.esem if self.cnt[e]]
        for q in self.dsem:
            m = self.dn[q]
            for i in range(KDMA):
                n = (m - i + KDMA - 1) // KDMA if m > i else 0
                if n:
                    evs.append((self.dsem[q][i], 16 * n))
        return evs

    def barrier(self):
        evs = self.all_events()
        for e in self.ENG:
            self.pend[e] = list(evs)

    def finish(self):
        self.barrier()
        for e in self.ENG:
            need = {}
            for s, v in self.pend[e]:
                if self.seen[e].get(id(s), 0) < v:
                    need[id(s)] = (s, v)
            self.ops[e].append((list(need.values()), None, None, 0))

    def emit(self):
        nc = self.nc
        names = {"pe": "tensor", "act": "scalar", "dve": "vector", "pool": "gpsimd", "sp": "sync"}
        with nc.Block() as block:
            for e in self.ENG:
                def body(engobj, e=e):
                    for need, fn, sem, inc in self.ops[e]:
                        for s, v in need:
                            engobj.wait_ge(s, v)
                        if fn is not None:
                            fn(engobj).then_inc(sem, inc)
                getattr(block, names[e])(body)


def build(cfg, phases="ABXDCEF", dbg=()):
    c = cfg
    nc = bass.Bass("TRN2", target_bir_lowering=False)
    D, KC, IN, H, AW, SI, SH, G, CD, DFF = c.D, c.KC, c.IN, c.H, c.AW, c.SI, c.SH, c.G, c.CD, c.DFF
    TQ, NT, DEC, PAST = c.TQ, c.NT, c.DEC, c.PAST
    NTOK = NSLOT * TQ
    NTILE = NSLOT * NT
    T0 = NTILE - NT
    GB = G * 128
    NKS = PAST // 128 + 1
    LAM0 = 0.2

    def din(name, shape, dt=F32):
        return nc.dram_tensor(name, list(shape), dt, kind="ExternalInput").ap()

    def dout(name, shape, dt=F32):
        return nc.dram_tensor(name, list(shape), dt, kind="ExternalOutput").ap()

    def dscr(name, shape, dt):
        kind = "ExternalOutput" if name in dbg else "Internal"
        return nc.dram_tensor(name, list(shape), dt, kind=kind).ap()

    xq = din("xq", [NTOK, D])
    valid = din("valid", [NTOK, 1])
    xs = din("xs", [DEC, D])
    ck = din("ck", [PAST, AW])
    cv = din("cv", [PAST, AW])
    sconv = din("sconv", [3, CD])
    sssm = din("sssm", [SH * 64, 128])
    norm1_w = din("norm1_w", [1, D])
    w_in = din("w_in", [D, IN])
    lam_in = din("lam_in", [1, 256])
    subln_w = din("subln_w", [1, 128])
    conv_w = din("conv_w", [4, CD])
    conv_b = din("conv_b", [1, CD])
    dt_bias = din("dt_bias", [1, SH])
    A_log = din("A_log", [1, SH])
    D_skip = din("D_skip", [1, SH])
    ssd_norm_w = din("ssd_norm_w", [1, SI])
    w_out = din("w_out", [D, D])
    norm2_w = din("norm2_w", [1, D])
    w_gate = din("w_gate", [D, DFF])
    w_up = din("w_up", [D, DFF])
    w_down = din("w_down", [DFF, D])
    final_norm_w = din("final_norm_w", [1, D])
    consts_in = din("consts", [128, 4 * 128])

    y_q = dout("y_q", [TQ, D])
    y_s = dout("y_s", [DEC, D])
    k_q = dout("k_q", [TQ, AW])
    v_q = dout("v_q", [TQ, AW])
    conv_p = dout("conv_p", [3, CD])
    ssm_p = dout("ssm_p", [SH * 64, 128])
    k_s = dout("k_s", [DEC, AW])
    v_s = dout("v_s", [DEC, AW])
    conv_s = dout("conv_s", [3, CD])
    ssm_s = dout("ssm_s", [SH * 64, 128])

    hT_d = dscr("hT_d", [NTILE + 1, 128, KC * 128], BF16)
    pQ_d = dscr("pQ_d", [TQ + DEC, AW], F32)
    pK_d = dscr("pK_d", [NTOK + DEC, AW], F32)
    pV_d = dscr("pV_d", [NTOK + DEC, AW], F32)
    pZ_d = dscr("pZ_d", [TQ + DEC, SI], F32)
    pX_d = dscr("pX_d", [NTOK + DEC + 6, CD], F32)
    pDT_d = dscr("pDT_d", [NTOK + DEC, SH], F32)
    act_d = dscr("act_d", [NTOK + DEC, CD], F32)
    KT_d = dscr("KT_d", [H, 128, NTOK], BF16)
    QT_d = dscr("QT_d", [H, 128, TQ], BF16)
    VA_d = dscr("VA_d", [H, 128, NTILE * 129], BF16)
    KTs_d = dscr("KTs_d", [H, 128, NKS * 128], BF16)
    QTs_d = dscr("QTs_d", [H, 128, DEC], BF16)
    VAs_d = dscr("VAs_d", [H, 128, NKS * 129], BF16)
    mix_d = dscr("mix_d", [TQ + DEC, D], BF16)
    mT_d = dscr("mT_d", [NT + 1, 128, KC * 128], BF16)
    x1_d = dscr("x1_d", [TQ + DEC, D], F32)
    h2T_d = dscr("h2T_d", [NT + 1, 128, KC * 128], BF16)
    ff_d = dscr("ff_d", [TQ + DEC, DFF], BF16)
    ffT_d = dscr("ffT_d", [NT + 1, 128, DFF], BF16)
    x2_d = dscr("x2_d", [TQ + DEC, D], F32)

    with contextlib.ExitStack() as stack:
        P = Prog(nc, stack)
        ARENA = 44 * 1024
        arena = stack.enter_context(nc.sbuf_tensor("arena", [128, ARENA], F32))
        psum = stack.enter_context(nc.psum_tensor("psum", [128, 8 * 512], F32))
        top = [0]

        def f32v(words):
            o = top[0]
            top[0] += words
            assert top[0] <= ARENA, "SBUF arena overflow %d" % top[0]
            return arena[:, o:o + words]

        def bf16v(elems):
            return f32v((elems + 1) // 2).bitcast(BF16)[:, :elems]

        def bank(i):
            return psum[:, i * 512:(i + 1) * 512]

        pbuf = [Buf("ps%d" % i) for i in range(8)]

        def O(eng, method, reads, writes, *a, **kw):
            return P.op(eng, lambda e: getattr(e, method)(*a, **kw), reads, writes)

        def DMA(q, out, in_, reads=(), writes=()):
            return P.op(q, lambda e: e.dma_start(out=out, in_=in_), reads, writes, dma=True)

        def COPY(eng, out, in_, reads, writes):
            if eng == "act":
                return O("act", "copy", reads, writes, out=out, in_=in_)
            return O(eng, "tensor_copy", reads, writes, out=out, in_=in_)

        cst = f32v(512)
        Bc = Buf("consts")
        DMA("sp", cst, consts_in[:, :], writes=[Bc])
        ident_f, tri_le, tri_gt, ones_f = (cst[:, i * 128:(i + 1) * 128] for i in range(4))
        ident_b = bf16v(128)
        O("dve", "tensor_copy", [Bc], [Bc], out=ident_b, in_=ident_f)
        base_top = top[0]

        def own_rows(ot):
            return 128 if ot < NT else DEC

        def normT_pass(src_of, ntiles, rows_of, ncols, wrow, dstT, src_bf16=False):
            mark = top[0]
            nch = ncols // 128
            grp = 8 if nch % 8 == 0 else (4 if nch % 4 == 0 else (2 if nch % 2 == 0 else 1))
            Bw = Buf("nw")
            if wrow is not None:
                wb = f32v(ncols)
                DMA("sp", wb, wrow.partition_broadcast(128), writes=[Bw])
            xt = [None if src_bf16 else f32v(ncols) for _ in range(2)]
            Bxt = [Buf("xt") for _ in range(2)]
            junk = None if src_bf16 else bf16v(ncols)
            Bj = Buf("junk")
            xn = [bf16v(ncols) for _ in range(2)]
            Bxn = [Buf("xn") for _ in range(2)]
            st = [f32v(4) for _ in range(2)]
            Bst = [Buf("st") for _ in range(2)]
            hs = [bf16v(ncols) for _ in range(2)]
            Bhs = [Buf("hs") for _ in range(2)]
            for t in range(ntiles):
                i = t % 2
                rows = rows_of(t)
                if src_bf16:
                    DMA("sp", xn[i][:rows, :], src_of(t), writes=[Bxn[i]])
                else:
                    DMA("sp", xt[i][:rows, :], src_of(t), writes=[Bxt[i]])
                    if wrow is not None:
                        O("pool", "memset", [], [Bst[i]], st[i], 0.0)
                        O("act", "activation", [Bxt[i]], [Bj, Bst[i]], out=junk[:rows, :], in_=xt[i][:rows, :],
                          func=AF.Square, accum_out=st[i][:rows, 0:1])
                        O("dve", "tensor_scalar", [Bst[i]], [Bst[i]], out=st[i][:rows, 1:2], in0=st[i][:rows, 0:1],
                          scalar1=1.0 / ncols, scalar2=EPS, op0=ALU.mult, op1=ALU.add)
                        O("act", "sqrt", [Bst[i]], [Bst[i]], out=st[i][:rows, 3:4], in_=st[i][:rows, 1:2])
                        O("dve", "reciprocal", [Bst[i]], [Bst[i]], out=st[i][:rows, 2:3], in_=st[i][:rows, 3:4])
                        O("dve", "scalar_tensor_tensor", [Bxt[i], Bst[i], Bw], [Bxn[i]], out=xn[i][:rows, :],
                          in0=xt[i][:rows, :], scalar=st[i][:rows, 2:3], in1=wb[:rows, :], op0=ALU.mult, op1=ALU.mult)
                    else:
                        O("dve", "tensor_copy", [Bxt[i]], [Bxn[i]], out=xn[i][:rows, :], in_=xt[i][:rows, :])
                for g in range(nch // grp):
                    bk = g % 2
                    pv = bank(bk).bitcast(BF16)
                    for kk in range(grp):
                        k = g * grp + kk
                        O("pe", "transpose", [Bxn[i], Bc], [pbuf[bk]], out=pv[:, kk * 128:kk * 128 + rows],
                          in_=xn[i][:rows, k * 128:(k + 1) * 128], identity=ident_b[:rows, :rows])
                    COPY("act" if g % 2 == 0 else "dve", hs[i][:, g * grp * 128:(g + 1) * grp * 128],
                         pv[:, :grp * 128], [pbuf[bk]], [Bhs[i]])
                DMA("pool", dstT[t, :, :], hs[i], reads=[Bhs[i]])
            P.barrier()
            top[0] = mark

        def proj_pass(srcT, KCH, weights, NCOL, CW, tiles_of_block, rows_of, evac, wbufs=2, hbufs=3, hq=("sp",)):
            mark = top[0]
            nw = len(weights)
            SUB = min(512, CW)
            wv = [[bf16v(KCH * CW) for _ in range(wbufs)] for _ in range(nw)]
            Bwv = [[Buf("wv") for _ in range(wbufs)] for _ in range(nw)]
            hb = [bf16v(KCH * 128) for _ in range(hbufs)]
            Bhb = [Buf("hb") for _ in range(hbufs)]
            nblk = (NCOL + CW - 1) // CW
            n_h = 0
            n_b = 0
            n_e = 0
            for cb in range(nblk):
                c0 = cb * CW
                cw = min(CW, NCOL - c0)
                wi = cb % wbufs
                wviews = []
                for j, w in enumerate(weights):
                    wview = wv[j][wi].rearrange("p (k c) -> p k c", k=KCH)
                    DMA("pool", wview[:, :, :cw], w.rearrange("(k p) c -> p k c", p=128)[:, :, c0:c0 + cw],
                        writes=[Bwv[j][wi]])
                    wviews.append(wview)
                for t in tiles_of_block(c0, cw):
                    rows = rows_of(t)
                    hi = n_h % hbufs
                    hview = hb[hi].rearrange("p (k n) -> p k n", k=KCH)
                    DMA(hq[n_h % len(hq)], hb[hi], srcT[t, :, :], writes=[Bhb[hi]])
                    n_h += 1
                    for s0 in range(0, cw, SUB):
                        sw = min(SUB, cw - s0)
                        bks = []
                        for j in range(nw):
                            bk = 2 + n_b % 6
                            n_b += 1
                            bks.append(bk)
                            for k in range(KCH):
                                O("pe", "matmul", [Bhb[hi], Bwv[j][wi]], [pbuf[bk]], bank(bk)[:rows, :sw],
                                  hview[:, k, :rows], wviews[j][:, k, s0:s0 + sw], start=(k == 0), stop=(k == KCH - 1))
                        n_e += 1
                        evac(t, rows, c0 + s0, sw, bks, n_e)
            P.barrier()
            top[0] = mark

        if "A" in phases:
            normT_pass(lambda t: xq[t * 128:(t + 1) * 128, :] if t < NTILE else xs[:, :], NTILE + 1,
                       lambda t: 128 if t < NTILE else DEC, D, norm1_w[0:1, :], hT_d)

        if "B" in phases:
            segs = [("Q", 0, AW, pQ_d, True), ("K", c.oK, AW, pK_d, False), ("V", c.oV, AW, pV_d, False),
                    ("Z", c.oZ, SI, pZ_d, True), ("X", c.oX, CD, pX_d, False), ("DT", c.oDT, SH, pDT_d, False)]
            for name, s0, sw, dst, own_only in segs:
                ob = [f32v(512) for _ in range(3)]
                Bob = [Buf("ob") for _ in range(3)]

                def evacB(t, rows, c0, cw, bks, n, dst=dst, own_only=own_only, name=name, ob=ob, Bob=Bob):
                    i = n % 3
                    COPY("dve", ob[i][:rows, :cw], bank(bks[0])[:rows, :cw], [pbuf[bks[0]]], [Bob[i]])
                    if own_only:
                        r0 = (t - T0) * 128
                    elif name == "X":
                        r0 = 3 + t * 128 if t < NTILE else NTOK + 6
                    else:
                        r0 = t * 128
                    DMA("pool", dst[r0:r0 + rows, c0:c0 + cw], ob[i][:rows, :cw], reads=[Bob[i]])

                tiles = list(range(T0, NTILE + 1)) if own_only else list(range(NTILE + 1))
                proj_pass(hT_d, KC, [w_in[:, s0:s0 + sw]], sw, 1024, lambda c0, cw, tiles=tiles: tiles,
                          lambda t: 128 if t < NTILE else DEC, evacB, hq=("sp", "act"))
                top[0] = base_top
            DMA("sp", k_q[:, :], pK_d[T0 * 128:NTOK, :])
            DMA("sp", v_q[:, :], pV_d[T0 * 128:NTOK, :])
            DMA("sp", conv_p[:, :], pX_d[3 + NTOK - 3:3 + NTOK, :])
            DMA("sp", k_s[:, :], pK_d[NTOK:NTOK + DEC, :])
            DMA("sp", v_s[:, :], pV_d[NTOK:NTOK + DEC, :])
            DMA("sp", conv_s[:, :], pX_d[NTOK + 6 + DEC - 3:NTOK + 6 + DEC, :])
            zt = f32v(CD)
            Bz = Buf("z")
            O("pool", "memset", [], [Bz], zt[:3, :], 0.0)
            DMA("sp", pX_d[0:3, :], zt[:3, :], reads=[Bz])
            DMA("sp", pX_d[NTOK + 3:NTOK + 6, :], sconv[:, :])
            P.barrier()
            top[0] = base_top

        if "X" in phases:
            CC = min(1024, CD)
            for ch0 in range(0, CD, CC):
                cwt = [f32v(CC) for _ in range(5)]
                Bcw = Buf("cw")
                for i in range(4):
                    DMA("sp", cwt[i], conv_w[i:i + 1, ch0:ch0 + CC].partition_broadcast(128), writes=[Bcw])
                DMA("sp", cwt[4], conv_b[0:1, ch0:ch0 + CC].partition_broadcast(128), writes=[Bcw])
                NS = 3
                win = [[f32v(CC) for _ in range(4)] for _ in range(NS)]
                Bwin = [[Buf("win") for _ in range(4)] for _ in range(NS)]
                only_own = ch0 >= SI + GB
                nx = 0
                for t in range(NTILE + 1):
                    if only_own and t < T0:
                        continue
                    rows = 128 if t < NTILE else DEC
                    r0 = 3 + t * 128 if t < NTILE else NTOK + 6
                    g0 = t * 128 if t < NTILE else NTOK
                    s = nx % NS
                    nx += 1
                    W_, B_ = win[s], Bwin[s]
                    for i in range(4):
                        DMA(("sp", "act", "sp", "pool")[i], W_[i][:rows, :], pX_d[r0 - 3 + i:r0 - 3 + i + rows, ch0:ch0 + CC], writes=[B_[i]])

                    def TT(eng, a, b, bb, op):
                        O(eng, "tensor_tensor", [B_[a], bb], [B_[a]], out=W_[a][:rows, :], in0=W_[a][:rows, :], in1=b[:rows, :], op=op)
                    TT("pool", 0, cwt[0], Bcw, ALU.mult)
                    TT("pool", 1, cwt[1], Bcw, ALU.mult)
                    TT("dve", 2, cwt[2], Bcw, ALU.mult)
                    TT("dve", 3, cwt[3], Bcw, ALU.mult)
                    TT("pool", 0, W_[1], B_[1], ALU.add)
                    TT("dve", 2, W_[3], B_[3], ALU.add)
                    TT("dve", 2, cwt[4], Bcw, ALU.add)
                    TT("dve", 0, W_[2], B_[2], ALU.add)
                    O("act", "activation", [B_[0]], [B_[1]], out=W_[1][:rows, :], in_=W_[0][:rows, :], func=AF.Silu)
                    DMA("pool", act_d[g0:g0 + rows, ch0:ch0 + CC], W_[1][:rows, :], reads=[B_[1]])
                P.barrier()
                top[0] = base_top

        if "D" in phases:
            dtb, albc, dskb = f32v(SH), f32v(SH), f32v(SH)
            nwb = f32v(SI)
            Bk = Buf("ssdconst")
            DMA("sp", dtb, dt_bias[0:1, :].partition_broadcast(128), writes=[Bk])
            DMA("sp", albc, A_log[0:1, :].partition_broadcast(128), writes=[Bk])
            DMA("sp", dskb, D_skip[0:1, :].partition_broadcast(128), writes=[Bk])
            DMA("sp", nwb, ssd_norm_w[0:1, :].partition_broadcast(128), writes=[Bk])
            Abc = f32v(SH)
            O("act", "activation", [Bk], [Bk], out=Abc, in_=albc, func=AF.Exp)
            O("dve", "tensor_scalar", [Bk], [Bk], out=Abc, in0=Abc, scalar1=-1.0, scalar2=None, op0=ALU.mult)
            S = f32v(SI)
            BS = Buf("S")
            O("pool", "memset", [], [BS], S, 0.0)
            Sb = bf16v(SI)
            BSb = Buf("Sb")
            NB = 2
            xa = [f32v(SI) for _ in range(NB)]
            Ba = [f32v(GB) for _ in range(NB)]
            Ca = [f32v(GB) for _ in range(NB)]
            za = [f32v(SI) for _ in range(NB)]
            dtr = [f32v(SH) for _ in range(NB)]
            vl = [f32v(1) for _ in range(NB)]
            Bld = [Buf("ld") for _ in range(NB)]
            sm = [f32v(8 * SH) for _ in range(NB)]
            Bsm = [Buf("sm") for _ in range(NB)]
            xdt = [bf16v(SI) for _ in range(NB)]
            xdte = [bf16v(SI) for _ in range(NB)]
            Bb = [bf16v(GB) for _ in range(NB)]
            Cb = [bf16v(GB) for _ in range(NB)]
            Bx = [Buf("xd") for _ in range(NB)]
            Y = [f32v(SI) for _ in range(NB)]
            BY = [Buf("Y") for _ in range(NB)]
            yo = [bf16v(SI) for _ in range(NB)]
            Byo = [Buf("yo") for _ in range(NB)]
            BT, CT = bf16v(128), bf16v(128)
            BBT = Buf("BT")
            cbm = f32v(128)
            Bcbm = Buf("cbm")
            Lm = f32v(4 * 128)
            BL = Buf("L")
            Em = f32v(4 * 128)
            BE = Buf("E")
            Mm = bf16v(4 * 128)
            BM = Buf("M")
            tmpg = f32v(256)
            Btg = Buf("tg")
            gst = f32v(4 * G)
            Bgst = Buf("gst")
            tr = f32v(128)
            Btr = Buf("tr")

            def ssd_tile(t, n):
                i = n % NB
                own = t >= T0
                samp = t == NTILE
                rows = DEC if samp else 128
                g0 = NTOK if samp else t * 128
                o0 = (t - T0) * 128
                DMA("sp", xa[i][:rows, :], act_d[g0:g0 + rows, 0:SI], writes=[Bld[i]])
                DMA("sp", Ba[i][:rows, :], act_d[g0:g0 + rows, SI:SI + GB], writes=[Bld[i]])
                DMA("sp", dtr[i][:rows, :], pDT_d[g0:g0 + rows, :], writes=[Bld[i]])
                if samp:
                    O("pool", "memset", [], [Bld[i]], vl[i], 1.0)
                else:
                    DMA("sp", vl[i][:rows, :], valid[g0:g0 + rows, :], writes=[Bld[i]])
                if own:
                    DMA("sp", Ca[i][:rows, :], act_d[g0:g0 + rows, SI + GB:SI + 2 * GB], writes=[Bld[i]])
                    DMA("sp", za[i][:rows, :], pZ_d[o0:o0 + rows, :], writes=[Bld[i]])
                m = sm[i]
                dt_, dA_, w2_, eacs_, dte_, cdec_, tmp_ = (m[:, j * SH:(j + 1) * SH] for j in range(7))
                O("dve", "tensor_tensor", [Bld[i], Bk], [Bsm[i]], out=tmp_[:rows, :], in0=dtr[i][:rows, :], in1=dtb[:rows, :], op=ALU.add)
                O("act", "activation", [Bsm[i]], [Bsm[i]], out=tmp_[:rows, :], in_=tmp_[:rows, :], func=AF.Exp)
                O("act", "activation", [Bsm[i]], [Bsm[i]], out=tmp_[:rows, :], in_=tmp_[:rows, :], func=AF.Ln, bias=1.0)
                O("dve", "tensor_scalar", [Bsm[i], Bld[i]], [Bsm[i]], out=dt_[:rows, :], in0=tmp_[:rows, :],
                  scalar1=vl[i][:rows, 0:1], scalar2=None, op0=ALU.mult)
                O("dve", "tensor_tensor", [Bsm[i], Bk], [Bsm[i]], out=dA_[:rows, :], in0=dt_[:rows, :], in1=Abc[:rows, :], op=ALU.mult)
                pb = bank(0)
                O("pe", "matmul", [Bsm[i], Bc], [pbuf[0]], pb[:rows, 0:SH], tri_le[:rows, :rows], dA_[:rows, :], start=True, stop=True)
                O("pe", "matmul", [Bsm[i], Bc], [pbuf[0]], pb[:rows, SH:2 * SH], tri_gt[:rows, :rows], dA_[:rows, :], start=True, stop=True)
                O("pe", "matmul", [Bsm[i], Bc], [pbuf[0]], pb[:, 2 * SH:3 * SH], ones_f[:rows, :], dA_[:rows, :], start=True, stop=True)
                O("act", "activation", [pbuf[0]], [Bsm[i]], out=eacs_[:rows, :], in_=pb[:rows, 0:SH], func=AF.Exp)
                O("act", "activation", [pbuf[0]], [Bsm[i]], out=dte_[:rows, :], in_=pb[:rows, SH:2 * SH], func=AF.Exp)
                O("act", "activation", [pbuf[0]], [Bsm[i]], out=cdec_, in_=pb[:, 2 * SH:3 * SH], func=AF.Exp)
                O("dve", "tensor_tensor", [Bsm[i]], [Bsm[i]], out=w2_[:rows, :], in0=dt_[:rows, :], in1=dte_[:rows, :], op=ALU.mult)
                xa3 = xa[i].rearrange("p (h d) -> p h d", d=64)
                O("dve", "tensor_tensor", [Bld[i], Bsm[i]], [Bx[i]], out=xdt[i].rearrange("p (h d) -> p h d", d=64)[:rows],
                  in0=xa3[:rows], in1=dt_[:rows, :].unsqueeze(2).to_broadcast([rows, SH, 64]), op=ALU.mult)
                O("pool", "tensor_tensor", [Bld[i], Bsm[i]], [Bx[i]], out=xdte[i].rearrange("p (h d) -> p h d", d=64)[:rows],
                  in0=xa3[:rows], in1=w2_[:rows, :].unsqueeze(2).to_broadcast([rows, SH, 64]), op=ALU.mult)
                O("act", "copy", [Bld[i]], [Bx[i]], out=Bb[i][:rows, :], in_=Ba[i][:rows, :])
                if own:
                    O("act", "copy", [Bld[i]], [Bx[i]], out=Cb[i][:rows, :], in_=Ca[i][:rows, :])
                    O("pool", "tensor_copy", [BS], [BSb], out=Sb, in_=S)
                    for g in range(G):
                        pv = bank(1).bitcast(BF16)
                        O("pe", "transpose", [Bx[i], Bc], [pbuf[1]], out=pv[:, 0:rows], in_=Bb[i][:rows, g * 128:(g + 1) * 128], identity=ident_b[:rows, :rows])
                        O("pe", "transpose", [Bx[i], Bc], [pbuf[1]], out=pv[:, 128:128 + rows], in_=Cb[i][:rows, g * 128:(g + 1) * 128], identity=ident_b[:rows, :rows])
                        O("dve", "tensor_copy", [pbuf[1]], [BBT], out=BT[:, :rows], in_=pv[:, 0:rows])
                        O("dve", "tensor_copy", [pbuf[1]], [BBT], out=CT[:, :rows], in_=pv[:, 128:128 + rows])
                        O("pe", "matmul", [BBT], [pbuf[2]], bank(2)[:rows, :rows], BT[:, :rows], CT[:, :rows], start=True, stop=True)
                        O("dve", "tensor_tensor", [pbuf[2], Bc], [Bcbm], out=cbm[:rows, :rows], in0=bank(2)[:rows, :rows], in1=tri_le[:rows, :rows], op=ALU.mult)
                        for r in range(4):
                            h = 4 * g + r
                            O("pool" if r % 2 else "dve", "tensor_scalar", [Bc, Bsm[i]], [BL], out=Lm[:rows, r * 128:r * 128 + rows],
                              in0=tri_gt[:rows, :rows], scalar1=dA_[:rows, h:h + 1], scalar2=None, op0=ALU.mult)
                        for r in range(4):
                            O("pe", "matmul", [BL, Bc], [pbuf[3]], bank(3)[:rows, r * 128:r * 128 + rows], Lm[:rows, r * 128:r * 128 + rows],
                              tri_le[:rows, :rows], start=True, stop=True)
                        for r in range(4):
                            O("act", "activation", [pbuf[3]], [BE], out=Em[:rows, r * 128:r * 128 + rows], in_=bank(3)[:rows, r * 128:r * 128 + rows], func=AF.Exp)
                            O("dve", "tensor_tensor", [BE, Bcbm], [BM], out=Mm[:rows, r * 128:r * 128 + rows], in0=Em[:rows, r * 128:r * 128 + rows],
                              in1=cbm[:rows, :rows], op=ALU.mult)
                        for r in range(4):
                            h = 4 * g + r
                            O("pe", "matmul", [BM, Bx[i]], [pbuf[4]], bank(4)[:rows, r * 64:(r + 1) * 64], Mm[:rows, r * 128:r * 128 + rows],
                              xdt[i][:rows, h * 64:(h + 1) * 64], start=True, stop=True)
                        O("pe", "matmul", [BBT, BSb], [pbuf[4]], bank(4)[:rows, 256:512], CT[:, :rows], Sb[:, g * 256:(g + 1) * 256], start=True, stop=True)
                        O("dve", "tensor_tensor", [pbuf[4], Bsm[i]], [Btg], out=tmpg.rearrange("p (h d) -> p h d", d=64)[:rows],
                          in0=bank(4)[:, 256:512].rearrange("p (h d) -> p h d", d=64)[:rows],
                          in1=eacs_[:rows, 4 * g:4 * g + 4].unsqueeze(2).to_broadcast([rows, 4, 64]), op=ALU.mult)
                        O("dve", "tensor_tensor", [pbuf[4], Btg], [BY[i]], out=Y[i][:rows, g * 256:(g + 1) * 256], in0=bank(4)[:rows, 0:256],
                          in1=tmpg[:rows, :], op=ALU.add)
                O("dve", "tensor_tensor", [BS, Bsm[i], BSb], [BS], out=S.rearrange("p (h d) -> p h d", d=64),
                  in0=S.rearrange("p (h d) -> p h d", d=64), in1=cdec_.unsqueeze(2).to_broadcast([128, SH, 64]), op=ALU.mult)
                for gp in range(0, G, 2):
                    bk = 5 + (gp // 2) % 2
                    for g in (gp, gp + 1):
                        O("pe", "matmul", [Bx[i]], [pbuf[bk]], bank(bk)[:, (g - gp) * 256:(g - gp + 1) * 256], Bb[i][:rows, g * 128:(g + 1) * 128],
                          xdte[i][:rows, g * 256:(g + 1) * 256], start=True, stop=True)
                    O("dve", "tensor_tensor", [BS, pbuf[bk]], [BS], out=S[:, gp * 256:(gp + 2) * 256], in0=S[:, gp * 256:(gp + 2) * 256],
                      in1=bank(bk), op=ALU.add)
                if own:
                    O("pool", "tensor_tensor", [Bld[i], Bk, Bx[i]], [Bld[i]], out=xa3[:rows], in0=xa3[:rows],
                      in1=dskb[:rows, :].unsqueeze(2).to_broadcast([rows, SH, 64]), op=ALU.mult)
                    O("dve", "tensor_tensor", [BY[i], Bld[i]], [BY[i]], out=Y[i][:rows, :], in0=Y[i][:rows, :], in1=xa[i][:rows, :], op=ALU.add)
                    O("act", "activation", [Bld[i]], [Bld[i]], out=za[i][:rows, :], in_=za[i][:rows, :], func=AF.Silu)
                    O("dve", "tensor_tensor", [BY[i], Bld[i]], [BY[i]], out=Y[i][:rows, :], in0=Y[i][:rows, :], in1=za[i][:rows, :], op=ALU.mult)
                    O("pool", "memset", [], [Bgst], gst, 0.0)
                    for g in range(G):
                        O("act", "activation", [BY[i]], [Bld[i], Bgst], out=xa[i][:rows, g * 256:(g + 1) * 256], in_=Y[i][:rows, g * 256:(g + 1) * 256],
                          func=AF.Square, accum_out=gst[:rows, g:g + 1])
                    O("dve", "tensor_scalar", [Bgst], [Bgst], out=gst[:rows, G:2 * G], in0=gst[:rows, 0:G], scalar1=1.0 / 256, scalar2=EPS,
                      op0=ALU.mult, op1=ALU.add)
                    O("act", "sqrt", [Bgst], [Bgst], out=gst[:rows, 2 * G:3 * G], in_=gst[:rows, G:2 * G])
                    O("dve", "reciprocal", [Bgst], [Bgst], out=gst[:rows, 3 * G:4 * G], in_=gst[:rows, 2 * G:3 * G])
                    O("dve", "tensor_tensor", [BY[i], Bgst], [BY[i]], out=Y[i].rearrange("p (g d) -> p g d", d=256)[:rows],
                      in0=Y[i].rearrange("p (g d) -> p g d", d=256)[:rows],
                      in1=gst[:rows, 3 * G:4 * G].unsqueeze(2).to_broadcast([rows, G, 256]), op=ALU.mult)
                    O("dve", "tensor_tensor", [BY[i], Bk], [Byo[i]], out=yo[i][:rows, :], in0=Y[i][:rows, :], in1=nwb[:rows, :], op=ALU.mult)
                    DMA("sp", mix_d[o0:o0 + rows, AW:AW + SI], yo[i][:rows, :], reads=[Byo[i]])

            def state_out(dst):
                for j in range(SI // 128):
                    O("pe", "transpose", [BS, Bc], [pbuf[7]], out=bank(7)[:, 0:128], in_=S[:, j * 128:(j + 1) * 128], identity=ident_f)
                    O("dve", "tensor_copy", [pbuf[7]], [Btr], out=tr, in_=bank(7)[:, 0:128])
                    DMA("sp", dst[j * 128:(j + 1) * 128, :], tr, reads=[Btr])

            n = 0
            for t in range(NTILE):
                ssd_tile(t, n)
                n += 1
            state_out(ssm_p)
            for j in range(SI // 128):
                DMA("sp", tr, sssm[j * 128:(j + 1) * 128, :], writes=[Btr])
                O("pe", "transpose", [Btr, Bc], [pbuf[7]], out=bank(7)[:, 0:128], in_=tr, identity=ident_f)
                O("dve", "tensor_copy", [pbuf[7], BSb], [BS], out=S[:, j * 128:(j + 1) * 128], in_=bank(7)[:, 0:128])
            ssd_tile(NTILE, n)
            state_out(ssm_s)
            P.barrier()
            top[0] = base_top

        if "C" in phases:
            lq = f32v(256)
            lt = f32v(8)
            Bl = Buf("lam")
            DMA("sp", lq, lam_in[0:1, :].partition_broadcast(128), writes=[Bl])
            O("pool", "memset", [], [Bl], lt, 0.0)
            ljunk = f32v(64)
            O("dve", "tensor_tensor", [Bl], [Bl], out=lq[:, 0:64], in0=lq[:, 0:64], in1=lq[:, 64:128], op=ALU.mult)
            O("dve", "tensor_tensor", [Bl], [Bl], out=lq[:, 128:192], in0=lq[:, 128:192], in1=lq[:, 192:256], op=ALU.mult)
            O("act", "activation", [Bl], [Bl], out=ljunk, in_=lq[:, 0:64], func=AF.Copy, accum_out=lt[:, 0:1])
            O("act", "activation", [Bl], [Bl], out=ljunk, in_=lq[:, 128:192], func=AF.Copy, accum_out=lt[:, 1:2])
            O("act", "activation", [Bl], [Bl], out=lt[:, 2:4], in_=lt[:, 0:2], func=AF.Exp)
            O("dve", "tensor_tensor", [Bl], [Bl], out=lt[:, 4:5], in0=lt[:, 2:3], in1=lt[:, 3:4], op=ALU.subtract)
            O("dve", "tensor_scalar", [Bl], [Bl], out=lt[:, 5:6], in0=lt[:, 4:5], scalar1=LAM0, scalar2=-1.0, op0=ALU.add, op1=ALU.mult)
            nlam = lt[:, 5:6]
            slw = f32v(128)
            DMA("sp", slw, subln_w[0:1, :].partition_broadcast(128), writes=[Bl])
            O("dve", "tensor_scalar", [Bl], [Bl], out=slw, in0=slw, scalar1=1.0 - LAM0, scalar2=None, op0=ALU.mult)
            c_top = top[0]

            def prep(ksrc, vsrc, qsrc, vlsrc, rows, kt, KTd, VAd, QTd, qcol, n):
                i = n % 2
                kf, vf, qf = pk[i], pvv[i], pq[i]
                DMA("sp", kf[:rows, :], ksrc, writes=[Bpk[i]])
                DMA("sp", vf[:rows, :], vsrc, writes=[Bpk[i]])
                kb_, vb_ = pkb[i], pvb[i]
                O("act", "copy", [Bpk[i]], [Bpb[i]], out=kb_[:rows, :], in_=kf[:rows, :])
                vb3 = vb_.rearrange("p (h d) -> p h d", d=129)
                O("dve", "tensor_copy", [Bpk[i]], [Bpb[i]], out=vb3[:rows, :, 0:128], in_=vf.rearrange("p (h d) -> p h d", d=128)[:rows])
                if vlsrc is None:
                    O("pool", "memset", [], [Bpb[i]], vb3[:rows, :, 128:129], 1.0)
                else:
                    DMA("sp", pvl[i][:rows, :], vlsrc, writes=[Bpk[i]])
                    O("pool", "tensor_copy", [Bpk[i]], [Bpb[i]], out=vb3[:rows, :, 128:129],
                      in_=pvl[i][:rows, 0:1].unsqueeze(1).to_broadcast([rows, H, 1]))
                DMA("pool", VAd.rearrange("h p (t d) -> p h t d", d=129)[:rows, :, kt, :], vb3[:rows], reads=[Bpb[i]])
                srcs = [(kb_, KTd, kt * 128)]
                if qsrc is not None:
                    DMA("sp", qf[:rows, :], qsrc, writes=[Bpk[i]])
                    O("act", "activation", [Bpk[i]], [Bpb[i]], out=pqb[i][:rows, :], in_=qf[:rows, :], func=AF.Copy, scale=0.125)
                    srcs.append((pqb[i], QTd, qcol))
                for sb_, dstd, col in srcs:
                    for g in range((H + 7) // 8):
                        hh = min(8, H - g * 8)
                        pv_ = bank(7).bitcast(BF16)
                        for j in range(hh):
                            h = g * 8 + j
                            O("pe", "transpose", [Bpb[i], Bc], [pbuf[7]], out=pv_[:, j * 128:j * 128 + rows], in_=sb_[:rows, h * 128:(h + 1) * 128],
                              identity=ident_b[:rows, :rows])
                        O("dve", "tensor_copy", [pbuf[7]], [Bpt], out=ptt[:, :hh * 128], in_=pv_[:, :hh * 128])
                        DMA("pool", dstd.rearrange("h p n -> p h n")[:, g * 8:g * 8 + hh, col:col + rows],
                            ptt.rearrange("p (h n) -> p h n", n=128)[:, :hh, :rows], reads=[Bpt])

            pk = [f32v(AW) for _ in range(2)]
            pvv = [f32v(AW) for _ in range(2)]
            pq = [f32v(AW) for _ in range(2)]
            pvl = [f32v(1) for _ in range(2)]
            Bpk = [Buf("pk") for _ in range(2)]
            pkb = [bf16v(AW) for _ in range(2)]
            pqb = [bf16v(AW) for _ in range(2)]
            pvb = [bf16v(H * 129) for _ in range(2)]
            Bpb = [Buf("pb") for _ in range(2)]
            ptt = bf16v(1024)
            Bpt = Buf("pt")
            n = 0
            for t in range(NTILE):
                own = t >= T0
                prep(pK_d[t * 128:(t + 1) * 128, :], pV_d[t * 128:(t + 1) * 128, :],
                     pQ_d[(t - T0) * 128:(t - T0 + 1) * 128, :] if own else None, valid[t * 128:(t + 1) * 128, :],
                     128, t, KT_d, VA_d, QT_d, (t - T0) * 128, n)
                n += 1
            for kt in range(NKS - 1):
                prep(ck[kt * 128:(kt + 1) * 128, :], cv[kt * 128:(kt + 1) * 128, :], None, None, 128, kt, KTs_d, VAs_d, None, 0, n)
                n += 1
            prep(pK_d[NTOK:NTOK + DEC, :], pV_d[NTOK:NTOK + DEC, :], pQ_d[TQ:TQ + DEC, :], None, DEC, NKS - 1, KTs_d, VAs_d, QTs_d, 0, n)
            P.barrier()
            top[0] = c_top

            def attn(KTd, VAd, QTd, nq, nkt, rows_last, causal, orow0, hn):
                QB = min(512, nq)
                SW = min(128, QB)
                sub = QB // SW
                i = hn % 2
                kts = (nkt - 1) * 128 + rows_last
                DMA("sp", KTb[i][:, :kts], KTd[:, :kts], writes=[BKT[i]])
                DMA("sp", VAb[i][:, :nkt * 129], VAd[:, :nkt * 129], writes=[BKT[i]])
                DMA("sp", QTb[i][:, :nq], QTd[:, :nq], writes=[BKT[i]])
                VA3 = VAb[i].rearrange("p (t d) -> p t d", d=129)
                def acc(m, si):
                    if si < 3:
                        return 4 + m, si * 129
                    return 6, m * 129

                pairs = []
                for qb in range(nq // QB):
                    kt_last = (T0 + qb * sub + sub - 1) if causal else nkt - 1
                    d0 = (T0 + qb * sub) if causal else nkt
                    for kt in range(kt_last + 1):
                        pairs.append((qb, kt, d0, kt == kt_last))

                def emit_scores(n):
                    qb, kt, d0, _ = pairs[n]
                    q0 = qb * QB
                    rk = rows_last if kt == nkt - 1 else 128
                    di = kt - d0 if kt >= d0 else -1
                    qlo = di * SW if di >= 0 else 0
                    ps = n % 3
                    st_ = n % 2
                    for m in range(2):
                        sbk = st_ * 2 + m
                        O("pe", "matmul", [BKT[i]], [pbuf[sbk]], bank(sbk)[:rk, qlo:QB], KTb[i][m * 64:(m + 1) * 64, kt * 128:kt * 128 + rk],
                          QTb[i][m * 64:(m + 1) * 64, q0 + qlo:q0 + QB], start=True, stop=True)
                    O("act", "activation", [pbuf[st_ * 2], pbuf[st_ * 2 + 1]], [BpT[ps][0], BpT[ps][1]],
                      out=pTT[ps].rearrange("p (m q) -> p m q", m=2)[:rk, :, qlo:QB],
                      in_=psum[:, st_ * 1024:(st_ + 1) * 1024].rearrange("p (m q) -> p m q", m=2)[:rk, :, qlo:QB], func=AF.Exp)
                    if di >= 0:
                        for m in range(2):
                            O("pool", "memset", [], [BpT[ps][m]], pT[ps][m][64:128, qlo:qlo + 64], 0.0)

                def emit_pv(n):
                    qb, kt, d0, last = pairs[n]
                    q0 = qb * QB
                    rk = rows_last if kt == nkt - 1 else 128
                    di = kt - d0 if kt >= d0 else -1
                    ps = n % 3
                    for si in range(max(di, 0), sub):
                        last_for_si = (d0 + si) if causal else nkt - 1
                        for m in range(2):
                            ab, ao = acc(m, si)
                            first_in_bank = (kt == 0) and ao == 0 and (ab != 6 or m == 0)
                            O("pe", "matmul", [BpT[ps][m], BKT[i]], [pbuf[ab]], bank(ab)[:SW, ao:ao + 129], pT[ps][m][:rk, si * SW:(si + 1) * SW],
                              VA3[:rk, kt, :], start=first_in_bank, stop=(kt == last_for_si), skip_group_check=True)
                    if not last:
                        return
                    for si in range(sub):
                        a1b, a1o = acc(0, si)
                        a2b, a2o = acc(1, si)
                        o1 = bank(a1b)[:SW, a1o:a1o + 129]
                        o2 = bank(a2b)[:SW, a2o:a2o + 129]
                        j = si % 2
                        s_ = fs[j]
                        O("dve", "reciprocal", [pbuf[a1b]], [Bfs[j]], out=s_[:SW, 0:1], in_=o1[:, 128:129])
                        O("dve", "reciprocal", [pbuf[a2b]], [Bfs[j]], out=s_[:SW, 1:2], in_=o2[:, 128:129])
                        O("dve", "tensor_tensor", [Bfs[j], Bl], [Bfs[j]], out=s_[:SW, 2:3], in0=s_[:SW, 1:2], in1=nlam[:SW, :], op=ALU.mult)
                        O("act", "activation", [pbuf[a1b], Bfs[j]], [Bfa[j]], out=fa[j][:SW, :], in_=o1[:, 0:128], func=AF.Copy, scale=s_[:SW, 0:1])
                        O("dve", "scalar_tensor_tensor", [pbuf[a2b], Bfs[j], Bfa[j]], [Bfa[j]], out=fa[j][:SW, :], in0=o2[:, 0:128],
                          scalar=s_[:SW, 2:3], in1=fa[j][:SW, :], op0=ALU.mult, op1=ALU.add)
                        O("pool", "memset", [], [Bfs[j]], s_[:, 3:4], 0.0)
                        O("act", "activation", [Bfa[j]], [Bfj, Bfs[j]], out=fj[:SW, :], in_=fa[j][:SW, :], func=AF.Square, accum_out=s_[:SW, 3:4])
                        O("dve", "tensor_scalar", [Bfs[j]], [Bfs[j]], out=s_[:SW, 4:5], in0=s_[:SW, 3:4], scalar1=1.0 / 128, scalar2=EPS,
                          op0=ALU.mult, op1=ALU.add)
                        O("act", "sqrt", [Bfs[j]], [Bfs[j]], out=s_[:SW, 5:6], in_=s_[:SW, 4:5])
                        O("dve", "reciprocal", [Bfs[j]], [Bfs[j]], out=s_[:SW, 6:7], in_=s_[:SW, 5:6])
                        O("dve", "scalar_tensor_tensor", [Bfa[j], Bfs[j], Bl], [Bfo[j]], out=fo[j][:SW, :], in0=fa[j][:SW, :], scalar=s_[:SW, 6:7],
                          in1=slw[:SW, :], op0=ALU.mult, op1=ALU.mult)
                        r0 = orow0 + q0 + si * SW
                        DMA("sp", mix_d[r0:r0 + SW, hcol[0]:hcol[0] + 128], fo[j][:SW, :], reads=[Bfo[j]])

                for n in range(len(pairs) + 1):
                    if n < len(pairs):
                        emit_scores(n)
                    if n >= 1:
                        emit_pv(n - 1)

            KTb = [bf16v(max(NTOK, NKS * 128)) for _ in range(2)]
            VAb = [bf16v(max(NTILE, NKS) * 129) for _ in range(2)]
            QTb = [bf16v(max(TQ, DEC)) for _ in range(2)]
            BKT = [Buf("KT") for _ in range(2)]
            pTT = [bf16v(1024) for _ in range(3)]
            pT = [[pTT[a][:, m * 512:(m + 1) * 512] for m in range(2)] for a in range(3)]
            BpT = [[Buf("pT") for _ in range(2)] for _ in range(3)]
            fs = [f32v(8) for _ in range(2)]
            Bfs = [Buf("fs") for _ in range(2)]
            fa = [f32v(128) for _ in range(2)]
            Bfa = [Buf("fa") for _ in range(2)]
            fj = f32v(128)
            Bfj = Buf("fj")
            fo = [bf16v(128) for _ in range(2)]
            Bfo = [Buf("fo") for _ in range(2)]
            hcol = [0]
            hn = 0
            for h in range(H):
                hcol[0] = h * 128
                attn(KT_d[h], VA_d[h], QT_d[h], TQ, NTILE, 128, True, 0, hn)
                hn += 1
                attn(KTs_d[h], VAs_d[h], QTs_d[h], DEC, NKS, DEC, False, TQ, hn)
                hn += 1
            P.barrier()
            top[0] = base_top

        if "E" in phases:
            def orow(ot):
                return ot * 128
            normT_pass(lambda ot: mix_d[ot * 128:ot * 128 + own_rows(ot), :], NT + 1, own_rows, D, None, mT_d, src_bf16=True)
            xr = [f32v(512) for _ in range(3)]
            Bxr = [Buf("xr") for _ in range(3)]

            def evacE(ot, rows, c0, cw, bks, n):
                i = n % 3
                src = xq[(T0 + ot) * 128:(T0 + ot) * 128 + rows, c0:c0 + cw] if ot < NT else xs[:, c0:c0 + cw]
                DMA("sp", xr[i][:rows, :cw], src, writes=[Bxr[i]])
                O("dve", "tensor_tensor", [pbuf[bks[0]], Bxr[i]], [Bxr[i]], out=xr[i][:rows, :cw], in0=bank(bks[0])[:rows, :cw],
                  in1=xr[i][:rows, :cw], op=ALU.add)
                DMA("pool", x1_d[ot * 128:ot * 128 + rows, c0:c0 + cw], xr[i][:rows, :cw], reads=[Bxr[i]])
            proj_pass(mT_d, KC, [w_out], D, 1024, lambda c0, cw: list(range(NT + 1)), own_rows, evacE, hq=("sp", "act"))
            top[0] = base_top
            normT_pass(lambda ot: x1_d[ot * 128:ot * 128 + own_rows(ot), :], NT + 1, own_rows, D, norm2_w[0:1, :], h2T_d)

        if "F" in phases:
            CWF = 512
            gs = [f32v(CWF) for _ in range(3)]
            go = [bf16v(CWF) for _ in range(3)]
            Bgs = [Buf("gs") for _ in range(3)]
            Bgo = [Buf("go") for _ in range(3)]

            def evacF1(ot, rows, c0, cw, bks, n):
                i = n % 3
                O("act", "activation", [pbuf[bks[0]]], [Bgs[i]], out=gs[i][:rows, :cw], in_=bank(bks[0])[:rows, :cw], func=AF.Silu)
                O("dve", "tensor_tensor", [Bgs[i], pbuf[bks[1]]], [Bgo[i]], out=go[i][:rows, :cw], in0=gs[i][:rows, :cw],
                  in1=bank(bks[1])[:rows, :cw], op=ALU.mult)
                DMA("pool", ff_d[ot * 128:ot * 128 + rows, c0:c0 + cw], go[i][:rows, :cw], reads=[Bgo[i]])
            proj_pass(h2T_d, KC, [w_gate, w_up], DFF, CWF, lambda c0, cw: list(range(NT + 1)), own_rows, evacF1)
            top[0] = base_top
            normT_pass(lambda ot: ff_d[ot * 128:ot * 128 + own_rows(ot), :], NT + 1, own_rows, DFF, None, ffT_d, src_bf16=True)
            xr = [f32v(CWF) for _ in range(3)]
            Bxr = [Buf("xr") for _ in range(3)]

            def evacF2(ot, rows, c0, cw, bks, n):
                i = n % 3
                DMA("sp", xr[i][:rows, :cw], x1_d[ot * 128:ot * 128 + rows, c0:c0 + cw], writes=[Bxr[i]])
                O("dve", "tensor_tensor", [pbuf[bks[0]], Bxr[i]], [Bxr[i]], out=xr[i][:rows, :cw], in0=bank(bks[0])[:rows, :cw],
                  in1=xr[i][:rows, :cw], op=ALU.add)
                DMA("pool", x2_d[ot * 128:ot * 128 + rows, c0:c0 + cw], xr[i][:rows, :cw], reads=[Bxr[i]])
            proj_pass(ffT_d, DFF // 128, [w_down], D, CWF, lambda c0, cw: list(range(NT + 1)), own_rows, evacF2, wbufs=1, hq=("sp", "act"))
            top[0] = base_top
            fw = f32v(D)
            Bfw = Buf("fw")
            DMA("sp", fw, final_norm_w[0:1, :].partition_broadcast(128), writes=[Bfw])
            xt = [f32v(D) for _ in range(2)]
            Bxt = [Buf("xt") for _ in range(2)]
            jk = bf16v(D)
            Bjk = Buf("jk")
            st = [f32v(4) for _ in range(2)]
            Bst = [Buf("st") for _ in range(2)]
            for ot in range(NT + 1):
                i = ot % 2
                rows = own_rows(ot)
                DMA("sp", xt[i][:rows, :], x2_d[ot * 128:ot * 128 + rows, :], writes=[Bxt[i]])
                O("pool", "memset", [], [Bst[i]], st[i], 0.0)
                O("act", "activation", [Bxt[i]], [Bjk, Bst[i]], out=jk[:rows, :], in_=xt[i][:rows, :], func=AF.Square, accum_out=st[i][:rows, 0:1])
                O("dve", "tensor_scalar", [Bst[i]], [Bst[i]], out=st[i][:rows, 1:2], in0=st[i][:rows, 0:1], scalar1=1.0 / D, scalar2=EPS,
                  op0=ALU.mult, op1=ALU.add)
                O("act", "sqrt", [Bst[i]], [Bst[i]], out=st[i][:rows, 3:4], in_=st[i][:rows, 1:2])
                O("dve", "reciprocal", [Bst[i]], [Bst[i]], out=st[i][:rows, 2:3], in_=st[i][:rows, 3:4])
                O("dve", "scalar_tensor_tensor", [Bxt[i], Bst[i], Bfw], [Bxt[i]], out=xt[i][:rows, :], in0=xt[i][:rows, :],
                  scalar=st[i][:rows, 2:3], in1=fw[:rows, :], op0=ALU.mult, op1=ALU.mult)
                dst = y_q[ot * 128:ot * 128 + rows, :] if ot < NT else y_s[:, :]
                DMA("sp", dst, xt[i][:rows, :], reads=[Bxt[i]])

        P.finish()
        P.emit()
    return nc


def host_inputs(cfg, inp):
    c = cfg
    f = lambda a: np.ascontiguousarray(np.asarray(a, dtype=np.float32))
    xp = f(inp["x_prompt"])
    B = xp.shape[0]
    ncores = B * NSLOT
    lam_in = np.concatenate([f(inp["lambda_q1"]), f(inp["lambda_k1"]), f(inp["lambda_q2"]), f(inp["lambda_k2"])], 0).reshape(1, 256)
    u = np.arange(128)
    consts = np.concatenate([np.eye(128), (u[:, None] <= u[None, :]), (u[:, None] > u[None, :]), np.ones((128, 128))], 1).astype(np.float32)
    shared = {
        "norm1_w": f(inp["norm1_w"]), "w_in": f(inp["w_in"])[0], "lam_in": lam_in,
        "subln_w": f(inp["subln_w"]), "conv_w": f(inp["conv_w"])[0], "conv_b": f(inp["conv_b"]),
        "dt_bias": f(inp["dt_bias"]), "A_log": f(inp["A_log"]), "D_skip": f(inp["D_skip"]),
        "ssd_norm_w": f(inp["ssd_norm_w"]), "w_out": f(inp["w_out"])[0], "norm2_w": f(inp["norm2_w"]),
        "w_gate": f(inp["w_gate"])[0], "w_up": f(inp["w_up"])[0], "w_down": f(inp["w_down"])[0],
        "final_norm_w": f(inp["final_norm_w"]).reshape(1, -1),
        "consts": consts,
    }
    maps = []
    for core in range(ncores):
        b, j = core // NSLOT, core % NSLOT
        xq = np.zeros((NSLOT * c.TQ, c.D), np.float32)
        valid = np.zeros((NSLOT * c.TQ, 1), np.float32)
        n = (j + 1) * c.TQ
        xq[NSLOT * c.TQ - n:] = xp[b, :n]
        valid[NSLOT * c.TQ - n:] = 1.0
        m = dict(shared)
        m.update({
            "xq": xq, "valid": valid, "xs": f(inp["x_sample"])[core],
            "ck": f(inp["cache_k"])[0, core].reshape(c.PAST, c.AW),
            "cv": f(inp["cache_v"])[0, core].reshape(c.PAST, c.AW),
            "sconv": f(inp["state_conv"])[0, core],
            "sssm": f(inp["state_ssm"])[0, core].reshape(c.SH * 64, 128),
        })
        maps.append(m)
    return maps


def assemble(cfg, res, B):
    c = cfg
    ncores = B * NSLOT
    g = lambda name: [np.asarray(res[i][name], dtype=np.float32) for i in range(ncores)]
    yq, ys, kq, vq, cp, sp_, ks, vs, cs, ss = (g(n) for n in
        ["y_q", "y_s", "k_q", "v_q", "conv_p", "ssm_p", "k_s", "v_s", "conv_s", "ssm_s"])
    cat = lambda lst, b: np.concatenate(lst[b * NSLOT:(b + 1) * NSLOT], 0)
    y_prompt = np.stack([cat(yq, b) for b in range(B)])
    k_prompt = np.stack([cat(kq, b) for b in range(B)]).reshape(1, B, c.SEQ, c.H, 128)
    v_prompt = np.stack([cat(vq, b) for b in range(B)]).reshape(1, B, c.SEQ, c.H, 128)
    conv_prompt = np.stack([cp[b * NSLOT + NSLOT - 1] for b in range(B)])[None]
    ssm_prompt = np.stack([sp_[b * NSLOT + NSLOT - 1] for b in range(B)]).reshape(1, B, c.SH, 64, 128)
    y_sample = np.stack(ys)
    k_sample = np.stack(ks).reshape(1, ncores, c.DEC, c.H, 128)
    v_sample = np.stack(vs).reshape(1, ncores, c.DEC, c.H, 128)
    conv_sample = np.stack(cs)[None]
    ssm_sample = np.stack(ss).reshape(1, ncores, c.SH, 64, 128)
    return (y_prompt, y_sample, k_prompt, v_prompt, conv_prompt, ssm_prompt,
            k_sample, v_sample, conv_sample, ssm_sample)


def run(cfg, inp, phases="ABXDCEF", dbg=()):
    B = np.asarray(inp["x_prompt"]).shape[0]
    maps = host_inputs(cfg, inp)
    nc = build(cfg, phases, dbg)
    res = run_bass_kernel_spmd(nc, maps, core_ids=list(range(len(maps))))
    if dbg:
        return assemble(cfg, res.results, B), res.results
    return assemble(cfg, res.results, B)


def kernel(**inputs):
    return run(Cfg(), inputs)
```
